# Optimizing a Trainium2 kernel written in Bass

```python
import math
import jax, jax.numpy as jnp
from jax import lax
import numpy as np

D_MODEL = 4096
BATCH = 2
SEQ = 8192
DEPTH = 4

CHUNK = 64
N_A = DEPTH // 2
N_B = DEPTH - N_A
A_EXPAND = 2
A_INNER = A_EXPAND * D_MODEL
A_HEADDIM = 64
A_HEADS = A_INNER // A_HEADDIM
A_GROUPS = 8
A_HPG = A_HEADS // A_GROUPS
A_STATE = 128
A_CONV = 4
A_GN = A_GROUPS * A_STATE
A_CONV_DIM = A_INNER + 2 * A_GN
A_PROJ = 2 * A_INNER + 2 * A_GN + A_HEADS
B_HEADDIM = 128
B_HEADS = D_MODEL // B_HEADDIM
B_KV_HEADS = 8
B_GQA = B_HEADS // B_KV_HEADS
B_INNER = B_HEADS * B_HEADDIM
B_KV_DIM = B_KV_HEADS * B_HEADDIM
Q_BLOCK = 128
EPS = 1e-6

kernel_name = "yoco_mamba2_fox_adaln_trunk"


def rmsnorm(x, w):
    xf = x.astype(jnp.float32)
    y = xf * lax.rsqrt(jnp.mean(xf * xf, axis=-1, keepdims=True) + EPS)
    return (y * w.astype(jnp.float32)).astype(x.dtype)


def causal_depthwise_conv(u, w, b):
    S = u.shape[1]
    up = jnp.pad(u, ((0, 0), (A_CONV - 1, 0), (0, 0)))
    out = b
    for k in range(A_CONV):
        out = out + up[:, k:k + S, :] * w[k]
    return out


def ssd_chunked_scan(x, dt, A, Bm, Cm):
    f32 = jnp.float32
    bsz, S = x.shape[:2]
    nc = S // CHUNK
    a = dt * A
    xdt = x.astype(f32) * dt[..., None]

    def chunks(t):
        return jnp.moveaxis(t.reshape((bsz, nc, CHUNK) + t.shape[2:]), 1, 0)

    xs = chunks(xdt.reshape(bsz, S, A_GROUPS, A_HPG, A_HEADDIM))
    As = chunks(a.reshape(bsz, S, A_GROUPS, A_HPG))
    Bs = chunks(Bm.astype(f32))
    Cs = chunks(Cm.astype(f32))
    tri = jnp.tril(jnp.ones((CHUNK, CHUNK), dtype=bool))[None, :, :, None, None]

    def step(state, inp):
        xc, ac, Bc, Cc = inp
        acs = jnp.cumsum(ac, axis=1)
        seg = acs[:, :, None] - acs[:, None, :]
        L = jnp.exp(jnp.where(tri, seg, -jnp.inf))
        CB = jnp.einsum('btgn,bsgn->btsg', Cc, Bc)
        y_diag = jnp.einsum('btsg,btsgj,bsgjp->btgjp', CB, L, xc)
        y_off = jnp.einsum('btgn,bgjpn->btgjp', Cc, state) * jnp.exp(acs)[..., None]
        decay = jnp.exp(acs[:, -1:] - acs)
        new_state = (state * jnp.exp(acs[:, -1])[..., None, None]
                     + jnp.einsum('bsgn,bsgj,bsgjp->bgjpn', Bc, decay, xc))
        return new_state, y_diag + y_off

    state0 = jnp.zeros((bsz, A_GROUPS, A_HPG, A_HEADDIM, A_STATE), f32)
    _, ys = lax.scan(step, state0, (xs, As, Bs, Cs))
    return jnp.moveaxis(ys, 0, 1).reshape(bsz, S, A_HEADS, A_HEADDIM)


def mamba2_mixer(h, in_proj, conv_w, conv_b, dt_bias, A_log, D_skip, gnorm, out_proj):
    f32 = jnp.float32
    bsz, S, _ = h.shape
    zxbcdt = h @ in_proj
    z = zxbcdt[..., :A_INNER]
    xBC = zxbcdt[..., A_INNER:A_INNER + A_CONV_DIM]
    dt_raw = zxbcdt[..., A_INNER + A_CONV_DIM:]
    xBC = jax.nn.silu(causal_depthwise_conv(xBC, conv_w, conv_b))
    xs = xBC[..., :A_INNER].reshape(bsz, S, A_HEADS, A_HEADDIM)
    Bm = xBC[..., A_INNER:A_INNER + A_GN].reshape(bsz, S, A_GROUPS, A_STATE)
    Cm = xBC[..., A_INNER + A_GN:].reshape(bsz, S, A_GROUPS, A_STATE)
    dt = jax.nn.softplus(dt_raw.astype(f32) + dt_bias.astype(f32))
    A = -jnp.exp(A_log.astype(f32))
    y = ssd_chunked_scan(xs, dt, A, Bm, Cm) + D_skip.astype(f32)[:, None] * xs.astype(f32)
    y = y.reshape(bsz, S, A_INNER) * jax.nn.silu(z.astype(f32))
    yg = y.reshape(bsz, S, A_GROUPS, A_INNER // A_GROUPS)
    yg = yg * lax.rsqrt(jnp.mean(yg * yg, axis=-1, keepdims=True) + EPS)
    y = (yg.reshape(bsz, S, A_INNER) * gnorm.astype(f32)).astype(h.dtype)
    return y @ out_proj


def shared_kv_stream(x, mod_kv, kv_norm, w_kv, w_f, b_f):
    bsz, S, _ = x.shape
    shift, scale = jnp.split(mod_kv[:, None, :], 2, axis=-1)
    hk = rmsnorm(x, kv_norm) * (1 + scale) + shift
    kv = hk @ w_kv
    k = kv[..., :B_KV_DIM].reshape(bsz, S, B_KV_HEADS, B_HEADDIM)
    v = kv[..., B_KV_DIM:].reshape(bsz, S, B_KV_HEADS, B_HEADDIM)
    logf = jax.nn.log_sigmoid((hk @ w_f).astype(jnp.float32) + b_f.astype(jnp.float32))
    F = jnp.cumsum(logf, axis=1)
    Ft = jnp.transpose(F.reshape(bsz, S, B_KV_HEADS, B_GQA), (0, 2, 3, 1))
    return k, v, Ft


def fox_attention(h, w_qz, k, v, Ft, out_proj):
    bsz, S, _ = h.shape
    qz = h @ w_qz
    q = qz[..., :B_INNER].reshape(bsz, S, B_KV_HEADS, B_GQA, B_HEADDIM)
    z = qz[..., B_INNER:]
    scale = B_HEADDIM ** -0.5
    outs = []
    for i in range(S // Q_BLOCK):
        q0 = i * Q_BLOCK
        T = q0 + Q_BLOCK
        kb = k[:, :T]
        vb = v[:, :T]
        s = jnp.einsum('bqhgd,bshd->bhgqs', q[:, q0:T], kb).astype(jnp.float32) * scale
        bias = Ft[..., q0:T, None] - Ft[..., None, :T]
        mask = jnp.arange(T)[None, :] <= (q0 + jnp.arange(Q_BLOCK))[:, None]
        p = jax.nn.softmax(jnp.where(mask, s + bias, -jnp.inf), axis=-1)
        outs.append(jnp.einsum('bhgqs,bshd->bqhgd', p.astype(vb.dtype), vb))
    o = jnp.concatenate(outs, axis=1).reshape(bsz, S, B_INNER)
    return (o * jax.nn.silu(z)) @ out_proj


def setup_inputs(seed: int = 0) -> dict:
    key = jax.random.key(seed)
    ks = jax.random.split(key, 24)
    nrm = jax.random.normal
    f32 = jnp.float32
    dt0 = jnp.exp(jax.random.uniform(ks[9], (N_A, A_HEADS), f32)
                  * (math.log(0.1) - math.log(0.001)) + math.log(0.001))
    return {
        "x": nrm(ks[0], (BATCH, SEQ, D_MODEL), f32),
        "c": nrm(ks[1], (BATCH, D_MODEL), f32),
        "ada_w": nrm(ks[2], (DEPTH, D_MODEL, 3 * D_MODEL), f32) * (0.5 * D_MODEL ** -0.5),
        "ada_b": 0.01 * nrm(ks[3], (DEPTH, 3 * D_MODEL), f32),
        "norm_w": 1.0 + 0.02 * nrm(ks[4], (DEPTH, D_MODEL), f32),
        "a_in_proj": nrm(ks[5], (N_A, D_MODEL, A_PROJ), f32) * D_MODEL ** -0.5,
        "a_conv_w": nrm(ks[6], (N_A, A_CONV, A_CONV_DIM), f32) * A_CONV ** -0.5,
        "a_conv_b": 0.01 * nrm(ks[7], (N_A, A_CONV_DIM), f32),
        "a_dt_bias": dt0 + jnp.log(-jnp.expm1(-dt0)),
        "a_A_log": jnp.log(jax.random.uniform(ks[8], (N_A, A_HEADS), f32, 1.0, 16.0)),
        "a_D": 1.0 + 0.02 * nrm(ks[10], (N_A, A_HEADS), f32),
        "a_gnorm": 1.0 + 0.02 * nrm(ks[11], (N_A, A_INNER), f32),
        "a_out_proj": nrm(ks[12], (N_A, A_INNER, D_MODEL), f32) * A_INNER ** -0.5,
        "kv_norm": 1.0 + 0.02 * nrm(ks[13], (D_MODEL,), f32),
        "kv_ada_w": nrm(ks[14], (D_MODEL, 2 * D_MODEL), f32) * (0.5 * D_MODEL ** -0.5),
        "kv_ada_b": 0.01 * nrm(ks[15], (2 * D_MODEL,), f32),
        "w_kv": nrm(ks[16], (D_MODEL, 2 * B_KV_DIM), f32) * D_MODEL ** -0.5,
        "w_f": nrm(ks[17], (D_MODEL, B_HEADS), f32) * (0.5 * D_MODEL ** -0.5),
        "b_f": jax.random.uniform(ks[18], (B_HEADS,), f32, 1.0, 6.0),
        "b_in_proj": nrm(ks[19], (N_B, D_MODEL, 2 * B_INNER), f32) * D_MODEL ** -0.5,
        "b_out_proj": nrm(ks[20], (N_B, B_INNER, D_MODEL), f32) * B_INNER ** -0.5,
        "final_norm": 1.0 + 0.02 * nrm(ks[21], (D_MODEL,), f32),
    }


def reference(x, c, ada_w, ada_b, norm_w, a_in_proj, a_conv_w, a_conv_b, a_dt_bias,
              a_A_log, a_D, a_gnorm, a_out_proj, kv_norm, kv_ada_w, kv_ada_b, w_kv,
              w_f, b_f, b_in_proj, b_out_proj, final_norm):
    k = v = Ft = None
    for i in range(DEPTH):
        mod = (c @ ada_w[i] + ada_b[i])[:, None, :]
        shift, scale, gate = jnp.split(mod, 3, axis=-1)
        if i == N_A:
            k, v, Ft = shared_kv_stream(x, c @ kv_ada_w + kv_ada_b, kv_norm, w_kv, w_f, b_f)
        h = rmsnorm(x, norm_w[i]) * (1 + scale) + shift
        if i < N_A:
            out = mamba2_mixer(h, a_in_proj[i], a_conv_w[i], a_conv_b[i], a_dt_bias[i],
                               a_A_log[i], a_D[i], a_gnorm[i], a_out_proj[i])
        else:
            j = i - N_A
            out = fox_attention(h, b_in_proj[j], k, v, Ft, b_out_proj[j])
        x = x + gate * out
    return rmsnorm(x, final_norm)
```

```python
from contextlib import ExitStack
import math
import numpy as np
import concourse.bass as bass
import concourse.mybir as mybir
from concourse.bass_utils import run_bass_kernel_spmd

F32 = mybir.dt.float32
BF16 = mybir.dt.bfloat16
ALU = mybir.AluOpType
AF = mybir.ActivationFunctionType
AX = mybir.AxisListType
EPS = 1e-6


class Tok:
    __slots__ = ("name", "lw", "rd", "ex")

    def __init__(self, name="", ex=False):
        self.name = name
        self.lw = None
        self.rd = {}
        self.ex = ex


def PTok():
    return Tok("psum", True)


class Prog:
    ENGS = ("pe", "act", "dve", "pool", "sp")

    def __init__(self, nc):
        self.nc = nc
        self.streams = {k: [] for k in self.ENGS}
        self.sems = {}
        self.cnt = {}
        self.isdma = {}
        self.seen = {k: {} for k in self.ENGS}
        self.stack = ExitStack()
        self.phase_stack = None
        self.ninstr = 0
        for k in self.ENGS:
            self._sem("c_" + k, False)

    def _sem(self, key, dma):
        if key not in self.sems:
            self.sems[key] = self.stack.enter_context(self.nc.semaphore(key))
            self.cnt[key] = 0
            self.isdma[key] = dma
        return self.sems[key]

    def begin_phase(self):
        assert self.phase_stack is None
        self.phase_stack = ExitStack()

    def end_phase(self):
        self.sync_all()
        self.phase_stack.close()
        self.phase_stack = None

    def sb(self, name, shape, dt):
        st = self.phase_stack if self.phase_stack is not None else self.stack
        self.uid = getattr(self, "uid", 0) + 1
        return st.enter_context(self.nc.sbuf_tensor(f"sb{self.uid}_{name}", list(shape), dt))

    def ps(self, name, shape, dt):
        st = self.phase_stack if self.phase_stack is not None else self.stack
        self.uid = getattr(self, "uid", 0) + 1
        return st.enter_context(self.nc.psum_tensor(f"ps{self.uid}_{name}", list(shape), dt))

    def op(self, eng, fn, r=(), w=(), dma=None):
        if dma is not None:
            semkey = "d_" + dma
            self._sem(semkey, True)
            inc = 16
        else:
            semkey = "c_" + eng
            inc = 1
        need = {}
        cnt = self.cnt
        isdma = self.isdma
        if any(t.ex for t in r):
            w = list(w) + [t for t in r if t.ex and t not in w]
        for t in r:
            d = t.lw
            if d is not None:
                k, v = d
                if isdma[k]:
                    v = cnt[k]
                if need.get(k, 0) < v:
                    need[k] = v
        for t in w:
            d = t.lw
            if d is not None:
                k, v = d
                if isdma[k]:
                    v = cnt[k]
                if need.get(k, 0) < v:
                    need[k] = v
            for k, v in t.rd.items():
                if isdma[k]:
                    v = cnt[k]
                if need.get(k, 0) < v:
                    need[k] = v
        seen = self.seen[eng]
        waits = []
        for k, v in need.items():
            if k == "c_pe" and eng == "pe" and dma is None:
                continue
            if seen.get(k, 0) >= v:
                continue
            seen[k] = v
            waits.append((self.sems[k], v))
        cnt[semkey] += inc
        val = cnt[semkey]
        for t in w:
            t.lw = (semkey, val)
            t.rd = {}
        for t in r:
            if t.rd.get(semkey, 0) < val:
                t.rd[semkey] = val
        self.streams[eng].append((waits, fn, self.sems[semkey], inc))
        self.ninstr += 1

    def sync_all(self):
        for eng in self.ENGS:
            seen = self.seen[eng]
            waits = []
            for k, v in self.cnt.items():
                if v > 0 and seen.get(k, 0) < v:
                    seen[k] = v
                    waits.append((self.sems[k], v))
            if waits:
                self.streams[eng].append((waits, None, None, 0))

    def emit(self):
        self.sync_all()
        nc = self.nc
        streams = self.streams

        def run(e, lst):
            for waits, fn, semh, inc in lst:
                for s, v in waits:
                    e.wait_ge(s, v)
                if fn is not None:
                    fn(e).then_inc(semh, inc)

        with nc.Block() as block:
            @block.tensor
            def _(e):
                run(e, streams["pe"])

            @block.scalar
            def _(e):
                run(e, streams["act"])

            @block.vector
            def _(e):
                run(e, streams["dve"])

            @block.gpsimd
            def _(e):
                run(e, streams["pool"])

            @block.sync
            def _(e):
                run(e, streams["sp"])
        self.stack.close()

    def dma(self, out, in_, r=(), w=(), grp="g", eng="sp"):
        shp = tuple(out.shape)
        if len(shp) == 3 and shp[1] > 8 and tuple(in_.shape) == shp:
            for k0 in range(0, shp[1], 8):
                k1 = min(shp[1], k0 + 8)
                self._dma1(out[:, k0:k1, :], in_[:, k0:k1, :], r, w, grp, eng)
            return
        self._dma1(out, in_, r, w, grp, eng)

    def _dma1(self, out, in_, r, w, grp, eng):
        self.op(eng, lambda e: e.dma_start(out=out, in_=in_), r=r, w=w, dma=grp)

    def mm(self, out, lhsT, rhs, start=True, stop=True, r=(), w=()):
        self.op("pe", lambda e: e.matmul(out, lhsT, rhs, start=start, stop=stop), r=r, w=w)

    def tr(self, out, in_, ident, r=(), w=()):
        self.op("pe", lambda e: e.transpose(out, in_, ident), r=r, w=w)

    def act(self, out, in_, func, r=(), w=(), **kw):
        self.op("act", lambda e: e.activation(out, in_, func, **kw), r=r, w=w)

    def tt(self, out, in0, in1, op, r=(), w=(), eng="dve"):
        self.op(eng, lambda e: e.tensor_tensor(out, in0, in1, op), r=r, w=w)

    def ts(self, out, in0, s1, s2, op0, op1=None, r=(), w=(), eng="dve"):
        if op1 is None:
            self.op(eng, lambda e: e.tensor_scalar(out, in0, s1, s2, op0), r=r, w=w)
        else:
            self.op(eng, lambda e: e.tensor_scalar(out, in0, s1, s2, op0, op1), r=r, w=w)

    def stt(self, out, in0, scalar, in1, op0, op1, r=(), w=(), eng="dve"):
        self.op(eng, lambda e: e.scalar_tensor_tensor(out, in0, scalar, in1, op0, op1), r=r, w=w)

    def copy(self, out, in_, r=(), w=(), eng="dve"):
        if eng == "act":
            self.op("act", lambda e: e.copy(out, in_), r=r, w=w)
        else:
            self.op(eng, lambda e: e.tensor_copy(out, in_), r=r, w=w)

    def memset(self, ap, val, w=(), eng="dve"):
        self.op(eng, lambda e: e.memset(ap, val), w=w)


class Cfg:
    def __init__(self, D, T, depth=4):
        self.D = D
        self.T = T
        self.KD = D // 128
        self.NA = depth // 2
        self.NB = depth - self.NA
        self.AI = 2 * D
        self.NG = self.AI // 1024
        self.GN = self.NG * 128
        self.CONV = self.AI + 2 * self.GN
        self.AH = self.AI // 64
        self.APROJ = 2 * self.AI + 2 * self.GN + self.AH
        self.BH = D // 128
        self.NKV = self.BH // 4
        self.BI = self.BH * 128
        self.KVD = self.NKV * 128


class K:
    def __init__(self, cfg, debug=()):
        self.cfg = cfg
        self.debug = set(debug)
        nc = bass.Bass("TRN2", target_bir_lowering=False)
        self.nc = nc
        self.P = Prog(nc)
        c = cfg
        D, T = c.D, c.T
        di = lambda n, s, dt=F32: nc.dram_tensor(n, list(s), dt, kind="ExternalInput").ap()
        ds = lambda n, s, dt=BF16: nc.dram_tensor(n, list(s), dt).ap()
        self.I = I = {}
        I["x"] = di("x", [T, D])
        I["cT"] = di("cT", [128, c.KD])
        for i in range(4):
            I[f"ada_w{i}"] = di(f"ada_w{i}", [D, 3 * D])
        I["ada_b"] = di("ada_b", [4, 3 * D])
        I["norm_w"] = di("norm_w", [4, D])
        for i in range(c.NA):
            I[f"a_in{i}"] = di(f"a_in{i}", [D, c.APROJ])
            I[f"a_out{i}"] = di(f"a_out{i}", [c.AI, D])
        I["a_convT"] = di("a_convT", [c.NA, c.CONV, 4])
        I["a_conv_b"] = di("a_conv_b", [c.NA, c.CONV])
        I["a_dt_bias"] = di("a_dt_bias", [c.NA, c.AH])
        I["a_A_log"] = di("a_A_log", [c.NA, c.AH])
        I["a_D"] = di("a_D", [c.NA, c.AH])
        I["a_gnorm"] = di("a_gnorm", [c.NA, c.AI])
        I["kv_norm"] = di("kv_norm", [1, D])
        I["kv_ada_w"] = di("kv_ada_w", [D, 2 * D])
        I["kv_ada_b"] = di("kv_ada_b", [1, 2 * D])
        I["w_kv"] = di("w_kv", [D, 2 * c.KVD])
        I["w_f"] = di("w_f", [D, c.BH])
        I["b_f"] = di("b_f", [1, c.BH])
        for i in range(c.NB):
            I[f"b_in{i}"] = di(f"b_in{i}", [D, 2 * c.BI])
            I[f"b_out{i}"] = di(f"b_out{i}", [c.BI, D])
        I["final_norm"] = di("final_norm", [1, D])
        self.out = nc.dram_tensor("out", [T, D], F32, kind="ExternalOutput").ap()
        self.S = S = {}
        S["xres"] = ds("xres", [T, D], F32)
        S["modbc"] = ds("modbc", [128, 14 * D], F32)
        S["hT"] = ds("hT", [D, T])
        S["yT"] = ds("yT", [c.AI, T])
        S["wb_in"] = ds("wb_in", [D, max(c.APROJ, 2 * c.BI)])
        S["wb_out"] = ds("wb_out", [max(c.AI, c.BI), D])
        S["wb_kv"] = ds("wb_kv", [D, 2 * c.KVD + c.BH])
        S["xs"] = ds("xs", [T, 1024])
        S["Bm"] = ds("Bm", [T, 128])
        S["BT"] = ds("BT", [128, T])
        S["CT"] = ds("CT", [128, T])
        S["KT"] = ds("KT", [c.KVD, T])
        S["V"] = ds("V", [T, c.KVD])
        S["FT"] = ds("FT", [c.BH, T], F32)
        S["NF3"] = ds("NF3", [3, c.BH, T])
        S["QT"] = ds("QT", [c.BI, T])
        S["ZS"] = ds("ZS", [T, c.BI])
        self.dbg = {}
        self.consts()

    def dbg_out(self, name, src_ap, shape, dt):
        if name not in self.debug:
            return
        P = self.P
        o = self.nc.dram_tensor("dbg_" + name, list(shape), dt, kind="ExternalOutput").ap()
        P.begin_phase()
        rows, cols = shape
        t = P.sb("dbgt", [128, cols], dt)
        tk = Tok()
        for r0 in range(0, rows, 128):
            n = min(128, rows - r0)
            P.dma(t[:n, :], src_ap[r0:r0 + n, :], w=[tk], grp="dbg_l")
            P.dma(o[r0:r0 + n, :], t[:n, :], r=[tk], grp="dbg_s")
        P.end_phase()

    def consts(self):
        P = self.P
        C = self.C = {}
        tk = self.Ctok = Tok("consts")
        C["identf"] = P.sb("identf", [128, 128], F32)
        C["ident"] = P.sb("ident", [128, 128], BF16)
        C["trif"] = P.sb("trif", [128, 128], F32)
        C["Uf"] = P.sb("Uf", [128, 128], F32)
        C["onesf"] = P.sb("onesf", [128, 128], F32)
        C["ones3"] = P.sb("ones3", [3, 128], BF16)
        C["maskb"] = P.sb("maskb", [128, 128], F32)
        P.memset(C["identf"][:], 0.0, w=[tk])
        P.op("pool", lambda e: e.affine_select(out=C["identf"][:], in_=C["identf"][:], pattern=[[-1, 128]],
                                               compare_op=ALU.not_equal, fill=1.0, base=0, channel_multiplier=1),
             r=[tk], w=[tk])
        P.copy(C["ident"][:], C["identf"][:], r=[tk], w=[tk])
        P.memset(C["onesf"][:], 1.0, w=[tk])
        P.memset(C["ones3"][:], 1.0, w=[tk])
        P.op("pool", lambda e: e.affine_select(out=C["trif"][:], in_=C["onesf"][:], pattern=[[1, 128]],
                                               compare_op=ALU.is_ge, fill=0.0, base=0, channel_multiplier=-1),
             r=[tk], w=[tk])
        P.op("pool", lambda e: e.affine_select(out=C["Uf"][:], in_=C["onesf"][:], pattern=[[-1, 128]],
                                               compare_op=ALU.is_gt, fill=0.0, base=0, channel_multiplier=1),
             r=[tk], w=[tk])
        P.memset(C["maskb"][:], 0.0, w=[tk])
        P.op("pool", lambda e: e.affine_select(out=C["maskb"][:], in_=C["maskb"][:], pattern=[[-1, 128]],
                                               compare_op=ALU.is_ge, fill=-1e30, base=0, channel_multiplier=1),
             r=[tk], w=[tk])
        P.sync_all()

    def cast_w(self, src, dst, rows, cols, c0=0, d0=0):
        P = self.P
        P.begin_phase()
        CW = 2048
        nb = 3
        st = [P.sb(f"cst{i}", [128, CW], F32) for i in range(nb)]
        ob = [P.sb(f"cob{i}", [128, CW], BF16) for i in range(nb)]
        ts_ = [Tok() for _ in range(nb)]
        to_ = [Tok() for _ in range(nb)]
        engs = ["dve", "pool", "act"]
        i = 0
        for r0 in range(0, rows, 128):
            for cc in range(0, cols, CW):
                w = min(CW, cols - cc)
                b = i % nb
                P.dma(st[b][:, :w], src[r0:r0 + 128, c0 + cc:c0 + cc + w], w=[ts_[b]], grp=f"cl{b}",
                      eng=("sp" if i % 2 == 0 else "act"))
                P.copy(ob[b][:, :w], st[b][:, :w], r=[ts_[b]], w=[to_[b]], eng=engs[i % 3])
                P.dma(dst[r0:r0 + 128, d0 + cc:d0 + cc + w], ob[b][:, :w], r=[to_[b]], grp=f"cs{b}", eng="sp")
                i += 1
        P.end_phase()

    def mods(self):
        P, c, I, S = self.P, self.cfg, self.I, self.S
        D, KD = c.D, c.KD
        P.begin_phase()
        cT = P.sb("cT", [128, KD], F32)
        cbc = P.sb("cbc", [128, KD, 128], F32)
        tc_ = Tok()
        P.dma(cT[:], I["cT"][:, :], w=[tc_], grp="m_c")
        P.copy(cbc[:], cT[:].unsqueeze(2).broadcast_to([128, KD, 128]), r=[tc_], w=[tc_])
        nb = 2
        wt = [P.sb(f"mw{i}", [128, KD, 512], F32) for i in range(nb)]
        bt = [P.sb(f"mb{i}", [128, 512], F32) for i in range(nb)]
        ot = [P.sb(f"mo{i}", [128, 512], F32) for i in range(nb)]
        pst = [P.ps(f"mps{i}", [128, 512], F32) for i in range(nb)]
        tw = [Tok() for _ in range(nb)]
        tb = [Tok() for _ in range(nb)]
        to = [Tok() for _ in range(nb)]
        tp = [PTok() for _ in range(nb)]
        jobs = []
        for i in range(4):
            for n0 in range(0, 3 * D, 512):
                jobs.append((I[f"ada_w{i}"], I["ada_b"][i:i + 1, :], n0, i * 3 * D + n0))
        for n0 in range(0, 2 * D, 512):
            jobs.append((I["kv_ada_w"], I["kv_ada_b"][0:1, :], n0, 12 * D + n0))
        for j, (W, bvec, n0, off) in enumerate(jobs):
            b = j % nb
            P.dma(wt[b][:], W[:, n0:n0 + 512].rearrange("(k p) n -> p k n", p=128), w=[tw[b]], grp=f"m_w{b}",
                  eng=("sp" if j % 2 == 0 else "act"))
            P.dma(bt[b][:], bvec[:, n0:n0 + 512].partition_broadcast(128), w=[tb[b]], grp=f"m_b{b}", eng="pool")
            for k in range(KD):
                P.mm(pst[b][:], cbc[:, k, :], wt[b][:, k, :], start=(k == 0), stop=(k == KD - 1),
                     r=[tc_, tw[b]], w=[tp[b]])
            P.tt(ot[b][:], pst[b][:], bt[b][:], ALU.add, r=[tp[b], tb[b]], w=[to[b]])
            P.dma(S["modbc"][:, off:off + 512], ot[b][:], r=[to[b]], grp=f"m_o{b}", eng="pool")
        P.end_phase()

    def norm(self, wvec, mod_off, final=False):
        P, c, S, C = self.P, self.cfg, self.S, self.C
        D, T, KD = c.D, c.T, c.KD
        P.begin_phase()
        sbc = P.sb("n_s", [128, D], F32)
        tsb = Tok()
        P.dma(sbc[:], wvec.partition_broadcast(128), w=[tsb], grp="n_w")
        if not final:
            shbc = P.sb("n_sh", [128, D], F32)
            tmpm = P.sb("n_tm", [128, D], F32)
            tsh = Tok()
            ttm = Tok()
            P.dma(shbc[:], S["modbc"][:, mod_off:mod_off + D], w=[tsh], grp="n_sh")
            P.dma(tmpm[:], S["modbc"][:, mod_off + D:mod_off + 2 * D], w=[ttm], grp="n_tm")
            P.ts(tmpm[:], tmpm[:], 1.0, None, ALU.add, r=[ttm], w=[ttm])
            P.tt(sbc[:], sbc[:], tmpm[:], ALU.mult, r=[tsb, ttm], w=[tsb])
        nb = 2
        xt = [P.sb(f"n_x{i}", [128, D], F32) for i in range(nb)]
        tx = [Tok() for _ in range(nb)]
        junk = P.sb("n_junk", [128, D], BF16)
        tj = Tok()
        st = [P.sb(f"n_st{i}", [128, 4], F32) for i in range(nb)]
        tst = [Tok() for _ in range(nb)]
        yt = [P.sb(f"n_y{i}", [128, D], F32) for i in range(nb)]
        ty = [Tok() for _ in range(nb)]
        if not final:
            hb = [P.sb(f"n_h{i}", [128, D], BF16) for i in range(nb)]
            th = [Tok() for _ in range(nb)]
            hT = [P.sb(f"n_hT{i}", [128, KD, 512], BF16) for i in range(nb)]
            thT = [Tok() for _ in range(nb)]
            NPB = (KD + 7) // 8
            pt = [P.ps(f"n_pt{i}", [128, 8, 128], BF16) for i in range(min(4, max(2, NPB)))]
            tpt = [PTok() for _ in pt]
        pi = 0
        for it in range(T // 128):
            b = it % nb
            P.dma(xt[b][:], S["xres"][it * 128:(it + 1) * 128, :], w=[tx[b]], grp=f"n_x{b}",
                  eng=("sp" if it % 2 == 0 else "act"))
            P.act(junk[:], xt[b][:], AF.Square, r=[tx[b]], w=[tj, tst[b]], accum_out=st[b][:, 0:1])
            P.ts(st[b][:, 1:2], st[b][:, 0:1], 1.0 / D, EPS, ALU.mult, ALU.add, r=[tst[b]], w=[tst[b]])
            P.act(st[b][:, 2:3], st[b][:, 1:2], AF.Sqrt, r=[tst[b]], w=[tst[b]])
            P.op("dve", lambda e, o=st[b][:, 3:4], i_=st[b][:, 2:3]: e.reciprocal(o, i_), r=[tst[b]], w=[tst[b]])
            P.stt(yt[b][:], xt[b][:], st[b][:, 3:4], sbc[:], ALU.mult, ALU.mult, r=[tx[b], tst[b], tsb], w=[ty[b]])
            if final:
                P.dma(self.out[it * 128:(it + 1) * 128, :], yt[b][:], r=[ty[b]], grp=f"n_o{b}", eng="pool")
                continue
            P.tt(hb[b][:], yt[b][:], shbc[:], ALU.add, r=[ty[b], tsh], w=[th[b]], eng="pool")
            sup = it // 4
            hb_ = sup % nb
            sub = it % 4
            for k0 in range(0, KD, 8):
                kn = min(8, KD - k0)
                pb = pi % len(pt)
                pi += 1
                for k in range(kn):
                    P.tr(pt[pb][:, k, :], hb[b][:, (k0 + k) * 128:(k0 + k + 1) * 128], C["ident"][:],
                         r=[th[b], self.Ctok], w=[tpt[pb]])
                P.copy(hT[hb_][:, k0:k0 + kn, sub * 128:(sub + 1) * 128], pt[pb][:, :kn, :], r=[tpt[pb]],
                       w=[thT[hb_]], eng=("act" if (k0 // 8) % 2 == 0 else "dve"))
            if sub == 3:
                P.dma(S["hT"][:, sup * 512:(sup + 1) * 512].rearrange("(k p) t -> p k t", p=128), hT[hb_][:],
                      r=[thT[hb_]], grp=f"n_hs{hb_}", eng="pool")
        P.end_phase()

    def proj_fm(self, wsb, tw, ncols, epi, pre=None):
        P, c, S = self.P, self.cfg, self.S
        KD, T = c.KD, c.T
        nb = 2
        hs = [P.sb(f"pf_h{i}", [128, KD, 512], BF16) for i in range(nb)]
        th = [Tok() for _ in range(nb)]
        pst = [P.ps(f"pf_ps{i}", [128, 512], F32) for i in range(3)]
        tps = [PTok() for _ in range(3)]
        pi = 0
        for tt in range(T // 512):
            b = tt % nb
            P.dma(hs[b][:], S["hT"][:, tt * 512:(tt + 1) * 512].rearrange("(k p) t -> p k t", p=128), w=[th[b]],
                  grp=f"pf_h{b}", eng=("sp" if tt % 2 == 0 else "act"))
            if pre is not None:
                pre(tt)
            for ct in range(ncols // 128):
                pb = pi % 3
                pi += 1
                for k in range(KD):
                    P.mm(pst[pb][:], wsb[:, k, ct * 128:(ct + 1) * 128], hs[b][:, k, :], start=(k == 0),
                         stop=(k == KD - 1), r=[tw, th[b]], w=[tps[pb]])
                epi(pst[pb], tps[pb], ct, tt)

    def proj_tm(self, wsb, tw, ncols, epi, pre=None):
        P, c, S = self.P, self.cfg, self.S
        KD, T = c.KD, c.T
        nb = 2
        hs = [P.sb(f"pt_h{i}", [128, KD, 512], BF16) for i in range(nb)]
        th = [Tok() for _ in range(nb)]
        pst = [P.ps(f"pt_ps{i}", [128, 512], F32) for i in range(3)]
        tps = [PTok() for _ in range(3)]
        pi = 0
        for tt in range(T // 512):
            b = tt % nb
            P.dma(hs[b][:], S["hT"][:, tt * 512:(tt + 1) * 512].rearrange("(k p) t -> p k t", p=128), w=[th[b]],
                  grp=f"pt_h{b}", eng=("sp" if tt % 2 == 0 else "act"))
            for sub in range(4):
                it = tt * 4 + sub
                if pre is not None:
                    pre(it)
                for c0 in range(0, ncols, 512):
                    w = min(512, ncols - c0)
                    pb = pi % 3
                    pi += 1
                    for k in range(KD):
                        P.mm(pst[pb][:, :w], hs[b][:, k, sub * 128:(sub + 1) * 128], wsb[:, k, c0:c0 + w],
                             start=(k == 0), stop=(k == KD - 1), r=[tw, th[b]], w=[tps[pb]])
                    epi(pst[pb], tps[pb], it, c0, w)

    def mamba_xbc(self, li, g):
        P, c, I, S, C = self.P, self.cfg, self.I, self.S, self.C
        KD, T = c.KD, c.T
        P.begin_phase()
        wsb = P.sb("mx_w", [128, KD, 1280], BF16)
        tw = Tok()
        Wb = S["wb_in"]
        xo = c.AI + g * 1024
        bo = c.AI + c.AI + g * 128
        co = c.AI + c.AI + c.GN + g * 128
        Wv = lambda o, n: Wb[:, o:o + n].rearrange("(k p) n -> p k n", p=128)
        P.dma(wsb[:, :, 0:1024], Wv(xo, 1024), w=[tw], grp="mx_w")
        P.dma(wsb[:, :, 1024:1152], Wv(bo, 128), w=[tw], grp="mx_w")
        P.dma(wsb[:, :, 1152:1280], Wv(co, 128), w=[tw], grp="mx_w")
        cw = P.sb("mx_cw", [128, 10, 4], F32)
        cb = P.sb("mx_cb", [128, 10], F32)
        tcw = Tok()
        cvT = I["a_convT"]
        cvb = I["a_conv_b"]
        for (o, ct0, n) in ((xo - c.AI, 0, 8), (bo - c.AI, 8, 1), (co - c.AI, 9, 1)):
            P.dma(cw[:, ct0:ct0 + n, :], cvT[li, o:o + n * 128, :].rearrange("(t p) k -> p t k", p=128), w=[tcw],
                  grp="mx_cw")
            for t_ in range(n):
                P.dma(cb[:, ct0 + t_:ct0 + t_ + 1],
                      cvb[li:li + 1, o + t_ * 128:o + (t_ + 1) * 128].rearrange("o p -> p o"), w=[tcw], grp="mx_cw")
        halo = P.sb("mx_halo", [128, 10, 3], F32)
        thalo = [Tok() for _ in range(10)]
        P.memset(halo[:], 0.0, w=thalo)
        nb = 2
        uext = [P.sb(f"mx_u{i}", [128, 515], F32) for i in range(nb)]
        tu = [Tok() for _ in range(nb)]
        acc = [P.sb(f"mx_a{i}", [128, 512], F32) for i in range(nb)]
        ta = [Tok() for _ in range(nb)]
        xc = [P.sb(f"mx_xc{i}", [128, 512], BF16) for i in range(nb)]
        txc = [Tok() for _ in range(nb)]
        ptr = [P.ps(f"mx_pt{i}", [128, 8, 128], BF16) for i in range(2)]
        tptr = [PTok() for _ in range(2)]
        xtm = [P.sb(f"mx_xtm{i}", [128, 4, 1024], BF16) for i in range(nb)]
        txtm = [Tok() for _ in range(nb)]
        btm = [P.sb(f"mx_btm{i}", [128, 4, 128], BF16) for i in range(nb)]
        tbtm = [Tok() for _ in range(nb)]
        cnt = [0]

        def epi(ps, tps, ct, tt):
            i = cnt[0]
            cnt[0] += 1
            b = i % nb
            tb_ = tt % nb
            P.copy(uext[b][:, 3:515], ps[:], r=[tps], w=[tu[b]], eng="act")
            P.copy(uext[b][:, 0:3], halo[:, ct, :], r=[thalo[ct]], w=[tu[b]], eng="pool")
            P.act(acc[b][:], ps[:], AF.Identity, r=[tps, tcw], w=[ta[b]], scale=cw[:, ct, 3:4], bias=cb[:, ct:ct + 1])
            for k in (2, 1, 0):
                P.stt(acc[b][:], uext[b][:, k:k + 512], cw[:, ct, k:k + 1], acc[b][:], ALU.mult, ALU.add,
                      r=[tu[b], tcw, ta[b]], w=[ta[b]])
            P.copy(halo[:, ct, :], uext[b][:, 512:515], r=[tu[b]], w=[thalo[ct]], eng="pool")
            P.act(xc[b][:], acc[b][:], AF.Silu, r=[ta[b]], w=[txc[b]])
            if ct <= 8:
                pb = i % 2
                for j in range(4):
                    P.tr(ptr[pb][:, j, :], xc[b][:, j * 128:(j + 1) * 128], C["ident"][:], r=[txc[b], self.Ctok],
                         w=[tptr[pb]])
                if ct < 8:
                    P.copy(xtm[tb_][:, :, ct * 128:(ct + 1) * 128], ptr[pb][:, 0:4, :], r=[tptr[pb]], w=[txtm[tb_]],
                           eng=("dve" if ct % 2 == 0 else "act"))
                    if ct == 7:
                        P.dma(S["xs"][tt * 512:(tt + 1) * 512, :].rearrange("(j p) c -> p j c", p=128), xtm[tb_][:],
                              r=[txtm[tb_]], grp=f"mx_xs{tb_}", eng="pool")
                else:
                    P.copy(btm[tb_][:], ptr[pb][:, 0:4, :], r=[tptr[pb]], w=[tbtm[tb_]], eng="dve")
                    P.dma(S["Bm"][tt * 512:(tt + 1) * 512, :].rearrange("(j p) c -> p j c", p=128), btm[tb_][:],
                          r=[tbtm[tb_]], grp=f"mx_bm{tb_}", eng="pool")
            if ct == 8:
                P.dma(S["BT"][:, tt * 512:(tt + 1) * 512], xc[b][:], r=[txc[b]], grp=f"mx_bt{b}", eng="pool")
            if ct == 9:
                P.dma(S["CT"][:, tt * 512:(tt + 1) * 512], xc[b][:], r=[txc[b]], grp=f"mx_ct{b}", eng="pool")

        self.proj_fm(wsb, tw, 1280, epi)
        P.end_phase()

    def mamba_scan(self, li, g):
        P, c, I, S, C = self.P, self.cfg, self.I, self.S, self.C
        KD, T = c.KD, c.T
        P.begin_phase()
        Wb = S["wb_in"]
        wz = P.sb("ms_w", [128, KD, 1040], BF16)
        tw = Tok()
        Wv = lambda o, n: Wb[:, o:o + n].rearrange("(k p) n -> p k n", p=128)
        P.dma(wz[:, :, 0:1024], Wv(g * 1024, 1024), w=[tw], grp="ms_w")
        P.dma(wz[:, :, 1024:1040], Wv(c.AI + c.CONV + g * 16, 16), w=[tw], grp="ms_w")
        pc = P.sb("ms_pc", [128, 64], F32)
        tpc = Tok()
        hs = slice(g * 16, (g + 1) * 16)
        P.dma(pc[:, 0:16], I["a_A_log"][li:li + 1, hs].partition_broadcast(128), w=[tpc], grp="ms_pc")
        P.dma(pc[:, 16:32], I["a_dt_bias"][li:li + 1, hs].partition_broadcast(128), w=[tpc], grp="ms_pc")
        P.dma(pc[:, 32:48], I["a_D"][li:li + 1, hs].partition_broadcast(128), w=[tpc], grp="ms_pc")
        P.act(pc[:, 0:16], pc[:, 0:16], AF.Exp, r=[tpc], w=[tpc])
        P.ts(pc[:, 0:16], pc[:, 0:16], -1.0, None, ALU.mult, r=[tpc], w=[tpc])
        Dd = P.sb("ms_Dd", [128, 16, 128], BF16)
        tDd = Tok()
        for j in range(16):
            P.ts(Dd[:, j, :], C["identf"][:], pc[:, 32 + j:33 + j], None, ALU.mult, r=[tpc, self.Ctok], w=[tDd])
        gw = P.sb("ms_gw", [128, 1024], F32)
        tgw = Tok()
        P.dma(gw[:], I["a_gnorm"][li:li + 1, g * 1024:(g + 1) * 1024].partition_broadcast(128), w=[tgw], grp="ms_gw")
        stf = P.sb("ms_stf", [128, 1024], F32)
        stb = P.sb("ms_stb", [128, 1024], BF16)
        tstf = Tok()
        tstb = Tok()
        P.memset(stf[:], 0.0, w=[tstf])
        P.memset(stb[:], 0.0, w=[tstb], eng="pool")
        pA = P.ps("ms_pA", [128, 1024], F32)
        pB = P.ps("ms_pB", [128, 512], F32)
        pC = P.ps("ms_pC", [128, 2048], F32)
        pD = P.ps("ms_pD", [128, 8, 128], BF16)
        tA = [PTok(), PTok()]
        tB = PTok()
        tCk = [PTok() for _ in range(4)]
        tD = PTok()
        nb = 2
        hsb = [P.sb(f"ms_h{i}", [128, KD, 256], BF16) for i in range(nb)]
        th = [Tok() for _ in range(nb)]
        bts = [P.sb(f"ms_bt{i}", [128, 512], BF16) for i in range(nb)]
        cts = [P.sb(f"ms_ct{i}", [128, 512], BF16) for i in range(nb)]
        tbc = [Tok() for _ in range(nb)]
        xcs = [P.sb(f"ms_x{i}", [128, 1024], BF16) for i in range(nb)]
        bcs = [P.sb(f"ms_b{i}", [128, 128], BF16) for i in range(nb)]
        txb = [Tok() for _ in range(nb)]
        sm = P.sb("ms_sm", [128, 256], F32)
        tsm = Tok()
        zs = P.sb("ms_zs", [128, 1024], F32)
        tzs = Tok()
        rseg = P.sb("ms_rseg", [128, 16, 128], F32)
        trs = Tok()
        LT = P.sb("ms_LT", [128, 16, 128], BF16)
        tLT = Tok()
        CBm = P.sb("ms_CBm", [128, 128], BF16)
        tCB = Tok()
        Wm = P.sb("ms_Wm", [128, 16, 128], BF16)
        tWm = Tok()
        xdt = P.sb("ms_xdt", [128, 1024], BF16)
        txdt = Tok()
        xdd = P.sb("ms_xdd", [128, 1024], BF16)
        txdd = Tok()
        t1 = P.sb("ms_t1", [128, 1024], F32)
        tt1 = Tok()
        yg = P.sb("ms_yg", [128, 1024], F32)
        tyg = Tok()
        junk = P.sb("ms_junk", [128, 1024], BF16)
        tjk = Tok()
        yn = P.sb("ms_yn", [128, 1024], BF16)
        tyn = Tok()
        yTs = [P.sb(f"ms_yT{i}", [128, 8, 512], BF16) for i in range(nb)]
        tyT = [Tok() for _ in range(nb)]
        for ci in range(T // 128):
            sup, sub = ci // 4, ci % 4
            b = sup % nb
            xb_ = ci % nb
            t0 = ci * 128
            hb2 = (ci // 2) % nb
            hsub = ci % 2
            if hsub == 0:
                P.dma(hsb[hb2][:], S["hT"][:, ci * 128:ci * 128 + 256].rearrange("(k p) t -> p k t", p=128),
                      w=[th[hb2]], grp=f"ms_h{hb2}", eng="sp")
            if sub == 0:
                P.dma(bts[b][:], S["BT"][:, sup * 512:(sup + 1) * 512], w=[tbc[b]], grp=f"ms_bc{b}", eng="act")
                P.dma(cts[b][:], S["CT"][:, sup * 512:(sup + 1) * 512], w=[tbc[b]], grp=f"ms_bc{b}", eng="act")
            P.dma(xcs[xb_][:], S["xs"][t0:t0 + 128, :], w=[txb[xb_]], grp=f"ms_x{xb_}", eng="sp")
            P.dma(bcs[xb_][:], S["Bm"][t0:t0 + 128, :], w=[txb[xb_]], grp=f"ms_x{xb_}", eng="sp")
            BTc = bts[b][:, sub * 128:(sub + 1) * 128]
            CTc = cts[b][:, sub * 128:(sub + 1) * 128]
            xc_ = xcs[xb_]
            for cbk in range(2):
                for k in range(KD):
                    P.mm(pA[:, cbk * 512:(cbk + 1) * 512], hsb[hb2][:, k, hsub * 128:(hsub + 1) * 128],
                         wz[:, k, cbk * 512:(cbk + 1) * 512], start=(k == 0), stop=(k == KD - 1), r=[tw, th[hb2]],
                         w=[tA[cbk]])
            for k in range(KD):
                P.mm(pB[:, 0:16], hsb[hb2][:, k, hsub * 128:(hsub + 1) * 128], wz[:, k, 1024:1040], start=(k == 0),
                     stop=(k == KD - 1), r=[tw, th[hb2]], w=[tB])
            P.tt(sm[:, 0:16], pB[:, 0:16], pc[:, 16:32], ALU.add, r=[tB, tpc], w=[tsm])
            P.ts(sm[:, 16:32], sm[:, 0:16], -1.0, None, ALU.mult, r=[tsm], w=[tsm])
            P.tt(sm[:, 16:32], sm[:, 16:32], sm[:, 0:16], ALU.min, r=[tsm], w=[tsm])
            P.act(sm[:, 32:48], sm[:, 16:32], AF.Exp, r=[tsm], w=[tsm])
            P.act(sm[:, 32:48], sm[:, 32:48], AF.Ln, r=[tsm], w=[tsm], bias=1.0)
            P.ts(sm[:, 16:32], sm[:, 0:16], 0.0, None, ALU.max, r=[tsm], w=[tsm])
            P.tt(sm[:, 48:64], sm[:, 16:32], sm[:, 32:48], ALU.add, r=[tsm], w=[tsm])
            P.tt(sm[:, 64:80], sm[:, 48:64], pc[:, 0:16], ALU.mult, r=[tsm, tpc], w=[tsm])
            for cbk in range(2):
                P.act(zs[:, cbk * 512:(cbk + 1) * 512], pA[:, cbk * 512:(cbk + 1) * 512], AF.Silu, r=[tA[cbk]],
                      w=[tzs])
            a_ = sm[:, 64:80]
            P.mm(pB[:, 16:32], C["trif"][:], a_, r=[tsm, self.Ctok], w=[tB])
            P.mm(pB[:, 32:48], C["Uf"][:], a_, r=[tsm, self.Ctok], w=[tB])
            P.mm(pB[:, 48:64], C["onesf"][:], a_, r=[tsm, self.Ctok], w=[tB])
            P.act(sm[:, 80:128], pB[:, 16:64], AF.Exp, r=[tB], w=[tsm])
            eacs = sm[:, 80:96]
            dec = sm[:, 96:112]
            dcl = sm[:, 112:128]
            P.tt(sm[:, 128:144], sm[:, 48:64], dec, ALU.mult, r=[tsm], w=[tsm])
            P.tt(rseg[:], a_.unsqueeze(2).broadcast_to([128, 16, 128]),
                 C["trif"][:].unsqueeze(1).broadcast_to([128, 16, 128]), ALU.mult, r=[tsm, self.Ctok], w=[trs],
                 eng="pool")
            rseg2 = rseg[:].rearrange("p j t -> p (j t)")
            LT2 = LT[:].rearrange("p j t -> p (j t)")
            for q in range(4):
                P.mm(pC[:, q * 512:(q + 1) * 512], C["Uf"][:], rseg2[:, q * 512:(q + 1) * 512], r=[trs, self.Ctok],
                     w=[tCk[q]])
            for q in range(4):
                P.act(LT2[:, q * 512:(q + 1) * 512], pC[:, q * 512:(q + 1) * 512], AF.Exp, r=[tCk[q]], w=[tLT])
            P.mm(pB[:, 128:256], BTc, CTc, r=[tbc[b]], w=[tB])
            P.tt(CBm[:], pB[:, 128:256], C["trif"][:], ALU.mult, r=[tB, self.Ctok], w=[tCB])
            P.tt(Wm[:], LT[:], CBm[:].unsqueeze(1).broadcast_to([128, 16, 128]), ALU.mult, r=[tLT, tCB], w=[tWm])
            x3 = xc_[:].rearrange("p (j d) -> p j d", j=16)
            P.tt(xdt[:].rearrange("p (j d) -> p j d", j=16), x3,
                 sm[:, 48:64].unsqueeze(2).broadcast_to([128, 16, 64]), ALU.mult, r=[txb[xb_], tsm], w=[txdt],
                 eng="pool")
            P.tt(xdd[:].rearrange("p (j d) -> p j d", j=16), x3,
                 sm[:, 128:144].unsqueeze(2).broadcast_to([128, 16, 64]), ALU.mult, r=[txb[xb_], tsm], w=[txdd],
                 eng="pool")
            for j in range(16):
                q = (j * 64) // 512
                P.mm(pC[:, j * 64:(j + 1) * 64], Wm[:, j, :], xdt[:, j * 64:(j + 1) * 64], start=True, stop=False,
                     r=[tWm, txdt], w=[tCk[q]])
                P.mm(pC[:, j * 64:(j + 1) * 64], Dd[:, j, :], xc_[:, j * 64:(j + 1) * 64], start=False, stop=True,
                     r=[tDd, txb[xb_]], w=[tCk[q]])
            for hh in range(2):
                P.mm(pA[:, hh * 512:(hh + 1) * 512], CTc, stb[:, hh * 512:(hh + 1) * 512], r=[tbc[b], tstb],
                     w=[tA[hh]])
            for hh in range(2):
                sl = slice(hh * 512, (hh + 1) * 512)
                P.tt(t1[:, sl].rearrange("p (j d) -> p j d", j=8), pA[:, sl].rearrange("p (j d) -> p j d", j=8),
                     eacs[:, hh * 8:(hh + 1) * 8].unsqueeze(2).broadcast_to([128, 8, 64]), ALU.mult,
                     r=[tA[hh], tsm], w=[tt1])
                P.tt(t1[:, sl], t1[:, sl], pC[:, sl], ALU.add, r=[tt1, tCk[hh]], w=[tt1])
            P.tt(yg[:], t1[:], zs[:], ALU.mult, r=[tt1, tzs], w=[tyg])
            P.act(junk[:], yg[:], AF.Square, r=[tyg], w=[tjk, tsm], accum_out=sm[:, 144:145])
            P.ts(sm[:, 145:146], sm[:, 144:145], 1.0 / 1024, EPS, ALU.mult, ALU.add, r=[tsm], w=[tsm])
            P.act(sm[:, 146:147], sm[:, 145:146], AF.Sqrt, r=[tsm], w=[tsm])
            P.op("dve", lambda e: e.reciprocal(sm[:, 147:148], sm[:, 146:147]), r=[tsm], w=[tsm])
            P.stt(yn[:], yg[:], sm[:, 147:148], gw[:], ALU.mult, ALU.mult, r=[tyg, tsm, tgw], w=[tyn])
            for i8 in range(8):
                P.tr(pD[:, i8, :], yn[:, i8 * 128:(i8 + 1) * 128], C["ident"][:], r=[tyn, self.Ctok], w=[tD])
            P.copy(yTs[b][:, :, sub * 128:(sub + 1) * 128], pD[:], r=[tD], w=[tyT[b]], eng="act")
            if sub == 3:
                P.dma(S["yT"][g * 1024:(g + 1) * 1024, sup * 512:(sup + 1) * 512].rearrange("(i p) t -> p i t", p=128),
                      yTs[b][:], r=[tyT[b]], grp=f"ms_yT{b}", eng="pool")
            for hh in range(2):
                P.mm(pC[:, 1024 + hh * 512:1024 + (hh + 1) * 512], bcs[xb_][:], xdd[:, hh * 512:(hh + 1) * 512],
                     r=[txb[xb_], txdd], w=[tCk[2 + hh]])
            P.tt(stf[:].rearrange("p (j d) -> p j d", j=16), stf[:].rearrange("p (j d) -> p j d", j=16),
                 dcl.unsqueeze(2).broadcast_to([128, 16, 64]), ALU.mult, r=[tstf, tsm], w=[tstf])
            for hh in range(2):
                sl = slice(hh * 512, (hh + 1) * 512)
                P.tt(stf[:, sl], stf[:, sl], pC[:, 1024 + hh * 512:1024 + (hh + 1) * 512], ALU.add,
                     r=[tstf, tCk[2 + hh]], w=[tstf])
            P.copy(stb[:], stf[:], r=[tstf], w=[tstb], eng="act")
        P.end_phase()

    def out_proj(self, KC, gate_off, src):
        P, c, S = self.P, self.cfg, self.S
        D, T = c.D, c.T
        NBW = 256
        P.begin_phase()
        gbc = P.sb("op_g", [128, D], F32)
        tg = Tok()
        P.dma(gbc[:], S["modbc"][:, gate_off:gate_off + D], w=[tg], grp="op_g")
        nb = 2
        ysb = [P.sb(f"op_y{i}", [128, KC, 512], BF16) for i in range(1)]
        ty = [Tok() for _ in range(1)]
        wo = [P.sb(f"op_w{i}", [128, KC, NBW], BF16) for i in range(nb)]
        two = [Tok() for _ in range(nb)]
        xb = [P.sb(f"op_x{i}", [128, NBW], F32) for i in range(4)]
        txb = [Tok() for _ in range(4)]
        tm = [P.sb(f"op_t{i}", [128, NBW], F32) for i in range(4)]
        ttm = [Tok() for _ in range(4)]
        pst = [P.ps(f"op_ps{i}", [128, 512], F32) for i in range(4)]
        tps = [PTok() for _ in range(4)]
        Wb = S["wb_out"]
        wi = 0
        ei = 0
        for sup in range(T // 512):
            b = 0
            P.dma(ysb[b][:], src[0:KC * 128, sup * 512:(sup + 1) * 512].rearrange("(k p) t -> p k t", p=128), w=[ty[b]],
                  grp=f"op_y{b}", eng="act")
            for n0 in range(0, D, NBW):
                wb_ = wi % nb
                wi += 1
                P.dma(wo[wb_][:], Wb[0:KC * 128, n0:n0 + NBW].rearrange("(k p) n -> p k n", p=128), w=[two[wb_]],
                      grp=f"op_w{wb_}", eng="sp")
                for m in range(4):
                    e4 = ei % 4
                    ei += 1
                    r0 = sup * 512 + m * 128
                    P.dma(xb[e4][:], S["xres"][r0:r0 + 128, n0:n0 + NBW], w=[txb[e4]], grp=f"op_x{e4}", eng="pool")
                    for k in range(KC):
                        P.mm(pst[e4][:, :NBW], ysb[b][:, k, m * 128:(m + 1) * 128], wo[wb_][:, k, :], start=(k == 0),
                             stop=(k == KC - 1), r=[ty[b], two[wb_]], w=[tps[e4]])
                    P.tt(tm[e4][:], pst[e4][:, :NBW], gbc[:, n0:n0 + NBW], ALU.mult, r=[tps[e4], tg], w=[ttm[e4]])
                    P.tt(tm[e4][:], tm[e4][:], xb[e4][:], ALU.add, r=[ttm[e4], txb[e4]], w=[ttm[e4]], eng="pool")
                    P.dma(S["xres"][r0:r0 + 128, n0:n0 + NBW], tm[e4][:], r=[ttm[e4]], grp=f"op_s{e4}", eng="pool")
        P.end_phase()

    def kv_stream(self):
        P, c, I, S, C = self.P, self.cfg, self.I, self.S, self.C
        KD, T = c.KD, c.T
        Wb = S["wb_kv"]
        Wv = lambda o, n: Wb[:, o:o + n].rearrange("(k p) n -> p k n", p=128)
        P.begin_phase()
        wk = P.sb("kv_wk", [128, KD, c.KVD], BF16)
        tw = Tok()
        P.dma(wk[:], Wv(0, c.KVD), w=[tw], grp="kv_w")
        ko = [P.sb(f"kv_ko{i}", [128, 512], BF16) for i in range(2)]
        tko = [Tok() for _ in range(2)]
        cnt = [0]

        def epik(ps, tps, ct, tt):
            b = cnt[0] % 2
            cnt[0] += 1
            P.copy(ko[b][:], ps[:], r=[tps], w=[tko[b]], eng=("act" if b == 0 else "dve"))
            P.dma(S["KT"][ct * 128:(ct + 1) * 128, tt * 512:(tt + 1) * 512], ko[b][:], r=[tko[b]], grp=f"kv_ko{b}",
                  eng="pool")

        self.proj_fm(wk, tw, c.KVD, epik)
        P.end_phase()
        P.begin_phase()
        NV = c.KVD
        BH = c.BH
        wv = P.sb("kv_wv", [128, KD, NV + BH], BF16)
        tw = Tok()
        P.dma(wv[:], Wv(c.KVD, NV + BH), w=[tw], grp="kv_w")
        bfb = P.sb("kv_bf", [128, BH], F32)
        tbf = Tok()
        P.dma(bfb[:], I["b_f"][0:1, :].partition_broadcast(128), w=[tbf], grp="kv_bf")
        vo = [P.sb(f"kv_vo{i}", [128, 512], BF16) for i in range(2)]
        tvo = [Tok() for _ in range(2)]
        sm = P.sb("kv_sm", [128, 4 * BH], F32)
        tsm = Tok()
        carT = P.sb("kv_carT", [BH, 4], F32)
        tcar = Tok()
        P.memset(carT[:], 0.0, w=[tcar])
        pF = P.ps("kv_pF", [128, 512], F32)
        tpF = PTok()
        ft = [P.sb(f"kv_ft{i}", [BH, 128], F32) for i in range(2)]
        tft = [Tok() for _ in range(2)]
        n3 = [P.sb(f"kv_n3{i}", [BH, 3, 128], BF16) for i in range(2)]
        tn3 = [Tok() for _ in range(2)]
        wk_ = P.sb("kv_wk2", [BH, 4, 128], F32)
        twk = Tok()
        cnt = [0]
        SQ = math.sqrt(128.0)

        def epiv(ps, tps, it, c0, w):
            if c0 < NV:
                wv_ = min(w, NV - c0)
                b = cnt[0] % 2
                cnt[0] += 1
                P.copy(vo[b][:, :wv_], ps[:, :wv_], r=[tps], w=[tvo[b]], eng=("act" if b == 0 else "dve"))
                P.dma(S["V"][it * 128:(it + 1) * 128, c0:c0 + wv_], vo[b][:, :wv_], r=[tvo[b]], grp=f"kv_vo{b}",
                      eng="pool")
            if c0 + w <= NV:
                return
            o = NV - c0
            x_ = sm[:, 0:BH]
            P.tt(x_, ps[:, o:o + BH], bfb[:], ALU.add, r=[tps, tbf], w=[tsm])
            P.ts(x_, x_, -1.0, None, ALU.mult, r=[tsm], w=[tsm])
            P.ts(sm[:, BH:2 * BH], x_, -1.0, None, ALU.mult, r=[tsm], w=[tsm])
            P.tt(sm[:, BH:2 * BH], sm[:, BH:2 * BH], x_, ALU.min, r=[tsm], w=[tsm])
            P.act(sm[:, BH:2 * BH], sm[:, BH:2 * BH], AF.Exp, r=[tsm], w=[tsm])
            P.act(sm[:, BH:2 * BH], sm[:, BH:2 * BH], AF.Ln, r=[tsm], w=[tsm], bias=1.0)
            P.ts(sm[:, 2 * BH:3 * BH], x_, 0.0, None, ALU.max, r=[tsm], w=[tsm])
            P.tt(sm[:, 3 * BH:4 * BH], sm[:, 2 * BH:3 * BH], sm[:, BH:2 * BH], ALU.add, r=[tsm], w=[tsm])
            lf = sm[:, 3 * BH:4 * BH]
            P.ts(lf, lf, -1.0, None, ALU.mult, r=[tsm], w=[tsm])
            P.mm(pF[0:BH, 0:128], lf, C["trif"][:], r=[tsm, self.Ctok], w=[tpF])
            P.mm(pF[0:BH, 128:129], lf, C["onesf"][:, 0:1], r=[tsm, self.Ctok], w=[tpF])
            b = it % 2
            P.ts(ft[b][:], pF[0:BH, 0:128], carT[:, 0:1], None, ALU.add, r=[tpF, tcar], w=[tft[b]])
            P.tt(carT[:, 0:1], carT[:, 0:1], pF[0:BH, 128:129], ALU.add, r=[tcar, tpF], w=[tcar])
            P.dma(S["FT"][:, it * 128:(it + 1) * 128], ft[b][:], r=[tft[b]], grp=f"kv_ft{b}", eng="pool")
            P.ts(wk_[:, 0, :], ft[b][:], -SQ, None, ALU.mult, r=[tft[b]], w=[twk])
            P.copy(n3[b][:, 0, :], wk_[:, 0, :], r=[twk], w=[tn3[b]])
            P.copy(wk_[:, 1, :], n3[b][:, 0, :], r=[tn3[b]], w=[twk])
            P.tt(wk_[:, 2, :], wk_[:, 0, :], wk_[:, 1, :], ALU.subtract, r=[twk], w=[twk])
            P.copy(n3[b][:, 1, :], wk_[:, 2, :], r=[twk], w=[tn3[b]])
            P.copy(wk_[:, 1, :], n3[b][:, 1, :], r=[tn3[b]], w=[twk])
            P.tt(wk_[:, 3, :], wk_[:, 2, :], wk_[:, 1, :], ALU.subtract, r=[twk], w=[twk])
            P.copy(n3[b][:, 2, :], wk_[:, 3, :], r=[twk], w=[tn3[b]])
            for part in range(3):
                P.dma(S["NF3"][part, :, it * 128:(it + 1) * 128], n3[b][:, part, :], r=[tn3[b]], grp=f"kv_n3{b}",
                      eng="pool")

        self.proj_tm(wv, tw, NV + BH, epiv)
        P.end_phase()

    def attn_proj(self):
        P, c, S = self.P, self.cfg, self.S
        KD, T = c.KD, c.T
        Wb = S["wb_in"]
        Wv = lambda o, n: Wb[:, o:o + n].rearrange("(k p) n -> p k n", p=128)
        CG = 1024 if c.BI >= 1024 else c.BI
        for c0 in range(0, c.BI, CG):
            P.begin_phase()
            wq = P.sb("ap_wq", [128, KD, CG], BF16)
            tw = Tok()
            P.dma(wq[:], Wv(c0, CG), w=[tw], grp="ap_w")
            qo = [P.sb(f"ap_qo{i}", [128, 512], BF16) for i in range(2)]
            tqo = [Tok() for _ in range(2)]
            cnt = [0]

            def epiq(ps, tps, ct, tt, c0=c0):
                b = cnt[0] % 2
                cnt[0] += 1
                P.copy(qo[b][:], ps[:], r=[tps], w=[tqo[b]], eng=("act" if b == 0 else "dve"))
                P.dma(S["QT"][c0 + ct * 128:c0 + (ct + 1) * 128, tt * 512:(tt + 1) * 512], qo[b][:], r=[tqo[b]],
                      grp=f"ap_qo{b}", eng="pool")

            self.proj_fm(wq, tw, CG, epiq)
            P.end_phase()
        for c0 in range(0, c.BI, CG):
            P.begin_phase()
            wz = P.sb("ap_wz", [128, KD, CG], BF16)
            tw = Tok()
            P.dma(wz[:], Wv(c.BI + c0, CG), w=[tw], grp="ap_w")
            zo = [P.sb(f"ap_zo{i}", [128, 512], BF16) for i in range(2)]
            tzo = [Tok() for _ in range(2)]
            cnt = [0]

            def epiz(ps, tps, it, cc, w, c0=c0):
                b = cnt[0] % 2
                cnt[0] += 1
                P.act(zo[b][:, :w], ps[:, :w], AF.Silu, r=[tps], w=[tzo[b]])
                P.dma(S["ZS"][it * 128:(it + 1) * 128, c0 + cc:c0 + cc + w], zo[b][:, :w], r=[tzo[b]],
                      grp=f"ap_zo{b}", eng="pool")

            self.proj_tm(wz, tw, CG, epiz)
            P.end_phase()

    def attn_core(self):
        P, c, S, C = self.P, self.cfg, self.S, self.C
        T = c.T
        NT = T // 128
        scale = 128.0 ** -0.5
        for hk in range(c.NKV):
            P.begin_phase()
            KT = P.sb("at_KT", [128, T], BF16)
            Vs = P.sb("at_V", [128, NT, 128], BF16)
            QT = P.sb("at_QT", [128, 4, T], BF16)
            NF = P.sb("at_NF", [3, 4, T], BF16)
            Fc = P.sb("at_Fc", [128, 4, NT], F32)
            tl = Tok()
            P.dma(KT[:], S["KT"][hk * 128:(hk + 1) * 128, :], w=[tl], grp="at_l")
            P.dma(Vs[:], S["V"][:, hk * 128:(hk + 1) * 128].rearrange("(i p) d -> p i d", p=128), w=[tl], grp="at_l",
                  eng="act")
            P.dma(QT[:], S["QT"][hk * 512:(hk + 1) * 512, :].rearrange("(g p) t -> p g t", p=128), w=[tl], grp="at_l")
            for part in range(3):
                P.dma(NF[part:part + 1, :, :], S["NF3"][part:part + 1, hk * 4:(hk + 1) * 4, :], w=[tl], grp="at_l",
                      eng="act")
            ftl = P.sb("at_ftl", [NT, 4, 128], F32)
            nsp = 2
            sps = [P.ps(f"at_s{i}", [128, 512], F32) for i in range(nsp)]
            tsp = [PTok() for _ in range(nsp)]
            pF = sps[0]
            tpF = tsp[0]
            for g in range(4):
                h = hk * 4 + g
                P.dma(ftl[:, g, :], S["FT"][h:h + 1, :].rearrange("o (i p) -> (o i) p", p=128), w=[tl], grp="at_l",
                      eng="act")
            for g in range(4):
                P.mm(pF[:, g * NT:(g + 1) * NT], ftl[:, g, :], C["identf"][0:NT, 0:NT], r=[tl, self.Ctok], w=[tpF])
            P.copy(Fc[:].rearrange("p g i -> p (g i)"), pF[:, 0:4 * NT], r=[tpF], w=[tl])
            ptp = [P.ps(f"at_pt{i}", [128, 8, 128], BF16) for i in range(2)]
            tptp = [PTok() for _ in range(2)]
            ops = [P.ps(f"at_o{i}", [128, 512], F32) for i in range(2)]
            tops = [PTok() for _ in range(2)]
            otp = P.ps("at_otp", [128, 8, 128], BF16)
            totp = PTok()
            Pb = [P.sb(f"at_P{i}", [128, 512], BF16) for i in range(2)]
            tPb = [Tok() for _ in range(2)]
            PT = [P.sb(f"at_PT{i}", [128, 4, 128], BF16) for i in range(2)]
            tPT = [Tok() for _ in range(2)]
            smk = P.sb("at_smk", [128, 128], F32)
            tsmk = Tok()
            rs = [P.sb(f"at_rs{i}", [128, 40], F32) for i in range(2)]
            trs = [Tok() for _ in range(2)]
            zt = [P.sb(f"at_z{i}", [128, 512], BF16) for i in range(2)]
            tz = [Tok() for _ in range(2)]
            og = [P.sb(f"at_og{i}", [128, 512], BF16) for i in range(2)]
            tog = [Tok() for _ in range(2)]
            oT = [P.sb(f"at_oT{i}", [128, 4, 512], BF16) for i in range(2)]
            toT = [Tok() for _ in range(2)]
            bi = 0
            hi = 0
            for qt in range(NT):
                zb = qt % 2
                sup, sub = qt // 4, qt % 4
                ob = sup % 2
                P.dma(zt[zb][:], S["ZS"][qt * 128:(qt + 1) * 128, hk * 512:(hk + 1) * 512], w=[tz[zb]],
                      grp=f"at_z{zb}", eng="sp")
                nblk = qt // 4 + 1
                for g in range(4):
                    hb = hi % 2
                    hi += 1
                    qsl = QT[:, g, qt * 128:(qt + 1) * 128]
                    ncol = 0
                    first = True
                    for blk in range(nblk):
                        diag = (blk == nblk - 1)
                        w = (sub + 1) * 128 if diag else 512
                        k0 = blk * 512
                        sb_ = bi % nsp
                        pb_ = bi % 2
                        bi += 1
                        P.mm(sps[sb_][:, :w], qsl, KT[:, k0:k0 + w], start=True, stop=False, r=[tl], w=[tsp[sb_]])
                        P.mm(sps[sb_][:, :w], C["ones3"][:], NF[:, g, k0:k0 + w], start=False, stop=True,
                             r=[tl, self.Ctok], w=[tsp[sb_]])
                        wn = w - 128 if diag else w
                        if wn > 0:
                            P.act(Pb[pb_][:, :wn], sps[sb_][:, :wn], AF.Exp, r=[tsp[sb_], tl], w=[tPb[pb_], trs[hb]],
                                  scale=scale, bias=Fc[:, g, qt:qt + 1], accum_out=rs[hb][:, ncol:ncol + 1])
                            ncol += 1
                        if diag:
                            P.tt(smk[:], sps[sb_][:, w - 128:w], C["maskb"][:], ALU.add, r=[tsp[sb_], self.Ctok],
                                 w=[tsmk])
                            P.act(Pb[pb_][:, w - 128:w], smk[:], AF.Exp, r=[tsmk, tl], w=[tPb[pb_], trs[hb]],
                                  scale=scale, bias=Fc[:, g, qt:qt + 1], accum_out=rs[hb][:, ncol:ncol + 1])
                            ncol += 1
                        nj = w // 128
                        for j in range(nj):
                            P.tr(ptp[pb_][:, j, :], Pb[pb_][:, j * 128:(j + 1) * 128], C["ident"][:],
                                 r=[tPb[pb_], self.Ctok], w=[tptp[pb_]])
                        P.copy(PT[pb_][:, :nj, :], ptp[pb_][:, :nj, :], r=[tptp[pb_]], w=[tPT[pb_]],
                               eng=("dve" if bi % 2 == 0 else "act"))
                        for j in range(nj):
                            last = diag and (j == nj - 1)
                            P.mm(ops[hb][:, 0:128], PT[pb_][:, j, :], Vs[:, blk * 4 + j, :], start=first, stop=last,
                                 r=[tPT[pb_], tl], w=[tops[hb]])
                            first = False
                    P.op("dve", lambda e, o_=rs[hb][:, 32:33], i_=rs[hb][:, 0:ncol]: e.reduce_sum(o_, i_, AX.X),
                         r=[trs[hb]], w=[trs[hb]])
                    P.op("dve", lambda e, o_=rs[hb][:, 33:34], i_=rs[hb][:, 32:33]: e.reciprocal(o_, i_),
                         r=[trs[hb]], w=[trs[hb]])
                    P.stt(og[zb][:, g * 128:(g + 1) * 128], ops[hb][:, 0:128], rs[hb][:, 33:34], zt[zb][:, g * 128:(g + 1) * 128],
                          ALU.mult, ALU.mult, r=[tops[hb], trs[hb], tz[zb]], w=[tog[zb]])
                for g in range(4):
                    P.tr(otp[:, g, :], og[zb][:, g * 128:(g + 1) * 128], C["ident"][:], r=[tog[zb], self.Ctok],
                         w=[totp])
                P.copy(oT[ob][:, :, sub * 128:(sub + 1) * 128], otp[:, 0:4, :], r=[totp], w=[toT[ob]], eng="act")
                if sub == 3:
                    P.dma(S["yT"][hk * 512:(hk + 1) * 512, sup * 512:(sup + 1) * 512].rearrange("(g p) t -> p g t", p=128),
                          oT[ob][:], r=[toT[ob]], grp=f"at_oT{ob}", eng="pool")
            P.end_phase()

    def copy_x(self):
        P, c, I, S = self.P, self.cfg, self.I, self.S
        P.begin_phase()
        t = [P.sb(f"cx{i}", [128, c.D], F32) for i in range(2)]
        tk = [Tok() for _ in range(2)]
        for it in range(c.T // 128):
            b = it % 2
            P.dma(t[b][:], I["x"][it * 128:(it + 1) * 128, :], w=[tk[b]], grp=f"cx_l{b}", eng="sp")
            P.dma(S["xres"][it * 128:(it + 1) * 128, :], t[b][:], r=[tk[b]], grp=f"cx_s{b}", eng="act")
        P.end_phase()

    def build(self, stop_after=None):
        c, I, S = self.cfg, self.I, self.S
        D = c.D

        def done(tag):
            return stop_after == tag

        self.copy_x()
        self.mods()
        self.dbg_out("modbc", S["modbc"], [128, 14 * D], F32)
        if not done("mods"):
            for li in range(c.NA):
                self.cast_w(I[f"a_in{li}"], S["wb_in"], D, c.APROJ)
                self.cast_w(I[f"a_out{li}"], S["wb_out"], c.AI, D)
                self.norm(I["norm_w"][li:li + 1, :], li * 3 * D)
                if li == 0:
                    self.dbg_out("hT0", S["hT"], [D, c.T], BF16)
                if done("norm0"):
                    break
                for g in range(c.NG):
                    self.mamba_xbc(li, g)
                    if li == 0 and g == 0:
                        self.dbg_out("xs", S["xs"], [c.T, 1024], BF16)
                        self.dbg_out("BT", S["BT"], [128, c.T], BF16)
                        self.dbg_out("CT", S["CT"], [128, c.T], BF16)
                        self.dbg_out("Bm", S["Bm"], [c.T, 128], BF16)
                    if done("xbc0"):
                        break
                    self.mamba_scan(li, g)
                if done("xbc0"):
                    break
                if li == 0:
                    self.dbg_out("yT0", S["yT"], [c.AI, c.T], BF16)
                if done("scan0"):
                    break
                self.out_proj(c.AI // 128, li * 3 * D + 2 * D, S["yT"])
                if li == 0:
                    self.dbg_out("x1", S["xres"], [c.T, D], F32)
                if done("layer0"):
                    break
            else:
                self.dbg_out("xA", S["xres"], [c.T, D], F32)
                if not done("mamba"):
                    self.cast_w(I["w_kv"], S["wb_kv"], D, 2 * c.KVD)
                    self.cast_w(I["w_f"], S["wb_kv"], D, c.BH, d0=2 * c.KVD)
                    self.norm(I["kv_norm"][0:1, :], 12 * D)
                    self.kv_stream()
                    self.dbg_out("KT", S["KT"], [c.KVD, c.T], BF16)
                    self.dbg_out("V", S["V"], [c.T, c.KVD], BF16)
                    self.dbg_out("FT", S["FT"], [c.BH, c.T], F32)
                    if not done("kv"):
                        for lj in range(c.NB):
                            li = c.NA + lj
                            self.cast_w(I[f"b_in{lj}"], S["wb_in"], D, 2 * c.BI)
                            self.cast_w(I[f"b_out{lj}"], S["wb_out"], c.BI, D)
                            self.norm(I["norm_w"][li:li + 1, :], li * 3 * D)
                            self.attn_proj()
                            if lj == 0:
                                self.dbg_out("QT", S["QT"], [c.BI, c.T], BF16)
                                self.dbg_out("ZS", S["ZS"], [c.T, c.BI], BF16)
                            if done("aproj"):
                                break
                            self.attn_core()
                            if lj == 0:
                                self.dbg_out("oT", S["yT"], [c.BI, c.T], BF16)
                            if done("acore"):
                                break
                            self.out_proj(c.BI // 128, li * 3 * D + 2 * D, S["yT"])
                            if lj == 0:
                                self.dbg_out("x3", S["xres"], [c.T, D], F32)
        self.norm(I["final_norm"][0:1, :], 0, final=True)
        self.P.emit()
        return self.nc


def make_inputs(cfg, b, x, c, ada_w, ada_b, norm_w, a_in_proj, a_conv_w, a_conv_b, a_dt_bias, a_A_log, a_D,
                a_gnorm, a_out_proj, kv_norm, kv_ada_w, kv_ada_b, w_kv, w_f, b_f, b_in_proj, b_out_proj, final_norm):
    f = lambda a: np.ascontiguousarray(a, dtype=np.float32)
    m = {}
    m["x"] = f(x[b])
    m["cT"] = f(np.asarray(c[b]).reshape(cfg.KD, 128).T)
    for i in range(4):
        m[f"ada_w{i}"] = f(ada_w[i])
    m["ada_b"] = f(ada_b)
    m["norm_w"] = f(norm_w)
    for i in range(cfg.NA):
        m[f"a_in{i}"] = f(a_in_proj[i])
        m[f"a_out{i}"] = f(a_out_proj[i])
    m["a_convT"] = f(np.transpose(np.asarray(a_conv_w), (0, 2, 1)))
    m["a_conv_b"] = f(a_conv_b)
    m["a_dt_bias"] = f(a_dt_bias)
    m["a_A_log"] = f(a_A_log)
    m["a_D"] = f(a_D)
    m["a_gnorm"] = f(a_gnorm)
    m["kv_norm"] = f(np.asarray(kv_norm).reshape(1, -1))
    m["kv_ada_w"] = f(kv_ada_w)
    m["kv_ada_b"] = f(np.asarray(kv_ada_b).reshape(1, -1))
    m["w_kv"] = f(w_kv)
    m["w_f"] = f(w_f)
    m["b_f"] = f(np.asarray(b_f).reshape(1, -1))
    for i in range(cfg.NB):
        m[f"b_in{i}"] = f(b_in_proj[i])
        m[f"b_out{i}"] = f(b_out_proj[i])
    m["final_norm"] = f(np.asarray(final_norm).reshape(1, -1))
    return m


def kernel(**inputs):
    x = np.asarray(inputs["x"])
    B, T, D = x.shape
    cfg = Cfg(D, T)
    kb = K(cfg)
    nc = kb.build()
    in_maps = [make_inputs(cfg, b, **inputs) for b in range(B)]
    res = run_bass_kernel_spmd(nc, in_maps, core_ids=list(range(B)))
    return np.stack([np.asarray(res.results[b]["out"]) for b in range(B)], axis=0).astype(np.float32)
```

```python
from contextlib import ExitStack
import math
import numpy as np
import concourse.bass as bass
import concourse.mybir as mybir
from concourse.bass_utils import run_bass_kernel_spmd

F32 = mybir.dt.float32
BF16 = mybir.dt.bfloat16
ALU = mybir.AluOpType
AF = mybir.ActivationFunctionType
AX = mybir.AxisListType
EPS = 1e-6


class Tok:
    __slots__ = ("name", "lw", "rd", "ex")

    def __init__(self, name="", ex=False):
        self.name = name
        self.lw = None
        self.rd = {}
        self.ex = ex


def PTok():
    return Tok("psum", True)


class Prog:
    ENGS = ("pe", "act", "dve", "pool", "sp")

    def __init__(self, nc):
        self.nc = nc
        self.streams = {k: [] for k in self.ENGS}
        self.sems = {}
        self.cnt = {}
        self.isdma = {}
        self.seen = {k: {} for k in self.ENGS}
        self.stack = ExitStack()
        self.phase_stack = None
        self.ninstr = 0
        for k in self.ENGS:
            self._sem("c_" + k, False)

    def _sem(self, key, dma):
        if key not in self.sems:
            self.sems[key] = self.stack.enter_context(self.nc.semaphore(key))
            self.cnt[key] = 0
            self.isdma[key] = dma
        return self.sems[key]

    def begin_phase(self):
        assert self.phase_stack is None
        self.phase_stack = ExitStack()

    def end_phase(self):
        self.sync_all()
        self.phase_stack.close()
        self.phase_stack = None

    def sb(self, name, shape, dt):
        st = self.phase_stack if self.phase_stack is not None else self.stack
        self.uid = getattr(self, "uid", 0) + 1
        return st.enter_context(self.nc.sbuf_tensor(f"sb{self.uid}_{name}", list(shape), dt))

    def ps(self, name, shape, dt):
        st = self.phase_stack if self.phase_stack is not None else self.stack
        self.uid = getattr(self, "uid", 0) + 1
        return st.enter_context(self.nc.psum_tensor(f"ps{self.uid}_{name}", list(shape), dt))

    def op(self, eng, fn, r=(), w=(), dma=None):
        if dma is not None:
            semkey = "d_" + dma
            self._sem(semkey, True)
            inc = 16
        else:
            semkey = "c_" + eng
            inc = 1
        need = {}
        cnt = self.cnt
        isdma = self.isdma
        if any(t.ex for t in r):
            w = list(w) + [t for t in r if t.ex and t not in w]
        for t in r:
            d = t.lw
            if d is not None:
                k, v = d
                if isdma[k]:
                    v = cnt[k]
                if need.get(k, 0) < v:
                    need[k] = v
        for t in w:
            d = t.lw
            if d is not None:
                k, v = d
                if isdma[k]:
                    v = cnt[k]
                if need.get(k, 0) < v:
                    need[k] = v
            for k, v in t.rd.items():
                if isdma[k]:
                    v = cnt[k]
                if need.get(k, 0) < v:
                    need[k] = v
        seen = self.seen[eng]
        waits = []
        for k, v in need.items():
            if k == "c_pe" and eng == "pe" and dma is None:
                continue
            if seen.get(k, 0) >= v:
                continue
            seen[k] = v
            waits.append((self.sems[k], v))
        cnt[semkey] += inc
        val = cnt[semkey]
        for t in w:
            t.lw = (semkey, val)
            t.rd = {}
        for t in r:
            if t.rd.get(semkey, 0) < val:
                t.rd[semkey] = val
        self.streams[eng].append((waits, fn, self.sems[semkey], inc))
        self.ninstr += 1

    def sync_all(self):
        for eng in self.ENGS:
            seen = self.seen[eng]
            waits = []
            for k, v in self.cnt.items():
                if v > 0 and seen.get(k, 0) < v:
                    seen[k] = v
                    waits.append((self.sems[k], v))
            if waits:
                self.streams[eng].append((waits, None, None, 0))

    def emit(self):
        self.sync_all()
        nc = self.nc
        streams = self.streams

        def run(e, lst):
            for waits, fn, semh, inc in lst:
                for s, v in waits:
                    e.wait_ge(s, v)
                if fn is not None:
                    fn(e).then_inc(semh, inc)

        with nc.Block() as block:
            @block.tensor
            def _(e):
                run(e, streams["pe"])

            @block.scalar
            def _(e):
                run(e, streams["act"])

            @block.vector
            def _(e):
                run(e, streams["dve"])

            @block.gpsimd
            def _(e):
                run(e, streams["pool"])

            @block.sync
            def _(e):
                run(e, streams["sp"])
        self.stack.close()

    def dma(self, out, in_, r=(), w=(), grp="g", eng="sp"):
        shp = tuple(out.shape)
        if len(shp) == 3 and shp[1] > 8 and tuple(in_.shape) == shp:
            for k0 in range(0, shp[1], 8):
                k1 = min(shp[1], k0 + 8)
                self._dma1(out[:, k0:k1, :], in_[:, k0:k1, :], r, w, grp, eng)
            return
        self._dma1(out, in_, r, w, grp, eng)

    def _dma1(self, out, in_, r, w, grp, eng):
        self.op(eng, lambda e: e.dma_start(out=out, in_=in_), r=r, w=w, dma=grp)

    def mm(self, out, lhsT, rhs, start=True, stop=True, r=(), w=()):
        self.op("pe", lambda e: e.matmul(out, lhsT, rhs, start=start, stop=stop), r=r, w=w)

    def tr(self, out, in_, ident, r=(), w=()):
        self.op("pe", lambda e: e.transpose(out, in_, ident), r=r, w=w)

    def act(self, out, in_, func, r=(), w=(), **kw):
        self.op("act", lambda e: e.activation(out, in_, func, **kw), r=r, w=w)

    def tt(self, out, in0, in1, op, r=(), w=(), eng="dve"):
        self.op(eng, lambda e: e.tensor_tensor(out, in0, in1, op), r=r, w=w)

    def ts(self, out, in0, s1, s2, op0, op1=None, r=(), w=(), eng="dve"):
        if op1 is None:
            self.op(eng, lambda e: e.tensor_scalar(out, in0, s1, s2, op0), r=r, w=w)
        else:
            self.op(eng, lambda e: e.tensor_scalar(out, in0, s1, s2, op0, op1), r=r, w=w)

    def stt(self, out, in0, scalar, in1, op0, op1, r=(), w=(), eng="dve"):
        self.op(eng, lambda e: e.scalar_tensor_tensor(out, in0, scalar, in1, op0, op1), r=r, w=w)

    def copy(self, out, in_, r=(), w=(), eng="dve"):
        if eng == "act":
            self.op("act", lambda e: e.copy(out, in_), r=r, w=w)
        else:
            self.op(eng, lambda e: e.tensor_copy(out, in_), r=r, w=w)

    def memset(self, ap, val, w=(), eng="dve"):
        self.op(eng, lambda e: e.memset(ap, val), w=w)


class Cfg:
    def __init__(self, D, T, depth=4):
        self.D = D
        self.T = T
        self.KD = D // 128
        self.NA = depth // 2
        self.NB = depth - self.NA
        self.AI = 2 * D
        self.NG = self.AI // 1024
        self.GN = self.NG * 128
        self.CONV = self.AI + 2 * self.GN
        self.AH = self.AI // 64
        self.APROJ = 2 * self.AI + 2 * self.GN + self.AH
        self.BH = D // 128
        self.NKV = self.BH // 4
        self.BI = self.BH * 128
        self.KVD = self.NKV * 128


class K:
    def __init__(self, cfg, debug=()):
        self.cfg = cfg
        self.debug = set(debug)
        nc = bass.Bass("TRN2", target_bir_lowering=False)
        self.nc = nc
        self.P = Prog(nc)
        c = cfg
        D, T = c.D, c.T
        di = lambda n, s, dt=F32: nc.dram_tensor(n, list(s), dt, kind="ExternalInput").ap()
        ds = lambda n, s, dt=BF16: nc.dram_tensor(n, list(s), dt).ap()
        self.I = I = {}
        I["x"] = di("x", [T, D])
        I["cT"] = di("cT", [128, c.KD])
        for i in range(4):
            I[f"ada_w{i}"] = di(f"ada_w{i}", [D, 3 * D])
        I["ada_b"] = di("ada_b", [4, 3 * D])
        I["norm_w"] = di("norm_w", [4, D])
        for i in range(c.NA):
            I[f"a_in{i}"] = di(f"a_in{i}", [D, c.APROJ])
            I[f"a_out{i}"] = di(f"a_out{i}", [c.AI, D])
        I["a_convT"] = di("a_convT", [c.NA, c.CONV, 4])
        I["a_conv_b"] = di("a_conv_b", [c.NA, c.CONV])
        I["a_dt_bias"] = di("a_dt_bias", [c.NA, c.AH])
        I["a_A_log"] = di("a_A_log", [c.NA, c.AH])
        I["a_D"] = di("a_D", [c.NA, c.AH])
        I["a_gnorm"] = di("a_gnorm", [c.NA, c.AI])
        I["kv_norm"] = di("kv_norm", [1, D])
        I["kv_ada_w"] = di("kv_ada_w", [D, 2 * D])
        I["kv_ada_b"] = di("kv_ada_b", [1, 2 * D])
        I["w_kv"] = di("w_kv", [D, 2 * c.KVD])
        I["w_f"] = di("w_f", [D, c.BH])
        I["b_f"] = di("b_f", [1, c.BH])
        for i in range(c.NB):
            I[f"b_in{i}"] = di(f"b_in{i}", [D, 2 * c.BI])
            I[f"b_out{i}"] = di(f"b_out{i}", [c.BI, D])
        I["final_norm"] = di("final_norm", [1, D])
        self.out = nc.dram_tensor("out", [T, D], F32, kind="ExternalOutput").ap()
        self.S = S = {}
        S["xres"] = ds("xres", [T, D], F32)
        S["modbc"] = ds("modbc", [128, 14 * D], F32)
        S["hT"] = ds("hT", [T // 256, 128, c.KD, 256])
        S["yT"] = ds("yT", [c.AI, T])
        S["wb_in"] = ds("wb_in", [D, max(c.APROJ, 2 * c.BI)])
        S["wb_out"] = ds("wb_out", [D // 512, 128, max(c.AI, c.BI) // 128, 512])
        S["wb_kv"] = ds("wb_kv", [D, 2 * c.KVD + c.BH])
        S["xs"] = ds("xs", [T, 1024])
        S["Bm"] = ds("Bm", [T, 128])
        S["BT"] = ds("BT", [128, T])
        S["CT"] = ds("CT", [128, T])
        S["KT"] = ds("KT", [c.KVD, T])
        S["V"] = ds("V", [T, c.KVD])
        S["FT"] = ds("FT", [c.BH, T], F32)
        S["NF3"] = ds("NF3", [3, c.BH, T])
        S["QT"] = ds("QT", [c.BI, T])
        S["ZT"] = ds("ZT", [c.BI, T])
        self.dbg = {}
        self.consts()

    def dbg_out(self, name, src_ap, shape, dt):
        if name not in self.debug:
            return
        P = self.P
        o = self.nc.dram_tensor("dbg_" + name, list(shape), dt, kind="ExternalOutput").ap()
        P.begin_phase()
        rows, cols = shape
        t = P.sb("dbgt", [128, cols], dt)
        tk = Tok()
        for r0 in range(0, rows, 128):
            n = min(128, rows - r0)
            P.dma(t[:n, :], src_ap[r0:r0 + n, :], w=[tk], grp="dbg_l")
            P.dma(o[r0:r0 + n, :], t[:n, :], r=[tk], grp="dbg_s")
        P.end_phase()

    def consts(self):
        P = self.P
        C = self.C = {}
        tk = self.Ctok = Tok("consts")
        C["identf"] = P.sb("identf", [128, 128], F32)
        C["ident"] = P.sb("ident", [128, 128], BF16)
        C["trif"] = P.sb("trif", [128, 128], F32)
        C["Uf"] = P.sb("Uf", [128, 128], F32)
        C["onesf"] = P.sb("onesf", [128, 128], F32)
        C["ones3"] = P.sb("ones3", [3, 128], BF16)
        C["maskb"] = P.sb("maskb", [128, 128], F32)
        P.memset(C["identf"][:], 0.0, w=[tk])
        P.op("pool", lambda e: e.affine_select(out=C["identf"][:], in_=C["identf"][:], pattern=[[-1, 128]],
                                               compare_op=ALU.not_equal, fill=1.0, base=0, channel_multiplier=1),
             r=[tk], w=[tk])
        P.copy(C["ident"][:], C["identf"][:], r=[tk], w=[tk])
        P.memset(C["onesf"][:], 1.0, w=[tk])
        P.memset(C["ones3"][:], 1.0, w=[tk])
        P.op("pool", lambda e: e.affine_select(out=C["trif"][:], in_=C["onesf"][:], pattern=[[1, 128]],
                                               compare_op=ALU.is_ge, fill=0.0, base=0, channel_multiplier=-1),
             r=[tk], w=[tk])
        P.op("pool", lambda e: e.affine_select(out=C["Uf"][:], in_=C["onesf"][:], pattern=[[-1, 128]],
                                               compare_op=ALU.is_gt, fill=0.0, base=0, channel_multiplier=1),
             r=[tk], w=[tk])
        P.memset(C["maskb"][:], 0.0, w=[tk])
        P.op("pool", lambda e: e.affine_select(out=C["maskb"][:], in_=C["maskb"][:], pattern=[[-1, 128]],
                                               compare_op=ALU.is_ge, fill=-1e30, base=0, channel_multiplier=1),
             r=[tk], w=[tk])
        C["sel64"] = P.sb("sel64", [128, 128], F32)
        C["onesb"] = P.sb("onesb", [128, 128], BF16)
        C["maskT"] = P.sb("maskT", [128, 128], F32)
        P.memset(C["sel64"][:], 0.0, w=[tk])
        P.op("pool", lambda e: e.affine_select(out=C["sel64"][:], in_=C["sel64"][:], pattern=[[0, 128]],
                                               compare_op=ALU.not_equal, fill=1.0, base=-64, channel_multiplier=1),
             r=[tk], w=[tk])
        P.memset(C["onesb"][:], 1.0, w=[tk])
        P.memset(C["maskT"][:], 0.0, w=[tk])
        P.op("pool", lambda e: e.affine_select(out=C["maskT"][:], in_=C["maskT"][:], pattern=[[1, 128]],
                                               compare_op=ALU.is_ge, fill=-1e30, base=0, channel_multiplier=-1),
             r=[tk], w=[tk])
        P.sync_all()

    def cast_w_blk(self, src, dst, rows, cols):
        P = self.P
        P.begin_phase()
        CW = 2048
        nb = 3
        st = [P.sb(f"cst{i}", [128, CW], F32) for i in range(nb)]
        ob = [P.sb(f"cob{i}", [128, CW], BF16) for i in range(nb)]
        ts_ = [Tok() for _ in range(nb)]
        to_ = [Tok() for _ in range(nb)]
        engs = ["dve", "pool", "act"]
        i = 0
        for k in range(rows // 128):
            for cc in range(0, cols, CW):
                w = min(CW, cols - cc)
                b = i % nb
                P.dma(st[b][:, :w], src[k * 128:(k + 1) * 128, cc:cc + w], w=[ts_[b]], grp=f"cl{b}",
                      eng=("sp" if i % 2 == 0 else "act"))
                P.copy(ob[b][:, :w], st[b][:, :w], r=[ts_[b]], w=[to_[b]], eng=engs[i % 3])
                for j in range(w // 512):
                    P.dma(dst[(cc + j * 512) // 512, :, k, :], ob[b][:, j * 512:(j + 1) * 512], r=[to_[b]],
                          grp=f"cs{b}", eng="sp")
                i += 1
        P.end_phase()

    def cast_w(self, src, dst, rows, cols, c0=0, d0=0):
        P = self.P
        P.begin_phase()
        CW = 2048
        nb = 3
        st = [P.sb(f"cst{i}", [128, CW], F32) for i in range(nb)]
        ob = [P.sb(f"cob{i}", [128, CW], BF16) for i in range(nb)]
        ts_ = [Tok() for _ in range(nb)]
        to_ = [Tok() for _ in range(nb)]
        engs = ["dve", "pool", "act"]
        i = 0
        for r0 in range(0, rows, 128):
            for cc in range(0, cols, CW):
                w = min(CW, cols - cc)
                b = i % nb
                P.dma(st[b][:, :w], src[r0:r0 + 128, c0 + cc:c0 + cc + w], w=[ts_[b]], grp=f"cl{b}",
                      eng=("sp" if i % 2 == 0 else "act"))
                P.copy(ob[b][:, :w], st[b][:, :w], r=[ts_[b]], w=[to_[b]], eng=engs[i % 3])
                P.dma(dst[r0:r0 + 128, d0 + cc:d0 + cc + w], ob[b][:, :w], r=[to_[b]], grp=f"cs{b}", eng="sp")
                i += 1
        P.end_phase()

    def mods(self):
        P, c, I, S = self.P, self.cfg, self.I, self.S
        D, KD = c.D, c.KD
        P.begin_phase()
        cT = P.sb("cT", [128, KD], F32)
        cbc = P.sb("cbc", [128, KD, 128], F32)
        tc_ = Tok()
        P.dma(cT[:], I["cT"][:, :], w=[tc_], grp="m_c")
        P.copy(cbc[:], cT[:].unsqueeze(2).broadcast_to([128, KD, 128]), r=[tc_], w=[tc_])
        nb = 2
        wt = [P.sb(f"mw{i}", [128, KD, 512], F32) for i in range(nb)]
        bt = [P.sb(f"mb{i}", [128, 512], F32) for i in range(nb)]
        ot = [P.sb(f"mo{i}", [128, 512], F32) for i in range(nb)]
        pst = [P.ps(f"mps{i}", [128, 512], F32) for i in range(nb)]
        tw = [Tok() for _ in range(nb)]
        tb = [Tok() for _ in range(nb)]
        to = [Tok() for _ in range(nb)]
        tp = [PTok() for _ in range(nb)]
        jobs = []
        for i in range(4):
            for n0 in range(0, 3 * D, 512):
                jobs.append((I[f"ada_w{i}"], I["ada_b"][i:i + 1, :], n0, i * 3 * D + n0))
        for n0 in range(0, 2 * D, 512):
            jobs.append((I["kv_ada_w"], I["kv_ada_b"][0:1, :], n0, 12 * D + n0))
        for j, (W, bvec, n0, off) in enumerate(jobs):
            b = j % nb
            P.dma(wt[b][:], W[:, n0:n0 + 512].rearrange("(k p) n -> p k n", p=128), w=[tw[b]], grp=f"m_w{b}",
                  eng=("sp" if j % 2 == 0 else "act"))
            P.dma(bt[b][:], bvec[:, n0:n0 + 512].partition_broadcast(128), w=[tb[b]], grp=f"m_b{b}", eng="pool")
            for k in range(KD):
                P.mm(pst[b][:], cbc[:, k, :], wt[b][:, k, :], start=(k == 0), stop=(k == KD - 1),
                     r=[tc_, tw[b]], w=[tp[b]])
            P.tt(ot[b][:], pst[b][:], bt[b][:], ALU.add, r=[tp[b], tb[b]], w=[to[b]])
            P.dma(S["modbc"][:, off:off + 512], ot[b][:], r=[to[b]], grp=f"m_o{b}", eng="pool")
        P.end_phase()

    def norm(self, wvec, mod_off, final=False):
        P, c, S, C = self.P, self.cfg, self.S, self.C
        D, T, KD = c.D, c.T, c.KD
        P.begin_phase()
        sbc = P.sb("n_s", [128, D], F32)
        tsb = Tok()
        P.dma(sbc[:], wvec.partition_broadcast(128), w=[tsb], grp="n_w")
        if not final:
            shbc = P.sb("n_sh", [128, D], F32)
            tmpm = P.sb("n_tm", [128, D], F32)
            tsh = Tok()
            ttm = Tok()
            P.dma(shbc[:], S["modbc"][:, mod_off:mod_off + D], w=[tsh], grp="n_sh")
            P.dma(tmpm[:], S["modbc"][:, mod_off + D:mod_off + 2 * D], w=[ttm], grp="n_tm")
            P.ts(tmpm[:], tmpm[:], 1.0, None, ALU.add, r=[ttm], w=[ttm])
            P.tt(sbc[:], sbc[:], tmpm[:], ALU.mult, r=[tsb, ttm], w=[tsb])
        nb = 2
        xt = [P.sb(f"n_x{i}", [128, D], F32) for i in range(nb)]
        tx = [Tok() for _ in range(nb)]
        junk = P.sb("n_junk", [128, D], BF16)
        tj = Tok()
        st = [P.sb(f"n_st{i}", [128, 4], F32) for i in range(nb)]
        tst = [Tok() for _ in range(nb)]
        yt = [P.sb(f"n_y{i}", [128, D], F32) for i in range(nb)]
        ty = [Tok() for _ in range(nb)]
        if not final:
            hb = [P.sb(f"n_h{i}", [128, D], BF16) for i in range(nb)]
            th = [Tok() for _ in range(nb)]
            hT = [P.sb(f"n_hT{i}", [128, KD, 256], BF16) for i in range(nb)]
            thT = [Tok() for _ in range(nb)]
            NPB = (KD + 7) // 8
            pt = [P.ps(f"n_pt{i}", [128, 8, 128], BF16) for i in range(min(4, max(2, NPB)))]
            tpt = [PTok() for _ in pt]
        pi = 0
        for it in range(T // 128):
            b = it % nb
            P.dma(xt[b][:], S["xres"][it * 128:(it + 1) * 128, :], w=[tx[b]], grp=f"n_x{b}",
                  eng=("sp" if it % 2 == 0 else "act"))
            P.act(junk[:], xt[b][:], AF.Square, r=[tx[b]], w=[tj, tst[b]], accum_out=st[b][:, 0:1])
            P.ts(st[b][:, 1:2], st[b][:, 0:1], 1.0 / D, EPS, ALU.mult, ALU.add, r=[tst[b]], w=[tst[b]])
            P.act(st[b][:, 2:3], st[b][:, 1:2], AF.Sqrt, r=[tst[b]], w=[tst[b]])
            P.op("dve", lambda e, o=st[b][:, 3:4], i_=st[b][:, 2:3]: e.reciprocal(o, i_), r=[tst[b]], w=[tst[b]])
            P.stt(yt[b][:], xt[b][:], st[b][:, 3:4], sbc[:], ALU.mult, ALU.mult, r=[tx[b], tst[b], tsb], w=[ty[b]])
            if final:
                P.dma(self.out[it * 128:(it + 1) * 128, :], yt[b][:], r=[ty[b]], grp=f"n_o{b}", eng="pool")
                continue
            P.tt(hb[b][:], yt[b][:], shbc[:], ALU.add, r=[ty[b], tsh], w=[th[b]], eng="pool")
            sup = it // 2
            hb_ = sup % nb
            sub = it % 2
            for k0 in range(0, KD, 8):
                kn = min(8, KD - k0)
                pb = pi % len(pt)
                pi += 1
                for k in range(kn):
                    P.tr(pt[pb][:, k, :], hb[b][:, (k0 + k) * 128:(k0 + k + 1) * 128], C["ident"][:],
                         r=[th[b], self.Ctok], w=[tpt[pb]])
                P.copy(hT[hb_][:, k0:k0 + kn, sub * 128:(sub + 1) * 128], pt[pb][:, :kn, :], r=[tpt[pb]],
                       w=[thT[hb_]], eng=("act" if (k0 // 8) % 2 == 0 else "dve"))
            if sub == 1:
                P.dma(S["hT"][sup], hT[hb_][:], r=[thT[hb_]], grp=f"n_hs{hb_}", eng="pool")
        P.end_phase()

    def proj_fm(self, wsb, tw, ncols, epi, pre=None):
        P, c, S = self.P, self.cfg, self.S
        KD, T = c.KD, c.T
        nb = 2
        hs = [P.sb(f"pf_h{i}", [128, KD, 512], BF16) for i in range(nb)]
        th = [Tok() for _ in range(nb)]
        pst = [P.ps(f"pf_ps{i}", [128, 512], F32) for i in range(3)]
        tps = [PTok() for _ in range(3)]
        pi = 0
        for tt in range(T // 512):
            b = tt % nb
            for hh_ in range(2):
                P.dma(hs[b][:, :, hh_ * 256:(hh_ + 1) * 256], S["hT"][tt * 2 + hh_], w=[th[b]],
                      grp=f"pf_h{b}", eng=("sp" if tt % 2 == 0 else "act"))
            if pre is not None:
                pre(tt)
            for ct in range(ncols // 128):
                pb = pi % 3
                pi += 1
                for k in range(KD):
                    P.mm(pst[pb][:], wsb[:, k, ct * 128:(ct + 1) * 128], hs[b][:, k, :], start=(k == 0),
                         stop=(k == KD - 1), r=[tw, th[b]], w=[tps[pb]])
                epi(pst[pb], tps[pb], ct, tt)

    def proj_tm(self, wsb, tw, ncols, epi, pre=None):
        P, c, S = self.P, self.cfg, self.S
        KD, T = c.KD, c.T
        nb = 2
        hs = [P.sb(f"pt_h{i}", [128, KD, 512], BF16) for i in range(nb)]
        th = [Tok() for _ in range(nb)]
        pst = [P.ps(f"pt_ps{i}", [128, 512], F32) for i in range(3)]
        tps = [PTok() for _ in range(3)]
        pi = 0
        for tt in range(T // 512):
            b = tt % nb
            for hh_ in range(2):
                P.dma(hs[b][:, :, hh_ * 256:(hh_ + 1) * 256], S["hT"][tt * 2 + hh_], w=[th[b]],
                      grp=f"pt_h{b}", eng=("sp" if tt % 2 == 0 else "act"))
            for sub in range(4):
                it = tt * 4 + sub
                if pre is not None:
                    pre(it)
                for c0 in range(0, ncols, 512):
                    w = min(512, ncols - c0)
                    pb = pi % 3
                    pi += 1
                    for k in range(KD):
                        P.mm(pst[pb][:, :w], hs[b][:, k, sub * 128:(sub + 1) * 128], wsb[:, k, c0:c0 + w],
                             start=(k == 0), stop=(k == KD - 1), r=[tw, th[b]], w=[tps[pb]])
                    epi(pst[pb], tps[pb], it, c0, w)

    def mamba_xbc(self, li, g):
        P, c, I, S, C = self.P, self.cfg, self.I, self.S, self.C
        KD, T = c.KD, c.T
        P.begin_phase()
        wsb = P.sb("mx_w", [128, KD, 1280], BF16)
        tw = Tok()
        Wb = S["wb_in"]
        xo = c.AI + g * 1024
        bo = c.AI + c.AI + g * 128
        co = c.AI + c.AI + c.GN + g * 128
        Wv = lambda o, n: Wb[:, o:o + n].rearrange("(k p) n -> p k n", p=128)
        P.dma(wsb[:, :, 0:1024], Wv(xo, 1024), w=[tw], grp="mx_w")
        P.dma(wsb[:, :, 1024:1152], Wv(bo, 128), w=[tw], grp="mx_w")
        P.dma(wsb[:, :, 1152:1280], Wv(co, 128), w=[tw], grp="mx_w")
        cw = P.sb("mx_cw", [128, 10, 4], F32)
        cb = P.sb("mx_cb", [128, 10], F32)
        tcw = Tok()
        cvT = I["a_convT"]
        cvb = I["a_conv_b"]
        for (o, ct0, n) in ((xo - c.AI, 0, 8), (bo - c.AI, 8, 1), (co - c.AI, 9, 1)):
            P.dma(cw[:, ct0:ct0 + n, :], cvT[li, o:o + n * 128, :].rearrange("(t p) k -> p t k", p=128), w=[tcw],
                  grp="mx_cw")
            for t_ in range(n):
                P.dma(cb[:, ct0 + t_:ct0 + t_ + 1],
                      cvb[li:li + 1, o + t_ * 128:o + (t_ + 1) * 128].rearrange("o p -> p o"), w=[tcw], grp="mx_cw")
        halo = P.sb("mx_halo", [128, 10, 3], F32)
        thalo = [Tok() for _ in range(10)]
        P.memset(halo[:], 0.0, w=thalo)
        nb = 2
        uext = [P.sb(f"mx_u{i}", [128, 515], F32) for i in range(nb)]
        tu = [Tok() for _ in range(nb)]
        acc = [P.sb(f"mx_a{i}", [128, 512], F32) for i in range(nb)]
        ta = [Tok() for _ in range(nb)]
        xc = [P.sb(f"mx_xc{i}", [128, 512], BF16) for i in range(nb)]
        txc = [Tok() for _ in range(nb)]
        ptr = [P.ps(f"mx_pt{i}", [128, 8, 128], BF16) for i in range(2)]
        tptr = [PTok() for _ in range(2)]
        xtm = [P.sb(f"mx_xtm{i}", [128, 4, 1024], BF16) for i in range(nb)]
        txtm = [Tok() for _ in range(nb)]
        btm = [P.sb(f"mx_btm{i}", [128, 4, 128], BF16) for i in range(nb)]
        tbtm = [Tok() for _ in range(nb)]
        cnt = [0]

        def epi(ps, tps, ct, tt):
            i = cnt[0]
            cnt[0] += 1
            b = i % nb
            tb_ = tt % nb
            P.copy(uext[b][:, 3:515], ps[:], r=[tps], w=[tu[b]], eng="act")
            P.copy(uext[b][:, 0:3], halo[:, ct, :], r=[thalo[ct]], w=[tu[b]], eng="pool")
            P.act(acc[b][:], ps[:], AF.Identity, r=[tps, tcw], w=[ta[b]], scale=cw[:, ct, 3:4], bias=cb[:, ct:ct + 1])
            for k in (2, 1, 0):
                P.stt(acc[b][:], uext[b][:, k:k + 512], cw[:, ct, k:k + 1], acc[b][:], ALU.mult, ALU.add,
                      r=[tu[b], tcw, ta[b]], w=[ta[b]])
            P.copy(halo[:, ct, :], uext[b][:, 512:515], r=[tu[b]], w=[thalo[ct]], eng="pool")
            P.act(xc[b][:], acc[b][:], AF.Silu, r=[ta[b]], w=[txc[b]])
            if ct <= 8:
                pb = i % 2
                for j in range(4):
                    P.tr(ptr[pb][:, j, :], xc[b][:, j * 128:(j + 1) * 128], C["ident"][:], r=[txc[b], self.Ctok],
                         w=[tptr[pb]])
                if ct < 8:
                    P.copy(xtm[tb_][:, :, ct * 128:(ct + 1) * 128], ptr[pb][:, 0:4, :], r=[tptr[pb]], w=[txtm[tb_]],
                           eng=("dve" if ct % 2 == 0 else "act"))
                    if ct == 7:
                        P.dma(S["xs"][tt * 512:(tt + 1) * 512, :].rearrange("(j p) c -> p j c", p=128), xtm[tb_][:],
                              r=[txtm[tb_]], grp=f"mx_xs{tb_}", eng="pool")
                else:
                    P.copy(btm[tb_][:], ptr[pb][:, 0:4, :], r=[tptr[pb]], w=[tbtm[tb_]], eng="dve")
                    P.dma(S["Bm"][tt * 512:(tt + 1) * 512, :].rearrange("(j p) c -> p j c", p=128), btm[tb_][:],
                          r=[tbtm[tb_]], grp=f"mx_bm{tb_}", eng="pool")
            if ct == 8:
                P.dma(S["BT"][:, tt * 512:(tt + 1) * 512], xc[b][:], r=[txc[b]], grp=f"mx_bt{b}", eng="pool")
            if ct == 9:
                P.dma(S["CT"][:, tt * 512:(tt + 1) * 512], xc[b][:], r=[txc[b]], grp=f"mx_ct{b}", eng="pool")

        self.proj_fm(wsb, tw, 1280, epi)
        P.end_phase()

    def mamba_scan(self, li, g):
        P, c, I, S, C = self.P, self.cfg, self.I, self.S, self.C
        KD, T = c.KD, c.T
        P.begin_phase()
        Wb = S["wb_in"]
        wz = P.sb("ms_w", [128, KD, 1040], BF16)
        tw = Tok()
        Wv = lambda o, n: Wb[:, o:o + n].rearrange("(k p) n -> p k n", p=128)
        P.dma(wz[:, :, 0:1024], Wv(g * 1024, 1024), w=[tw], grp="ms_w")
        P.dma(wz[:, :, 1024:1040], Wv(c.AI + c.CONV + g * 16, 16), w=[tw], grp="ms_w")
        pc = P.sb("ms_pc", [128, 64], F32)
        tpc = Tok()
        hs = slice(g * 16, (g + 1) * 16)
        P.dma(pc[:, 0:16], I["a_A_log"][li:li + 1, hs].partition_broadcast(128), w=[tpc], grp="ms_pc")
        P.dma(pc[:, 16:32], I["a_dt_bias"][li:li + 1, hs].partition_broadcast(128), w=[tpc], grp="ms_pc")
        P.dma(pc[:, 32:48], I["a_D"][li:li + 1, hs].partition_broadcast(128), w=[tpc], grp="ms_pc")
        P.act(pc[:, 0:16], pc[:, 0:16], AF.Exp, r=[tpc], w=[tpc])
        P.ts(pc[:, 0:16], pc[:, 0:16], -1.0, None, ALU.mult, r=[tpc], w=[tpc])
        Dd = P.sb("ms_Dd", [128, 16, 128], BF16)
        tDd = Tok()
        for j in range(16):
            P.ts(Dd[:, j, :], C["identf"][:], pc[:, 32 + j:33 + j], None, ALU.mult, r=[tpc, self.Ctok], w=[tDd])
        gw = P.sb("ms_gw", [128, 1024], F32)
        tgw = Tok()
        P.dma(gw[:], I["a_gnorm"][li:li + 1, g * 1024:(g + 1) * 1024].partition_broadcast(128), w=[tgw], grp="ms_gw")
        stf = P.sb("ms_stf", [128, 1024], F32)
        stb = P.sb("ms_stb", [128, 1024], BF16)
        tstf = Tok()
        tstb = Tok()
        P.memset(stf[:], 0.0, w=[tstf])
        P.memset(stb[:], 0.0, w=[tstb], eng="pool")
        pA = P.ps("ms_pA", [128, 1024], F32)
        pB = P.ps("ms_pB", [128, 512], F32)
        pC = P.ps("ms_pC", [128, 2048], F32)
        pD = P.ps("ms_pD", [128, 8, 128], BF16)
        tA = [PTok(), PTok()]
        tB = PTok()
        tCk = [PTok() for _ in range(4)]
        tD = PTok()
        nb = 2
        hsb = [P.sb(f"ms_h{i}", [128, KD, 256], BF16) for i in range(nb)]
        th = [Tok() for _ in range(nb)]
        bts = [P.sb(f"ms_bt{i}", [128, 512], BF16) for i in range(nb)]
        cts = [P.sb(f"ms_ct{i}", [128, 512], BF16) for i in range(nb)]
        tbc = [Tok() for _ in range(nb)]
        xcs = [P.sb(f"ms_x{i}", [128, 1024], BF16) for i in range(nb)]
        bcs = [P.sb(f"ms_b{i}", [128, 128], BF16) for i in range(nb)]
        txb = [Tok() for _ in range(nb)]
        sm = P.sb("ms_sm", [128, 256], F32)
        tsm = Tok()
        zs = P.sb("ms_zs", [128, 1024], F32)
        tzs = Tok()
        rseg = P.sb("ms_rseg", [128, 16, 128], F32)
        trs = Tok()
        LT = P.sb("ms_LT", [128, 16, 128], BF16)
        tLT = Tok()
        CBm = P.sb("ms_CBm", [128, 128], BF16)
        tCB = Tok()
        Wm = P.sb("ms_Wm", [128, 16, 128], BF16)
        tWm = Tok()
        xdt = P.sb("ms_xdt", [128, 1024], BF16)
        txdt = Tok()
        xdd = P.sb("ms_xdd", [128, 1024], BF16)
        txdd = Tok()
        t1 = P.sb("ms_t1", [128, 1024], F32)
        tt1 = Tok()
        yg = P.sb("ms_yg", [128, 1024], F32)
        tyg = Tok()
        junk = P.sb("ms_junk", [128, 1024], BF16)
        tjk = Tok()
        yn = P.sb("ms_yn", [128, 1024], BF16)
        tyn = Tok()
        yTs = [P.sb(f"ms_yT{i}", [128, 8, 512], BF16) for i in range(nb)]
        tyT = [Tok() for _ in range(nb)]
        for ci in range(T // 128):
            sup, sub = ci // 4, ci % 4
            b = sup % nb
            xb_ = ci % nb
            t0 = ci * 128
            hb2 = (ci // 2) % nb
            hsub = ci % 2
            if hsub == 0:
                P.dma(hsb[hb2][:], S["hT"][ci // 2], w=[th[hb2]], grp=f"ms_h{hb2}", eng="sp")
            if sub == 0:
                P.dma(bts[b][:], S["BT"][:, sup * 512:(sup + 1) * 512], w=[tbc[b]], grp=f"ms_bc{b}", eng="act")
                P.dma(cts[b][:], S["CT"][:, sup * 512:(sup + 1) * 512], w=[tbc[b]], grp=f"ms_bc{b}", eng="act")
            P.dma(xcs[xb_][:], S["xs"][t0:t0 + 128, :], w=[txb[xb_]], grp=f"ms_x{xb_}", eng="sp")
            P.dma(bcs[xb_][:], S["Bm"][t0:t0 + 128, :], w=[txb[xb_]], grp=f"ms_x{xb_}", eng="sp")
            BTc = bts[b][:, sub * 128:(sub + 1) * 128]
            CTc = cts[b][:, sub * 128:(sub + 1) * 128]
            xc_ = xcs[xb_]
            for cbk in range(2):
                for k in range(KD):
                    P.mm(pA[:, cbk * 512:(cbk + 1) * 512], hsb[hb2][:, k, hsub * 128:(hsub + 1) * 128],
                         wz[:, k, cbk * 512:(cbk + 1) * 512], start=(k == 0), stop=(k == KD - 1), r=[tw, th[hb2]],
                         w=[tA[cbk]])
            for k in range(KD):
                P.mm(pB[:, 0:16], hsb[hb2][:, k, hsub * 128:(hsub + 1) * 128], wz[:, k, 1024:1040], start=(k == 0),
                     stop=(k == KD - 1), r=[tw, th[hb2]], w=[tB])
            P.tt(sm[:, 0:16], pB[:, 0:16], pc[:, 16:32], ALU.add, r=[tB, tpc], w=[tsm])
            P.ts(sm[:, 16:32], sm[:, 0:16], -1.0, None, ALU.mult, r=[tsm], w=[tsm])
            P.tt(sm[:, 16:32], sm[:, 16:32], sm[:, 0:16], ALU.min, r=[tsm], w=[tsm])
            P.act(sm[:, 32:48], sm[:, 16:32], AF.Exp, r=[tsm], w=[tsm])
            P.act(sm[:, 32:48], sm[:, 32:48], AF.Ln, r=[tsm], w=[tsm], bias=1.0)
            P.ts(sm[:, 16:32], sm[:, 0:16], 0.0, None, ALU.max, r=[tsm], w=[tsm])
            P.tt(sm[:, 48:64], sm[:, 16:32], sm[:, 32:48], ALU.add, r=[tsm], w=[tsm])
            P.tt(sm[:, 64:80], sm[:, 48:64], pc[:, 0:16], ALU.mult, r=[tsm, tpc], w=[tsm])
            for cbk in range(2):
                P.act(zs[:, cbk * 512:(cbk + 1) * 512], pA[:, cbk * 512:(cbk + 1) * 512], AF.Silu, r=[tA[cbk]],
                      w=[tzs])
            a_ = sm[:, 64:80]
            P.mm(pB[:, 16:32], C["trif"][:], a_, r=[tsm, self.Ctok], w=[tB])
            P.mm(pB[:, 32:48], C["Uf"][:], a_, r=[tsm, self.Ctok], w=[tB])
            P.mm(pB[:, 48:64], C["onesf"][:], a_, r=[tsm, self.Ctok], w=[tB])
            P.act(sm[:, 80:128], pB[:, 16:64], AF.Exp, r=[tB], w=[tsm])
            eacs = sm[:, 80:96]
            dec = sm[:, 96:112]
            dcl = sm[:, 112:128]
            P.tt(sm[:, 128:144], sm[:, 48:64], dec, ALU.mult, r=[tsm], w=[tsm])
            P.tt(rseg[:], a_.unsqueeze(2).broadcast_to([128, 16, 128]),
                 C["trif"][:].unsqueeze(1).broadcast_to([128, 16, 128]), ALU.mult, r=[tsm, self.Ctok], w=[trs],
                 eng="pool")
            rseg2 = rseg[:].rearrange("p j t -> p (j t)")
            LT2 = LT[:].rearrange("p j t -> p (j t)")
            for q in range(4):
                P.mm(pC[:, q * 512:(q + 1) * 512], C["Uf"][:], rseg2[:, q * 512:(q + 1) * 512], r=[trs, self.Ctok],
                     w=[tCk[q]])
            for q in range(4):
                P.act(LT2[:, q * 512:(q + 1) * 512], pC[:, q * 512:(q + 1) * 512], AF.Exp, r=[tCk[q]], w=[tLT])
            P.mm(pB[:, 128:256], BTc, CTc, r=[tbc[b]], w=[tB])
            P.tt(CBm[:], pB[:, 128:256], C["trif"][:], ALU.mult, r=[tB, self.Ctok], w=[tCB])
            P.tt(Wm[:], LT[:], CBm[:].unsqueeze(1).broadcast_to([128, 16, 128]), ALU.mult, r=[tLT, tCB], w=[tWm])
            x3 = xc_[:].rearrange("p (j d) -> p j d", j=16)
            P.tt(xdt[:].rearrange("p (j d) -> p j d", j=16), x3,
                 sm[:, 48:64].unsqueeze(2).broadcast_to([128, 16, 64]), ALU.mult, r=[txb[xb_], tsm], w=[txdt],
                 eng="pool")
            P.tt(xdd[:].rearrange("p (j d) -> p j d", j=16), x3,
                 sm[:, 128:144].unsqueeze(2).broadcast_to([128, 16, 64]), ALU.mult, r=[txb[xb_], tsm], w=[txdd],
                 eng="pool")
            for j in range(16):
                q = (j * 64) // 512
                P.mm(pC[:, j * 64:(j + 1) * 64], Wm[:, j, :], xdt[:, j * 64:(j + 1) * 64], start=True, stop=False,
                     r=[tWm, txdt], w=[tCk[q]])
                P.mm(pC[:, j * 64:(j + 1) * 64], Dd[:, j, :], xc_[:, j * 64:(j + 1) * 64], start=False, stop=True,
                     r=[tDd, txb[xb_]], w=[tCk[q]])
            for hh in range(2):
                P.mm(pA[:, hh * 512:(hh + 1) * 512], CTc, stb[:, hh * 512:(hh + 1) * 512], r=[tbc[b], tstb],
                     w=[tA[hh]])
            for hh in range(2):
                sl = slice(hh * 512, (hh + 1) * 512)
                P.tt(t1[:, sl].rearrange("p (j d) -> p j d", j=8), pA[:, sl].rearrange("p (j d) -> p j d", j=8),
                     eacs[:, hh * 8:(hh + 1) * 8].unsqueeze(2).broadcast_to([128, 8, 64]), ALU.mult,
                     r=[tA[hh], tsm], w=[tt1])
                P.tt(t1[:, sl], t1[:, sl], pC[:, sl], ALU.add, r=[tt1, tCk[hh]], w=[tt1])
            P.tt(yg[:], t1[:], zs[:], ALU.mult, r=[tt1, tzs], w=[tyg])
            P.act(junk[:], yg[:], AF.Square, r=[tyg], w=[tjk, tsm], accum_out=sm[:, 144:145])
            P.ts(sm[:, 145:146], sm[:, 144:145], 1.0 / 1024, EPS, ALU.mult, ALU.add, r=[tsm], w=[tsm])
            P.act(sm[:, 146:147], sm[:, 145:146], AF.Sqrt, r=[tsm], w=[tsm])
            P.op("dve", lambda e: e.reciprocal(sm[:, 147:148], sm[:, 146:147]), r=[tsm], w=[tsm])
            P.stt(yn[:], yg[:], sm[:, 147:148], gw[:], ALU.mult, ALU.mult, r=[tyg, tsm, tgw], w=[tyn])
            for i8 in range(8):
                P.tr(pD[:, i8, :], yn[:, i8 * 128:(i8 + 1) * 128], C["ident"][:], r=[tyn, self.Ctok], w=[tD])
            P.copy(yTs[b][:, :, sub * 128:(sub + 1) * 128], pD[:], r=[tD], w=[tyT[b]], eng="act")
            if sub == 3:
                P.dma(S["yT"][g * 1024:(g + 1) * 1024, sup * 512:(sup + 1) * 512].rearrange("(i p) t -> p i t", p=128),
                      yTs[b][:], r=[tyT[b]], grp=f"ms_yT{b}", eng="pool")
            for hh in range(2):
                P.mm(pC[:, 1024 + hh * 512:1024 + (hh + 1) * 512], bcs[xb_][:], xdd[:, hh * 512:(hh + 1) * 512],
                     r=[txb[xb_], txdd], w=[tCk[2 + hh]])
            P.tt(stf[:].rearrange("p (j d) -> p j d", j=16), stf[:].rearrange("p (j d) -> p j d", j=16),
                 dcl.unsqueeze(2).broadcast_to([128, 16, 64]), ALU.mult, r=[tstf, tsm], w=[tstf])
            for hh in range(2):
                sl = slice(hh * 512, (hh + 1) * 512)
                P.tt(stf[:, sl], stf[:, sl], pC[:, 1024 + hh * 512:1024 + (hh + 1) * 512], ALU.add,
                     r=[tstf, tCk[2 + hh]], w=[tstf])
            P.copy(stb[:], stf[:], r=[tstf], w=[tstb], eng="act")
        P.end_phase()

    def out_proj(self, KC, gate_off, src):
        P, c, S = self.P, self.cfg, self.S
        D, T = c.D, c.T
        NBW = 512
        P.begin_phase()
        gbc = P.sb("op_g", [128, D], F32)
        tg = Tok()
        P.dma(gbc[:], S["modbc"][:, gate_off:gate_off + D], w=[tg], grp="op_g")
        nb = 2
        ysb = [P.sb(f"op_y{i}", [128, KC, 512], BF16) for i in range(1)]
        ty = [Tok() for _ in range(1)]
        nwb = 2 if KC <= 32 else 1
        wo = [P.sb(f"op_w{i}", [128, KC, NBW], BF16) for i in range(nwb)]
        two = [Tok() for _ in range(nwb)]
        xb = [P.sb(f"op_x{i}", [128, NBW], F32) for i in range(4)]
        txb = [Tok() for _ in range(4)]
        tm = [P.sb(f"op_t{i}", [128, NBW], F32) for i in range(4)]
        ttm = [Tok() for _ in range(4)]
        pst = [P.ps(f"op_ps{i}", [128, 512], F32) for i in range(4)]
        tps = [PTok() for _ in range(4)]
        Wb = S["wb_out"]
        wi = 0
        ei = 0
        for sup in range(T // 512):
            b = 0
            P.dma(ysb[b][:], src[0:KC * 128, sup * 512:(sup + 1) * 512].rearrange("(k p) t -> p k t", p=128), w=[ty[b]],
                  grp=f"op_y{b}", eng="act")
            for n0 in range(0, D, NBW):
                wb_ = wi % nwb
                wi += 1
                P.dma(wo[wb_][:], Wb[n0 // 512, :, 0:KC, :], w=[two[wb_]], grp=f"op_w{wb_}", eng="sp")
                for m in range(4):
                    e4 = ei % 4
                    ei += 1
                    r0 = sup * 512 + m * 128
                    P.dma(xb[e4][:], S["xres"][r0:r0 + 128, n0:n0 + NBW], w=[txb[e4]], grp=f"op_x{e4}", eng="pool")
                    for k in range(KC):
                        P.mm(pst[e4][:, :NBW], ysb[b][:, k, m * 128:(m + 1) * 128], wo[wb_][:, k, :], start=(k == 0),
                             stop=(k == KC - 1), r=[ty[b], two[wb_]], w=[tps[e4]])
                    P.tt(tm[e4][:], pst[e4][:, :NBW], gbc[:, n0:n0 + NBW], ALU.mult, r=[tps[e4], tg], w=[ttm[e4]])
                    P.tt(tm[e4][:], tm[e4][:], xb[e4][:], ALU.add, r=[ttm[e4], txb[e4]], w=[ttm[e4]], eng="pool")
                    P.dma(S["xres"][r0:r0 + 128, n0:n0 + NBW], tm[e4][:], r=[ttm[e4]], grp=f"op_s{e4}", eng="pool")
        P.end_phase()

    def kv_stream(self):
        P, c, I, S, C = self.P, self.cfg, self.I, self.S, self.C
        KD, T = c.KD, c.T
        Wb = S["wb_kv"]
        Wv = lambda o, n: Wb[:, o:o + n].rearrange("(k p) n -> p k n", p=128)
        P.begin_phase()
        wk = P.sb("kv_wk", [128, KD, c.KVD], BF16)
        tw = Tok()
        P.dma(wk[:], Wv(0, c.KVD), w=[tw], grp="kv_w")
        ko = [P.sb(f"kv_ko{i}", [128, 512], BF16) for i in range(2)]
        tko = [Tok() for _ in range(2)]
        cnt = [0]

        def epik(ps, tps, ct, tt):
            b = cnt[0] % 2
            cnt[0] += 1
            P.copy(ko[b][:], ps[:], r=[tps], w=[tko[b]], eng=("act" if b == 0 else "dve"))
            P.dma(S["KT"][ct * 128:(ct + 1) * 128, tt * 512:(tt + 1) * 512], ko[b][:], r=[tko[b]], grp=f"kv_ko{b}",
                  eng="pool")

        self.proj_fm(wk, tw, c.KVD, epik)
        P.end_phase()
        P.begin_phase()
        NV = c.KVD
        BH = c.BH
        wv = P.sb("kv_wv", [128, KD, NV + BH], BF16)
        tw = Tok()
        P.dma(wv[:], Wv(c.KVD, NV + BH), w=[tw], grp="kv_w")
        bfb = P.sb("kv_bf", [128, BH], F32)
        tbf = Tok()
        P.dma(bfb[:], I["b_f"][0:1, :].partition_broadcast(128), w=[tbf], grp="kv_bf")
        vo = [P.sb(f"kv_vo{i}", [128, 512], BF16) for i in range(2)]
        tvo = [Tok() for _ in range(2)]
        sm = P.sb("kv_sm", [128, 4 * BH], F32)
        tsm = Tok()
        carT = P.sb("kv_carT", [BH, 4], F32)
        tcar = Tok()
        P.memset(carT[:], 0.0, w=[tcar])
        pF = P.ps("kv_pF", [128, 512], F32)
        tpF = PTok()
        ft = [P.sb(f"kv_ft{i}", [BH, 128], F32) for i in range(2)]
        tft = [Tok() for _ in range(2)]
        n3 = [P.sb(f"kv_n3{i}", [BH, 3, 128], BF16) for i in range(2)]
        tn3 = [Tok() for _ in range(2)]
        wk_ = P.sb("kv_wk2", [BH, 4, 128], F32)
        twk = Tok()
        cnt = [0]
        SQ = math.sqrt(128.0)

        def epiv(ps, tps, it, c0, w):
            if c0 < NV:
                wv_ = min(w, NV - c0)
                b = cnt[0] % 2
                cnt[0] += 1
                P.copy(vo[b][:, :wv_], ps[:, :wv_], r=[tps], w=[tvo[b]], eng=("act" if b == 0 else "dve"))
                P.dma(S["V"][it * 128:(it + 1) * 128, c0:c0 + wv_], vo[b][:, :wv_], r=[tvo[b]], grp=f"kv_vo{b}",
                      eng="pool")
            if c0 + w <= NV:
                return
            o = NV - c0
            x_ = sm[:, 0:BH]
            P.tt(x_, ps[:, o:o + BH], bfb[:], ALU.add, r=[tps, tbf], w=[tsm])
            P.ts(x_, x_, -1.0, None, ALU.mult, r=[tsm], w=[tsm])
            P.ts(sm[:, BH:2 * BH], x_, -1.0, None, ALU.mult, r=[tsm], w=[tsm])
            P.tt(sm[:, BH:2 * BH], sm[:, BH:2 * BH], x_, ALU.min, r=[tsm], w=[tsm])
            P.act(sm[:, BH:2 * BH], sm[:, BH:2 * BH], AF.Exp, r=[tsm], w=[tsm])
            P.act(sm[:, BH:2 * BH], sm[:, BH:2 * BH], AF.Ln, r=[tsm], w=[tsm], bias=1.0)
            P.ts(sm[:, 2 * BH:3 * BH], x_, 0.0, None, ALU.max, r=[tsm], w=[tsm])
            P.tt(sm[:, 3 * BH:4 * BH], sm[:, 2 * BH:3 * BH], sm[:, BH:2 * BH], ALU.add, r=[tsm], w=[tsm])
            lf = sm[:, 3 * BH:4 * BH]
            P.ts(lf, lf, -1.0, None, ALU.mult, r=[tsm], w=[tsm])
            P.mm(pF[0:BH, 0:128], lf, C["trif"][:], r=[tsm, self.Ctok], w=[tpF])
            P.mm(pF[0:BH, 128:129], lf, C["onesf"][:, 0:1], r=[tsm, self.Ctok], w=[tpF])
            b = it % 2
            P.ts(ft[b][:], pF[0:BH, 0:128], carT[:, 0:1], None, ALU.add, r=[tpF, tcar], w=[tft[b]])
            P.tt(carT[:, 0:1], carT[:, 0:1], pF[0:BH, 128:129], ALU.add, r=[tcar, tpF], w=[tcar])
            P.dma(S["FT"][:, it * 128:(it + 1) * 128], ft[b][:], r=[tft[b]], grp=f"kv_ft{b}", eng="pool")
            P.ts(wk_[:, 0, :], ft[b][:], -SQ, None, ALU.mult, r=[tft[b]], w=[twk])
            P.copy(n3[b][:, 0, :], wk_[:, 0, :], r=[twk], w=[tn3[b]])
            P.copy(wk_[:, 1, :], n3[b][:, 0, :], r=[tn3[b]], w=[twk])
            P.tt(wk_[:, 2, :], wk_[:, 0, :], wk_[:, 1, :], ALU.subtract, r=[twk], w=[twk])
            P.copy(n3[b][:, 1, :], wk_[:, 2, :], r=[twk], w=[tn3[b]])
            P.copy(wk_[:, 1, :], n3[b][:, 1, :], r=[tn3[b]], w=[twk])
            P.tt(wk_[:, 3, :], wk_[:, 2, :], wk_[:, 1, :], ALU.subtract, r=[twk], w=[twk])
            P.copy(n3[b][:, 2, :], wk_[:, 3, :], r=[twk], w=[tn3[b]])
            for part in range(3):
                P.dma(S["NF3"][part, :, it * 128:(it + 1) * 128], n3[b][:, part, :], r=[tn3[b]], grp=f"kv_n3{b}",
                      eng="pool")

        self.proj_tm(wv, tw, NV + BH, epiv)
        P.end_phase()

    def attn_proj(self):
        P, c, S = self.P, self.cfg, self.S
        KD, T = c.KD, c.T
        Wb = S["wb_in"]
        Wv = lambda o, n: Wb[:, o:o + n].rearrange("(k p) n -> p k n", p=128)
        CG = 1024 if c.BI >= 1024 else c.BI
        for c0 in range(0, c.BI, CG):
            P.begin_phase()
            wq = P.sb("ap_wq", [128, KD, CG], BF16)
            tw = Tok()
            P.dma(wq[:], Wv(c0, CG), w=[tw], grp="ap_w")
            qo = [P.sb(f"ap_qo{i}", [128, 512], BF16) for i in range(2)]
            tqo = [Tok() for _ in range(2)]
            cnt = [0]

            def epiq(ps, tps, ct, tt, c0=c0):
                b = cnt[0] % 2
                cnt[0] += 1
                P.copy(qo[b][:], ps[:], r=[tps], w=[tqo[b]], eng=("act" if b == 0 else "dve"))
                P.dma(S["QT"][c0 + ct * 128:c0 + (ct + 1) * 128, tt * 512:(tt + 1) * 512], qo[b][:], r=[tqo[b]],
                      grp=f"ap_qo{b}", eng="pool")

            self.proj_fm(wq, tw, CG, epiq)
            P.end_phase()
        for c0 in range(0, c.BI, CG):
            P.begin_phase()
            wz = P.sb("ap_wz", [128, KD, CG], BF16)
            tw = Tok()
            P.dma(wz[:], Wv(c.BI + c0, CG), w=[tw], grp="ap_w")
            zo = [P.sb(f"ap_zo{i}", [128, 512], BF16) for i in range(2)]
            tzo = [Tok() for _ in range(2)]
            cnt = [0]

            def epiz(ps, tps, ct, tt, c0=c0):
                b = cnt[0] % 2
                cnt[0] += 1
                P.act(zo[b][:], ps[:], AF.Silu, r=[tps], w=[tzo[b]])
                P.dma(S["ZT"][c0 + ct * 128:c0 + (ct + 1) * 128, tt * 512:(tt + 1) * 512], zo[b][:], r=[tzo[b]],
                      grp=f"ap_zo{b}", eng="pool")

            self.proj_fm(wz, tw, CG, epiz)
            P.end_phase()

    def attn_core(self):
        P, c, S, C = self.P, self.cfg, self.S, self.C
        T = c.T
        NT = T // 128
        scale = 128.0 ** -0.5
        SQ = math.sqrt(128.0)
        for hk in range(c.NKV):
            P.begin_phase()
            KT = P.sb("at_KT", [128, T], BF16)
            Vs = P.sb("at_V", [128, NT, 128], BF16)
            QT = P.sb("at_QT", [128, NT, 4, 128], BF16)
            Fc = P.sb("at_Fc", [128, 4, NT], F32)
            Fr = P.sb("at_Fr", [128, 4, NT], F32)
            bM = [P.sb(f"at_bM{i}", [128, 4, NT], F32) for i in range(2)]
            tbM = [Tok() for _ in range(2)]
            tl = Tok()
            P.dma(KT[:], S["KT"][hk * 128:(hk + 1) * 128, :], w=[tl], grp="at_l")
            P.dma(Vs[:], S["V"][:, hk * 128:(hk + 1) * 128].rearrange("(i p) d -> p i d", p=128), w=[tl], grp="at_l",
                  eng="act")
            for g in range(4):
                h = hk * 4 + g
                P.dma(QT[:, :, g, :], S["QT"][h * 128:(h + 1) * 128, :].rearrange("p (i q) -> p i q", q=128), w=[tl],
                      grp="at_l", eng=("sp" if g % 2 == 0 else "act"))
            ftl = P.sb("at_ftl", [NT, 4, 128], F32)
            sps = [P.ps(f"at_s{i}", [128, 512], F32) for i in range(2)]
            tsp = [PTok() for _ in range(2)]
            for g in range(4):
                h = hk * 4 + g
                P.dma(ftl[:, g, :], S["FT"][h:h + 1, :].rearrange("o (i p) -> (o i) p", p=128), w=[tl], grp="at_l",
                      eng="act")
            for g in range(4):
                P.mm(sps[0][:, g * NT:(g + 1) * NT], ftl[:, g, :], C["identf"][0:NT, 0:NT], r=[tl, self.Ctok],
                     w=[tsp[0]])
            P.copy(Fc[:].rearrange("p g i -> p (g i)"), sps[0][:, 0:4 * NT], r=[tsp[0]], w=[tl])
            P.mm(sps[1][:, 0:4 * NT], C["sel64"][:], Fc[:].rearrange("p g i -> p (g i)"), r=[tl, self.Ctok],
                 w=[tsp[1]])
            P.ts(Fr[:].rearrange("p g i -> p (g i)"), sps[1][:, 0:4 * NT], SQ, None, ALU.mult, r=[tsp[1]], w=[tl])
            P.ts(Fc[:], Fc[:], -SQ, None, ALU.mult, r=[tl], w=[tl])
            ops = [P.ps(f"at_o{i}", [128, 512], F32) for i in range(2)]
            tops = [PTok() for _ in range(2)]
            lps = [P.ps(f"at_l{i}", [128, 512], F32) for i in range(2)]
            tlps = [PTok() for _ in range(2)]
            sm = [P.sb(f"at_sm{i}", [128, 4, 128], F32) for i in range(2)]
            tsm = [Tok() for _ in range(2)]
            PT = [P.sb(f"at_PT{i}", [128, 512], BF16) for i in range(2)]
            tPT = [Tok() for _ in range(2)]
            rl = P.sb("at_rl", [128, 512], F32)
            trl = Tok()
            zt = [P.sb(f"at_z{i}", [128, 4, 512], BF16) for i in range(2)]
            tz = [Tok() for _ in range(2)]
            oT = [P.sb(f"at_oT{i}", [128, 4, 512], BF16) for i in range(2)]
            toT = [Tok() for _ in range(2)]
            bi = 0
            for qt in range(NT):
                sup, sub = qt // 4, qt % 4
                ob = sup % 2
                ab = qt % 2
                if sub == 0:
                    P.dma(zt[ob][:], S["ZT"][hk * 512:(hk + 1) * 512, sup * 512:(sup + 1) * 512]
                          .rearrange("(g p) t -> p g t", p=128), w=[tz[ob]], grp=f"at_z{ob}", eng="sp")
                nk = qt + 1
                P.tt(bM[ab][:, :, 0:nk], Fc[:, :, 0:nk], Fr[:, :, qt:qt + 1].broadcast_to([128, 4, nk]), ALU.add,
                     r=[tl], w=[tbM[ab]])
                qsl = QT[:, qt, :, :].rearrange("p g q -> p (g q)")
                for kb in range(nk):
                    b = bi % 2
                    bi += 1
                    P.mm(sps[b][:], KT[:, kb * 128:(kb + 1) * 128], qsl, r=[tl], w=[tsp[b]])
                    P.tt(sm[b][:], sps[b][:].rearrange("p (g q) -> p g q", g=4),
                         bM[ab][:, :, kb:kb + 1].broadcast_to([128, 4, 128]), ALU.add, r=[tsp[b], tbM[ab]],
                         w=[tsm[b]])
                    if kb == qt:
                        P.tt(sm[b][:], sm[b][:], C["maskT"][:].unsqueeze(1).broadcast_to([128, 4, 128]), ALU.add,
                             r=[tsm[b], self.Ctok], w=[tsm[b]], eng="pool")
                    P.act(PT[b][:], sm[b][:].rearrange("p g q -> p (g q)"), AF.Exp, r=[tsm[b]], w=[tPT[b]],
                          scale=scale)
                    P.mm(ops[ab][:], Vs[:, kb, :], PT[b][:], start=(kb == 0), stop=(kb == qt), r=[tl, tPT[b]],
                         w=[tops[ab]])
                    P.mm(lps[ab][:], C["onesb"][:], PT[b][:], start=(kb == 0), stop=(kb == qt),
                         r=[self.Ctok, tPT[b]], w=[tlps[ab]])
                P.op("dve", lambda e, o_=rl[:], i_=lps[ab][:]: e.reciprocal(o_, i_), r=[tlps[ab]], w=[trl])
                P.tt(rl[:], rl[:], ops[ab][:], ALU.mult, r=[trl, tops[ab]], w=[trl])
                P.tt(oT[ob][:, :, sub * 128:(sub + 1) * 128], rl[:].rearrange("p (g q) -> p g q", g=4),
                     zt[ob][:, :, sub * 128:(sub + 1) * 128], ALU.mult, r=[trl, tz[ob]], w=[toT[ob]], eng="pool")
                if sub == 3:
                    P.dma(S["yT"][hk * 512:(hk + 1) * 512, sup * 512:(sup + 1) * 512].rearrange("(g p) t -> p g t", p=128),
                          oT[ob][:], r=[toT[ob]], grp=f"at_oT{ob}", eng="pool")
            P.end_phase()

    def copy_x(self):
        P, c, I, S = self.P, self.cfg, self.I, self.S
        P.begin_phase()
        t = [P.sb(f"cx{i}", [128, c.D], F32) for i in range(2)]
        tk = [Tok() for _ in range(2)]
        for it in range(c.T // 128):
            b = it % 2
            P.dma(t[b][:], I["x"][it * 128:(it + 1) * 128, :], w=[tk[b]], grp=f"cx_l{b}", eng="sp")
            P.dma(S["xres"][it * 128:(it + 1) * 128, :], t[b][:], r=[tk[b]], grp=f"cx_s{b}", eng="act")
        P.end_phase()

    def build(self, stop_after=None):
        c, I, S = self.cfg, self.I, self.S
        D = c.D

        def done(tag):
            return stop_after == tag

        self.copy_x()
        self.mods()
        self.dbg_out("modbc", S["modbc"], [128, 14 * D], F32)
        if not done("mods"):
            for li in range(c.NA):
                self.cast_w(I[f"a_in{li}"], S["wb_in"], D, c.APROJ)
                self.cast_w_blk(I[f"a_out{li}"], S["wb_out"], c.AI, D)
                self.norm(I["norm_w"][li:li + 1, :], li * 3 * D)
                if li == 0:
                    pass
                if done("norm0"):
                    break
                for g in range(c.NG):
                    self.mamba_xbc(li, g)
                    if li == 0 and g == 0:
                        self.dbg_out("xs", S["xs"], [c.T, 1024], BF16)
                        self.dbg_out("BT", S["BT"], [128, c.T], BF16)
                        self.dbg_out("CT", S["CT"], [128, c.T], BF16)
                        self.dbg_out("Bm", S["Bm"], [c.T, 128], BF16)
                    if done("xbc0"):
                        break
                    self.mamba_scan(li, g)
                if done("xbc0"):
                    break
                if li == 0:
                    self.dbg_out("yT0", S["yT"], [c.AI, c.T], BF16)
                if done("scan0"):
                    break
                self.out_proj(c.AI // 128, li * 3 * D + 2 * D, S["yT"])
                if li == 0:
                    self.dbg_out("x1", S["xres"], [c.T, D], F32)
                if done("layer0"):
                    break
            else:
                self.dbg_out("xA", S["xres"], [c.T, D], F32)
                if not done("mamba"):
                    self.cast_w(I["w_kv"], S["wb_kv"], D, 2 * c.KVD)
                    self.cast_w(I["w_f"], S["wb_kv"], D, c.BH, d0=2 * c.KVD)
                    self.norm(I["kv_norm"][0:1, :], 12 * D)
                    self.kv_stream()
                    self.dbg_out("KT", S["KT"], [c.KVD, c.T], BF16)
                    self.dbg_out("V", S["V"], [c.T, c.KVD], BF16)
                    self.dbg_out("FT", S["FT"], [c.BH, c.T], F32)
                    if not done("kv"):
                        for lj in range(c.NB):
                            li = c.NA + lj
                            self.cast_w(I[f"b_in{lj}"], S["wb_in"], D, 2 * c.BI)
                            self.cast_w_blk(I[f"b_out{lj}"], S["wb_out"], c.BI, D)
                            self.norm(I["norm_w"][li:li + 1, :], li * 3 * D)
                            self.attn_proj()
                            if lj == 0:
                                self.dbg_out("QT", S["QT"], [c.BI, c.T], BF16)
                                self.dbg_out("ZT", S["ZT"], [c.BI, c.T], BF16)
                            if done("aproj"):
                                break
                            self.attn_core()
                            if lj == 0:
                                self.dbg_out("oT", S["yT"], [c.BI, c.T], BF16)
                            if done("acore"):
                                break
                            self.out_proj(c.BI // 128, li * 3 * D + 2 * D, S["yT"])
                            if lj == 0:
                                self.dbg_out("x3", S["xres"], [c.T, D], F32)
        self.norm(I["final_norm"][0:1, :], 0, final=True)
        self.P.emit()
        return self.nc


def make_inputs(cfg, b, x, c, ada_w, ada_b, norm_w, a_in_proj, a_conv_w, a_conv_b, a_dt_bias, a_A_log, a_D,
                a_gnorm, a_out_proj, kv_norm, kv_ada_w, kv_ada_b, w_kv, w_f, b_f, b_in_proj, b_out_proj, final_norm):
    f = lambda a: np.ascontiguousarray(a, dtype=np.float32)
    m = {}
    m["x"] = f(x[b])
    m["cT"] = f(np.asarray(c[b]).reshape(cfg.KD, 128).T)
    for i in range(4):
        m[f"ada_w{i}"] = f(ada_w[i])
    m["ada_b"] = f(ada_b)
    m["norm_w"] = f(norm_w)
    for i in range(cfg.NA):
        m[f"a_in{i}"] = f(a_in_proj[i])
        m[f"a_out{i}"] = f(a_out_proj[i])
    m["a_convT"] = f(np.transpose(np.asarray(a_conv_w), (0, 2, 1)))
    m["a_conv_b"] = f(a_conv_b)
    m["a_dt_bias"] = f(a_dt_bias)
    m["a_A_log"] = f(a_A_log)
    m["a_D"] = f(a_D)
    m["a_gnorm"] = f(a_gnorm)
    m["kv_norm"] = f(np.asarray(kv_norm).reshape(1, -1))
    m["kv_ada_w"] = f(kv_ada_w)
    m["kv_ada_b"] = f(np.asarray(kv_ada_b).reshape(1, -1))
    m["w_kv"] = f(w_kv)
    m["w_f"] = f(w_f)
    m["b_f"] = f(np.asarray(b_f).reshape(1, -1))
    for i in range(cfg.NB):
        m[f"b_in{i}"] = f(b_in_proj[i])
        m[f"b_out{i}"] = f(b_out_proj[i])
    m["final_norm"] = f(np.asarray(final_norm).reshape(1, -1))
    return m


def kernel(**inputs):
    x = np.asarray(inputs["x"])
    B, T, D = x.shape
    cfg = Cfg(D, T)
    kb = K(cfg)
    nc = kb.build()
    in_maps = [make_inputs(cfg, b, **inputs) for b in range(B)]
    res = run_bass_kernel_spmd(nc, in_maps, core_ids=list(range(B)))
    return np.stack([np.asarray(res.results[b]["out"]) for b in range(B)], axis=0).astype(np.float32)
```

```python
from contextlib import ExitStack
import math
import numpy as np
import concourse.bass as bass
import concourse.mybir as mybir
from concourse.bass_utils import run_bass_kernel_spmd

F32 = mybir.dt.float32
BF16 = mybir.dt.bfloat16
ALU = mybir.AluOpType
AF = mybir.ActivationFunctionType
AX = mybir.AxisListType
EPS = 1e-6


class Tok:
    __slots__ = ("name", "lw", "rd", "ex")

    def __init__(self, name="", ex=False):
        self.name = name
        self.lw = None
        self.rd = {}
        self.ex = ex


def PTok():
    return Tok("psum", True)


class Prog:
    ENGS = ("pe", "act", "dve", "pool", "sp")

    def __init__(self, nc):
        self.nc = nc
        self.streams = {k: [] for k in self.ENGS}
        self.sems = {}
        self.cnt = {}
        self.isdma = {}
        self.seen = {k: {} for k in self.ENGS}
        self.stack = ExitStack()
        self.phase_stack = None
        self.ninstr = 0
        for k in self.ENGS:
            self._sem("c_" + k, False)

    def _sem(self, key, dma):
        if key not in self.sems:
            self.sems[key] = self.stack.enter_context(self.nc.semaphore(key))
            self.cnt[key] = 0
            self.isdma[key] = dma
        return self.sems[key]

    def begin_phase(self):
        assert self.phase_stack is None
        self.phase_stack = ExitStack()

    def end_phase(self):
        self.sync_all()
        self.phase_stack.close()
        self.phase_stack = None

    def sb(self, name, shape, dt):
        st = self.phase_stack if self.phase_stack is not None else self.stack
        self.uid = getattr(self, "uid", 0) + 1
        return st.enter_context(self.nc.sbuf_tensor(f"sb{self.uid}_{name}", list(shape), dt))

    def ps(self, name, shape, dt):
        st = self.phase_stack if self.phase_stack is not None else self.stack
        self.uid = getattr(self, "uid", 0) + 1
        return st.enter_context(self.nc.psum_tensor(f"ps{self.uid}_{name}", list(shape), dt))

    def op(self, eng, fn, r=(), w=(), dma=None):
        if dma is not None:
            semkey = "d_" + dma
            self._sem(semkey, True)
            inc = 16
        else:
            semkey = "c_" + eng
            inc = 1
        need = {}
        cnt = self.cnt
        isdma = self.isdma
        if any(t.ex for t in r):
            w = list(w) + [t for t in r if t.ex and t not in w]
        for t in r:
            d = t.lw
            if d is not None:
                k, v = d
                if isdma[k]:
                    v = cnt[k]
                if need.get(k, 0) < v:
                    need[k] = v
        for t in w:
            d = t.lw
            if d is not None:
                k, v = d
                if isdma[k]:
                    v = cnt[k]
                if need.get(k, 0) < v:
                    need[k] = v
            for k, v in t.rd.items():
                if isdma[k]:
                    v = cnt[k]
                if need.get(k, 0) < v:
                    need[k] = v
        seen = self.seen[eng]
        waits = []
        for k, v in need.items():
            if k == "c_pe" and eng == "pe" and dma is None:
                continue
            if seen.get(k, 0) >= v:
                continue
            seen[k] = v
            waits.append((self.sems[k], v))
        cnt[semkey] += inc
        val = cnt[semkey]
        for t in w:
            t.lw = (semkey, val)
            t.rd = {}
        for t in r:
            if t.rd.get(semkey, 0) < val:
                t.rd[semkey] = val
        self.streams[eng].append((waits, fn, self.sems[semkey], inc))
        self.ninstr += 1

    def sync_all(self):
        for eng in self.ENGS:
            seen = self.seen[eng]
            waits = []
            for k, v in self.cnt.items():
                if v > 0 and seen.get(k, 0) < v:
                    seen[k] = v
                    waits.append((self.sems[k], v))
            if waits:
                self.streams[eng].append((waits, None, None, 0))

    def emit(self):
        self.sync_all()
        nc = self.nc
        streams = self.streams

        def run(e, lst):
            for waits, fn, semh, inc in lst:
                for s, v in waits:
                    e.wait_ge(s, v)
                if fn is not None:
                    fn(e).then_inc(semh, inc)

        with nc.Block() as block:
            @block.tensor
            def _(e):
                run(e, streams["pe"])

            @block.scalar
            def _(e):
                run(e, streams["act"])

            @block.vector
            def _(e):
                run(e, streams["dve"])

            @block.gpsimd
            def _(e):
                run(e, streams["pool"])

            @block.sync
            def _(e):
                run(e, streams["sp"])
        self.stack.close()

    def dma(self, out, in_, r=(), w=(), grp="g", eng="sp"):
        shp = tuple(out.shape)
        if len(shp) == 3 and shp[1] > 8 and tuple(in_.shape) == shp:
            for k0 in range(0, shp[1], 8):
                k1 = min(shp[1], k0 + 8)
                self._dma1(out[:, k0:k1, :], in_[:, k0:k1, :], r, w, grp, eng)
            return
        self._dma1(out, in_, r, w, grp, eng)

    def _dma1(self, out, in_, r, w, grp, eng):
        self.op(eng, lambda e: e.dma_start(out=out, in_=in_), r=r, w=w, dma=grp)

    def mm(self, out, lhsT, rhs, start=True, stop=True, r=(), w=()):
        self.op("pe", lambda e: e.matmul(out, lhsT, rhs, start=start, stop=stop), r=r, w=w)

    def tr(self, out, in_, ident, r=(), w=()):
        self.op("pe", lambda e: e.transpose(out, in_, ident), r=r, w=w)

    def act(self, out, in_, func, r=(), w=(), **kw):
        self.op("act", lambda e: e.activation(out, in_, func, **kw), r=r, w=w)

    def tt(self, out, in0, in1, op, r=(), w=(), eng="dve"):
        self.op(eng, lambda e: e.tensor_tensor(out, in0, in1, op), r=r, w=w)

    def ts(self, out, in0, s1, s2, op0, op1=None, r=(), w=(), eng="dve"):
        if op1 is None:
            self.op(eng, lambda e: e.tensor_scalar(out, in0, s1, s2, op0), r=r, w=w)
        else:
            self.op(eng, lambda e: e.tensor_scalar(out, in0, s1, s2, op0, op1), r=r, w=w)

    def stt(self, out, in0, scalar, in1, op0, op1, r=(), w=(), eng="dve"):
        self.op(eng, lambda e: e.scalar_tensor_tensor(out, in0, scalar, in1, op0, op1), r=r, w=w)

    def copy(self, out, in_, r=(), w=(), eng="dve"):
        if eng == "act":
            self.op("act", lambda e: e.copy(out, in_), r=r, w=w)
        else:
            self.op(eng, lambda e: e.tensor_copy(out, in_), r=r, w=w)

    def memset(self, ap, val, w=(), eng="dve"):
        self.op(eng, lambda e: e.memset(ap, val), w=w)


class Cfg:
    def __init__(self, D, T, depth=4):
        self.D = D
        self.T = T
        self.KD = D // 128
        self.NA = depth // 2
        self.NB = depth - self.NA
        self.AI = 2 * D
        self.NG = self.AI // 1024
        self.GN = self.NG * 128
        self.CONV = self.AI + 2 * self.GN
        self.AH = self.AI // 64
        self.APROJ = 2 * self.AI + 2 * self.GN + self.AH
        self.BH = D // 128
        self.NKV = self.BH // 4
        self.BI = self.BH * 128
        self.KVD = self.NKV * 128


class K:
    def __init__(self, cfg, debug=()):
        self.cfg = cfg
        self.debug = set(debug)
        nc = bass.Bass("TRN2", target_bir_lowering=False)
        self.nc = nc
        self.P = Prog(nc)
        c = cfg
        D, T = c.D, c.T
        di = lambda n, s, dt=F32: nc.dram_tensor(n, list(s), dt, kind="ExternalInput").ap()
        ds = lambda n, s, dt=BF16: nc.dram_tensor(n, list(s), dt).ap()
        self.I = I = {}
        I["x"] = di("x", [T, D])
        I["cT"] = di("cT", [128, c.KD])
        for i in range(4):
            I[f"ada_w{i}"] = di(f"ada_w{i}", [D, 3 * D])
        I["ada_b"] = di("ada_b", [4, 3 * D])
        I["norm_w"] = di("norm_w", [4, D])
        for i in range(c.NA):
            I[f"a_in{i}"] = di(f"a_in{i}", [D, c.APROJ])
            I[f"a_out{i}"] = di(f"a_out{i}", [c.AI, D])
        I["a_convT"] = di("a_convT", [c.NA, c.CONV, 4])
        I["a_conv_b"] = di("a_conv_b", [c.NA, c.CONV])
        I["a_dt_bias"] = di("a_dt_bias", [c.NA, c.AH])
        I["a_A_log"] = di("a_A_log", [c.NA, c.AH])
        I["a_D"] = di("a_D", [c.NA, c.AH])
        I["a_gnorm"] = di("a_gnorm", [c.NA, c.AI])
        I["kv_norm"] = di("kv_norm", [1, D])
        I["kv_ada_w"] = di("kv_ada_w", [D, 2 * D])
        I["kv_ada_b"] = di("kv_ada_b", [1, 2 * D])
        I["w_kv"] = di("w_kv", [D, 2 * c.KVD])
        I["w_f"] = di("w_f", [D, c.BH])
        I["b_f"] = di("b_f", [1, c.BH])
        for i in range(c.NB):
            I[f"b_in{i}"] = di(f"b_in{i}", [D, 2 * c.BI])
            I[f"b_out{i}"] = di(f"b_out{i}", [c.BI, D])
        I["final_norm"] = di("final_norm", [1, D])
        self.out = nc.dram_tensor("out", [T, D], F32, kind="ExternalOutput").ap()
        self.S = S = {}
        S["xres"] = ds("xres", [T, D], F32)
        S["modbc"] = ds("modbc", [128, 14 * D], F32)
        S["hT"] = ds("hT", [T // 256, 128, c.KD, 256])
        S["yT"] = ds("yT", [c.AI, T])
        S["wb_in"] = ds("wb_in", [D, max(c.APROJ, 2 * c.BI)])
        S["wb_out"] = ds("wb_out", [D // 512, 128, max(c.AI, c.BI) // 128, 512])
        S["wb_kv"] = ds("wb_kv", [D, 2 * c.KVD + c.BH])
        S["xs"] = ds("xs", [T, 1024])
        S["Bm"] = ds("Bm", [T, 128])
        S["BT"] = ds("BT", [128, T])
        S["CT"] = ds("CT", [128, T])
        S["KT"] = ds("KT", [c.KVD, T])
        S["V"] = ds("V", [T, c.KVD])
        S["FT"] = ds("FT", [c.BH, T], F32)
        S["NF3"] = ds("NF3", [3, c.BH, T])
        S["QT"] = ds("QT", [c.BI, T])
        S["ZT"] = ds("ZT", [c.BI, T])
        self.dbg = {}
        self.consts()

    def dbg_out(self, name, src_ap, shape, dt):
        if name not in self.debug:
            return
        P = self.P
        o = self.nc.dram_tensor("dbg_" + name, list(shape), dt, kind="ExternalOutput").ap()
        P.begin_phase()
        rows, cols = shape
        t = P.sb("dbgt", [128, cols], dt)
        tk = Tok()
        for r0 in range(0, rows, 128):
            n = min(128, rows - r0)
            P.dma(t[:n, :], src_ap[r0:r0 + n, :], w=[tk], grp="dbg_l")
            P.dma(o[r0:r0 + n, :], t[:n, :], r=[tk], grp="dbg_s")
        P.end_phase()

    def consts(self):
        P = self.P
        C = self.C = {}
        tk = self.Ctok = Tok("consts")
        C["identf"] = P.sb("identf", [128, 128], F32)
        C["ident"] = P.sb("ident", [128, 128], BF16)
        C["trif"] = P.sb("trif", [128, 128], F32)
        C["Uf"] = P.sb("Uf", [128, 128], F32)
        C["onesf"] = P.sb("onesf", [128, 128], F32)
        C["ones3"] = P.sb("ones3", [3, 128], BF16)
        C["maskb"] = P.sb("maskb", [128, 128], F32)
        P.memset(C["identf"][:], 0.0, w=[tk])
        P.op("pool", lambda e: e.affine_select(out=C["identf"][:], in_=C["identf"][:], pattern=[[-1, 128]],
                                               compare_op=ALU.not_equal, fill=1.0, base=0, channel_multiplier=1),
             r=[tk], w=[tk])
        P.copy(C["ident"][:], C["identf"][:], r=[tk], w=[tk])
        P.memset(C["onesf"][:], 1.0, w=[tk])
        P.memset(C["ones3"][:], 1.0, w=[tk])
        P.op("pool", lambda e: e.affine_select(out=C["trif"][:], in_=C["onesf"][:], pattern=[[1, 128]],
                                               compare_op=ALU.is_ge, fill=0.0, base=0, channel_multiplier=-1),
             r=[tk], w=[tk])
        P.op("pool", lambda e: e.affine_select(out=C["Uf"][:], in_=C["onesf"][:], pattern=[[-1, 128]],
                                               compare_op=ALU.is_gt, fill=0.0, base=0, channel_multiplier=1),
             r=[tk], w=[tk])
        P.memset(C["maskb"][:], 0.0, w=[tk])
        P.op("pool", lambda e: e.affine_select(out=C["maskb"][:], in_=C["maskb"][:], pattern=[[-1, 128]],
                                               compare_op=ALU.is_ge, fill=-1e30, base=0, channel_multiplier=1),
             r=[tk], w=[tk])
        C["sel64"] = P.sb("sel64", [128, 128], F32)
        C["onesb"] = P.sb("onesb", [128, 128], BF16)
        C["maskT"] = P.sb("maskT", [128, 128], F32)
        P.memset(C["sel64"][:], 0.0, w=[tk])
        P.op("pool", lambda e: e.affine_select(out=C["sel64"][:], in_=C["sel64"][:], pattern=[[0, 128]],
                                               compare_op=ALU.not_equal, fill=1.0, base=-64, channel_multiplier=1),
             r=[tk], w=[tk])
        P.memset(C["onesb"][:], 1.0, w=[tk])
        P.memset(C["maskT"][:], 0.0, w=[tk])
        P.op("pool", lambda e: e.affine_select(out=C["maskT"][:], in_=C["maskT"][:], pattern=[[1, 128]],
                                               compare_op=ALU.is_ge, fill=-1e30, base=0, channel_multiplier=-1),
             r=[tk], w=[tk])
        P.sync_all()

    def cast_w_blk(self, src, dst, rows, cols):
        P = self.P
        P.begin_phase()
        CW = 2048
        nb = 3
        st = [P.sb(f"cst{i}", [128, CW], F32) for i in range(nb)]
        ob = [P.sb(f"cob{i}", [128, CW], BF16) for i in range(nb)]
        ts_ = [Tok() for _ in range(nb)]
        to_ = [Tok() for _ in range(nb)]
        engs = ["dve", "pool", "act"]
        i = 0
        for k in range(rows // 128):
            for cc in range(0, cols, CW):
                w = min(CW, cols - cc)
                b = i % nb
                P.dma(st[b][:, :w], src[k * 128:(k + 1) * 128, cc:cc + w], w=[ts_[b]], grp=f"cl{b}",
                      eng=("sp" if i % 2 == 0 else "act"))
                P.copy(ob[b][:, :w], st[b][:, :w], r=[ts_[b]], w=[to_[b]], eng=engs[i % 3])
                for j in range(w // 512):
                    P.dma(dst[(cc + j * 512) // 512, :, k, :], ob[b][:, j * 512:(j + 1) * 512], r=[to_[b]],
                          grp=f"cs{b}", eng="pool")
                i += 1
        P.end_phase()

    def cast_w(self, src, dst, rows, cols, c0=0, d0=0):
        P = self.P
        P.begin_phase()
        CW = 2048
        nb = 3
        st = [P.sb(f"cst{i}", [128, CW], F32) for i in range(nb)]
        ob = [P.sb(f"cob{i}", [128, CW], BF16) for i in range(nb)]
        ts_ = [Tok() for _ in range(nb)]
        to_ = [Tok() for _ in range(nb)]
        engs = ["dve", "pool", "act"]
        i = 0
        for r0 in range(0, rows, 128):
            for cc in range(0, cols, CW):
                w = min(CW, cols - cc)
                b = i % nb
                P.dma(st[b][:, :w], src[r0:r0 + 128, c0 + cc:c0 + cc + w], w=[ts_[b]], grp=f"cl{b}",
                      eng=("sp" if i % 2 == 0 else "act"))
                P.copy(ob[b][:, :w], st[b][:, :w], r=[ts_[b]], w=[to_[b]], eng=engs[i % 3])
                P.dma(dst[r0:r0 + 128, d0 + cc:d0 + cc + w], ob[b][:, :w], r=[to_[b]], grp=f"cs{b}", eng="pool")
                i += 1
        P.end_phase()

    def mods(self):
        P, c, I, S = self.P, self.cfg, self.I, self.S
        D, KD = c.D, c.KD
        P.begin_phase()
        cT = P.sb("cT", [128, KD], F32)
        cbc = P.sb("cbc", [128, KD, 128], F32)
        tc_ = Tok()
        P.dma(cT[:], I["cT"][:, :], w=[tc_], grp="m_c")
        P.copy(cbc[:], cT[:].unsqueeze(2).broadcast_to([128, KD, 128]), r=[tc_], w=[tc_])
        nb = 2
        wt = [P.sb(f"mw{i}", [128, KD, 512], F32) for i in range(nb)]
        bt = [P.sb(f"mb{i}", [128, 512], F32) for i in range(nb)]
        ot = [P.sb(f"mo{i}", [128, 512], F32) for i in range(nb)]
        pst = [P.ps(f"mps{i}", [128, 512], F32) for i in range(nb)]
        tw = [Tok() for _ in range(nb)]
        tb = [Tok() for _ in range(nb)]
        to = [Tok() for _ in range(nb)]
        tp = [PTok() for _ in range(nb)]
        jobs = []
        for i in range(4):
            for n0 in range(0, 3 * D, 512):
                jobs.append((I[f"ada_w{i}"], I["ada_b"][i:i + 1, :], n0, i * 3 * D + n0))
        for n0 in range(0, 2 * D, 512):
            jobs.append((I["kv_ada_w"], I["kv_ada_b"][0:1, :], n0, 12 * D + n0))
        for j, (W, bvec, n0, off) in enumerate(jobs):
            b = j % nb
            P.dma(wt[b][:], W[:, n0:n0 + 512].rearrange("(k p) n -> p k n", p=128), w=[tw[b]], grp=f"m_w{b}",
                  eng=("sp" if j % 2 == 0 else "act"))
            P.dma(bt[b][:], bvec[:, n0:n0 + 512].partition_broadcast(128), w=[tb[b]], grp=f"m_b{b}", eng="pool")
            for k in range(KD):
                P.mm(pst[b][:], cbc[:, k, :], wt[b][:, k, :], start=(k == 0), stop=(k == KD - 1),
                     r=[tc_, tw[b]], w=[tp[b]])
            P.tt(ot[b][:], pst[b][:], bt[b][:], ALU.add, r=[tp[b], tb[b]], w=[to[b]])
            P.dma(S["modbc"][:, off:off + 512], ot[b][:], r=[to[b]], grp=f"m_o{b}", eng="pool")
        P.end_phase()

    def norm(self, wvec, mod_off, final=False):
        P, c, S, C = self.P, self.cfg, self.S, self.C
        D, T, KD = c.D, c.T, c.KD
        P.begin_phase()
        sbc = P.sb("n_s", [128, D], F32)
        tsb = Tok()
        P.dma(sbc[:], wvec.partition_broadcast(128), w=[tsb], grp="n_w")
        if not final:
            shbc = P.sb("n_sh", [128, D], F32)
            tmpm = P.sb("n_tm", [128, D], F32)
            tsh = Tok()
            ttm = Tok()
            P.dma(shbc[:], S["modbc"][:, mod_off:mod_off + D], w=[tsh], grp="n_sh")
            P.dma(tmpm[:], S["modbc"][:, mod_off + D:mod_off + 2 * D], w=[ttm], grp="n_tm")
            P.ts(tmpm[:], tmpm[:], 1.0, None, ALU.add, r=[ttm], w=[ttm])
            P.tt(sbc[:], sbc[:], tmpm[:], ALU.mult, r=[tsb, ttm], w=[tsb])
        nb = 2
        xt = [P.sb(f"n_x{i}", [128, D], F32) for i in range(nb)]
        tx = [Tok() for _ in range(nb)]
        junk = P.sb("n_junk", [128, D], BF16)
        tj = Tok()
        st = [P.sb(f"n_st{i}", [128, 4], F32) for i in range(nb)]
        tst = [Tok() for _ in range(nb)]
        yt = [P.sb(f"n_y{i}", [128, D], F32) for i in range(nb)]
        ty = [Tok() for _ in range(nb)]
        if not final:
            hb = [P.sb(f"n_h{i}", [128, D], BF16) for i in range(nb)]
            th = [Tok() for _ in range(nb)]
            hT = [P.sb(f"n_hT{i}", [128, KD, 256], BF16) for i in range(nb)]
            thT = [Tok() for _ in range(nb)]
            NPB = (KD + 7) // 8
            pt = [P.ps(f"n_pt{i}", [128, 8, 128], BF16) for i in range(min(4, max(2, NPB)))]
            tpt = [PTok() for _ in pt]
        pi = 0
        for it in range(T // 128):
            b = it % nb
            P.dma(xt[b][:], S["xres"][it * 128:(it + 1) * 128, :], w=[tx[b]], grp=f"n_x{b}",
                  eng=("sp" if it % 2 == 0 else "act"))
            P.act(junk[:], xt[b][:], AF.Square, r=[tx[b]], w=[tj, tst[b]], accum_out=st[b][:, 0:1])
            P.ts(st[b][:, 1:2], st[b][:, 0:1], 1.0 / D, EPS, ALU.mult, ALU.add, r=[tst[b]], w=[tst[b]])
            P.act(st[b][:, 2:3], st[b][:, 1:2], AF.Sqrt, r=[tst[b]], w=[tst[b]])
            P.op("dve", lambda e, o=st[b][:, 3:4], i_=st[b][:, 2:3]: e.reciprocal(o, i_), r=[tst[b]], w=[tst[b]])
            P.stt(yt[b][:], xt[b][:], st[b][:, 3:4], sbc[:], ALU.mult, ALU.mult, r=[tx[b], tst[b], tsb], w=[ty[b]])
            if final:
                P.dma(self.out[it * 128:(it + 1) * 128, :], yt[b][:], r=[ty[b]], grp=f"n_o{b}", eng="pool")
                continue
            P.tt(hb[b][:], yt[b][:], shbc[:], ALU.add, r=[ty[b], tsh], w=[th[b]], eng="pool")
            sup = it // 2
            hb_ = sup % nb
            sub = it % 2
            for k0 in range(0, KD, 8):
                kn = min(8, KD - k0)
                pb = pi % len(pt)
                pi += 1
                for k in range(kn):
                    P.tr(pt[pb][:, k, :], hb[b][:, (k0 + k) * 128:(k0 + k + 1) * 128], C["ident"][:],
                         r=[th[b], self.Ctok], w=[tpt[pb]])
                P.copy(hT[hb_][:, k0:k0 + kn, sub * 128:(sub + 1) * 128], pt[pb][:, :kn, :], r=[tpt[pb]],
                       w=[thT[hb_]], eng=("act" if (k0 // 8) % 2 == 0 else "dve"))
            if sub == 1:
                P.dma(S["hT"][sup], hT[hb_][:], r=[thT[hb_]], grp=f"n_hs{hb_}", eng="pool")
        P.end_phase()

    def proj_fm(self, wsb, tw, ncols, epi, pre=None):
        P, c, S = self.P, self.cfg, self.S
        KD, T = c.KD, c.T
        nb = 2
        hs = [P.sb(f"pf_h{i}", [128, KD, 512], BF16) for i in range(nb)]
        th = [Tok() for _ in range(nb)]
        pst = [P.ps(f"pf_ps{i}", [128, 512], F32) for i in range(3)]
        tps = [PTok() for _ in range(3)]
        pi = 0
        for tt in range(T // 512):
            b = tt % nb
            for hh_ in range(2):
                P.dma(hs[b][:, :, hh_ * 256:(hh_ + 1) * 256], S["hT"][tt * 2 + hh_], w=[th[b]],
                      grp=f"pf_h{b}", eng=("sp" if tt % 2 == 0 else "act"))
            if pre is not None:
                pre(tt)
            for ct in range(ncols // 128):
                pb = pi % 3
                pi += 1
                for k in range(KD):
                    P.mm(pst[pb][:], wsb[:, k, ct * 128:(ct + 1) * 128], hs[b][:, k, :], start=(k == 0),
                         stop=(k == KD - 1), r=[tw, th[b]], w=[tps[pb]])
                epi(pst[pb], tps[pb], ct, tt)

    def proj_tm(self, wsb, tw, ncols, epi, pre=None):
        P, c, S = self.P, self.cfg, self.S
        KD, T = c.KD, c.T
        nb = 2
        hs = [P.sb(f"pt_h{i}", [128, KD, 512], BF16) for i in range(nb)]
        th = [Tok() for _ in range(nb)]
        pst = [P.ps(f"pt_ps{i}", [128, 512], F32) for i in range(3)]
        tps = [PTok() for _ in range(3)]
        pi = 0
        for tt in range(T // 512):
            b = tt % nb
            for hh_ in range(2):
                P.dma(hs[b][:, :, hh_ * 256:(hh_ + 1) * 256], S["hT"][tt * 2 + hh_], w=[th[b]],
                      grp=f"pt_h{b}", eng=("sp" if tt % 2 == 0 else "act"))
            for sub in range(4):
                it = tt * 4 + sub
                if pre is not None:
                    pre(it)
                for c0 in range(0, ncols, 512):
                    w = min(512, ncols - c0)
                    pb = pi % 3
                    pi += 1
                    for k in range(KD):
                        P.mm(pst[pb][:, :w], hs[b][:, k, sub * 128:(sub + 1) * 128], wsb[:, k, c0:c0 + w],
                             start=(k == 0), stop=(k == KD - 1), r=[tw, th[b]], w=[tps[pb]])
                    epi(pst[pb], tps[pb], it, c0, w)

    def mamba_xbc(self, li, g):
        P, c, I, S, C = self.P, self.cfg, self.I, self.S, self.C
        KD, T = c.KD, c.T
        P.begin_phase()
        wsb = P.sb("mx_w", [128, KD, 1280], BF16)
        tw = Tok()
        Wb = S["wb_in"]
        xo = c.AI + g * 1024
        bo = c.AI + c.AI + g * 128
        co = c.AI + c.AI + c.GN + g * 128
        Wv = lambda o, n: Wb[:, o:o + n].rearrange("(k p) n -> p k n", p=128)
        P.dma(wsb[:, :, 0:1024], Wv(xo, 1024), w=[tw], grp="mx_w")
        P.dma(wsb[:, :, 1024:1152], Wv(bo, 128), w=[tw], grp="mx_w")
        P.dma(wsb[:, :, 1152:1280], Wv(co, 128), w=[tw], grp="mx_w")
        cw = P.sb("mx_cw", [128, 10, 4], F32)
        cb = P.sb("mx_cb", [128, 10], F32)
        tcw = Tok()
        cvT = I["a_convT"]
        cvb = I["a_conv_b"]
        for (o, ct0, n) in ((xo - c.AI, 0, 8), (bo - c.AI, 8, 1), (co - c.AI, 9, 1)):
            P.dma(cw[:, ct0:ct0 + n, :], cvT[li, o:o + n * 128, :].rearrange("(t p) k -> p t k", p=128), w=[tcw],
                  grp="mx_cw")
            for t_ in range(n):
                P.dma(cb[:, ct0 + t_:ct0 + t_ + 1],
                      cvb[li:li + 1, o + t_ * 128:o + (t_ + 1) * 128].rearrange("o p -> p o"), w=[tcw], grp="mx_cw")
        halo = P.sb("mx_halo", [128, 10, 3], F32)
        thalo = [Tok() for _ in range(10)]
        P.memset(halo[:], 0.0, w=thalo)
        nb = 2
        uext = [P.sb(f"mx_u{i}", [128, 515], F32) for i in range(nb)]
        tu = [Tok() for _ in range(nb)]
        acc = [P.sb(f"mx_a{i}", [128, 512], F32) for i in range(nb)]
        ta = [Tok() for _ in range(nb)]
        xc = [P.sb(f"mx_xc{i}", [128, 512], BF16) for i in range(nb)]
        txc = [Tok() for _ in range(nb)]
        ptr = [P.ps(f"mx_pt{i}", [128, 8, 128], BF16) for i in range(2)]
        tptr = [PTok() for _ in range(2)]
        xtm = [P.sb(f"mx_xtm{i}", [128, 4, 1024], BF16) for i in range(nb)]
        txtm = [Tok() for _ in range(nb)]
        btm = [P.sb(f"mx_btm{i}", [128, 4, 128], BF16) for i in range(nb)]
        tbtm = [Tok() for _ in range(nb)]
        cnt = [0]

        def epi(ps, tps, ct, tt):
            i = cnt[0]
            cnt[0] += 1
            b = i % nb
            tb_ = tt % nb
            P.copy(uext[b][:, 3:515], ps[:], r=[tps], w=[tu[b]], eng="act")
            P.copy(uext[b][:, 0:3], halo[:, ct, :], r=[thalo[ct]], w=[tu[b]], eng="pool")
            P.act(acc[b][:], ps[:], AF.Identity, r=[tps, tcw], w=[ta[b]], scale=cw[:, ct, 3:4], bias=cb[:, ct:ct + 1])
            for k in (2, 1, 0):
                P.stt(acc[b][:], uext[b][:, k:k + 512], cw[:, ct, k:k + 1], acc[b][:], ALU.mult, ALU.add,
                      r=[tu[b], tcw, ta[b]], w=[ta[b]])
            P.copy(halo[:, ct, :], uext[b][:, 512:515], r=[tu[b]], w=[thalo[ct]], eng="pool")
            P.act(xc[b][:], acc[b][:], AF.Silu, r=[ta[b]], w=[txc[b]])
            if ct <= 8:
                pb = i % 2
                for j in range(4):
                    P.tr(ptr[pb][:, j, :], xc[b][:, j * 128:(j + 1) * 128], C["ident"][:], r=[txc[b], self.Ctok],
                         w=[tptr[pb]])
                if ct < 8:
                    P.copy(xtm[tb_][:, :, ct * 128:(ct + 1) * 128], ptr[pb][:, 0:4, :], r=[tptr[pb]], w=[txtm[tb_]],
                           eng=("dve" if ct % 2 == 0 else "act"))
                    if ct == 7:
                        P.dma(S["xs"][tt * 512:(tt + 1) * 512, :].rearrange("(j p) c -> p j c", p=128), xtm[tb_][:],
                              r=[txtm[tb_]], grp=f"mx_xs{tb_}", eng="pool")
                else:
                    P.copy(btm[tb_][:], ptr[pb][:, 0:4, :], r=[tptr[pb]], w=[tbtm[tb_]], eng="dve")
                    P.dma(S["Bm"][tt * 512:(tt + 1) * 512, :].rearrange("(j p) c -> p j c", p=128), btm[tb_][:],
                          r=[tbtm[tb_]], grp=f"mx_bm{tb_}", eng="pool")
            if ct == 8:
                P.dma(S["BT"][:, tt * 512:(tt + 1) * 512], xc[b][:], r=[txc[b]], grp=f"mx_bt{b}", eng="pool")
            if ct == 9:
                P.dma(S["CT"][:, tt * 512:(tt + 1) * 512], xc[b][:], r=[txc[b]], grp=f"mx_ct{b}", eng="pool")

        self.proj_fm(wsb, tw, 1280, epi)
        P.end_phase()

    def mamba_scan(self, li, g):
        P, c, I, S, C = self.P, self.cfg, self.I, self.S, self.C
        KD, T = c.KD, c.T
        P.begin_phase()
        Wb = S["wb_in"]
        wz = P.sb("ms_w", [128, KD, 1040], BF16)
        tw = Tok()
        Wv = lambda o, n: Wb[:, o:o + n].rearrange("(k p) n -> p k n", p=128)
        P.dma(wz[:, :, 0:1024], Wv(g * 1024, 1024), w=[tw], grp="ms_w")
        P.dma(wz[:, :, 1024:1040], Wv(c.AI + c.CONV + g * 16, 16), w=[tw], grp="ms_w")
        pc = P.sb("ms_pc", [128, 64], F32)
        tpc = Tok()
        hs = slice(g * 16, (g + 1) * 16)
        P.dma(pc[:, 0:16], I["a_A_log"][li:li + 1, hs].partition_broadcast(128), w=[tpc], grp="ms_pc")
        P.dma(pc[:, 16:32], I["a_dt_bias"][li:li + 1, hs].partition_broadcast(128), w=[tpc], grp="ms_pc")
        P.dma(pc[:, 32:48], I["a_D"][li:li + 1, hs].partition_broadcast(128), w=[tpc], grp="ms_pc")
        P.act(pc[:, 0:16], pc[:, 0:16], AF.Exp, r=[tpc], w=[tpc])
        P.ts(pc[:, 0:16], pc[:, 0:16], -1.0, None, ALU.mult, r=[tpc], w=[tpc])
        Dd = P.sb("ms_Dd", [128, 16, 128], BF16)
        tDd = Tok()
        for j in range(16):
            P.ts(Dd[:, j, :], C["identf"][:], pc[:, 32 + j:33 + j], None, ALU.mult, r=[tpc, self.Ctok], w=[tDd])
        gw = P.sb("ms_gw", [128, 1024], F32)
        tgw = Tok()
        P.dma(gw[:], I["a_gnorm"][li:li + 1, g * 1024:(g + 1) * 1024].partition_broadcast(128), w=[tgw], grp="ms_gw")
        stf = P.sb("ms_stf", [128, 1024], F32)
        stb = P.sb("ms_stb", [128, 1024], BF16)
        tstf = Tok()
        tstb = Tok()
        P.memset(stf[:], 0.0, w=[tstf])
        P.memset(stb[:], 0.0, w=[tstb], eng="pool")
        pA = P.ps("ms_pA", [128, 1024], F32)
        pB = P.ps("ms_pB", [128, 512], F32)
        pC = P.ps("ms_pC", [128, 2048], F32)
        pD = P.ps("ms_pD", [128, 8, 128], BF16)
        tA = [PTok(), PTok()]
        tB = PTok()
        tCk = [PTok() for _ in range(4)]
        tD = PTok()
        nb = 2
        hsb = [P.sb(f"ms_h{i}", [128, KD, 256], BF16) for i in range(nb)]
        th = [Tok() for _ in range(nb)]
        bts = [P.sb(f"ms_bt{i}", [128, 512], BF16) for i in range(nb)]
        cts = [P.sb(f"ms_ct{i}", [128, 512], BF16) for i in range(nb)]
        tbc = [Tok() for _ in range(nb)]
        xcs = [P.sb(f"ms_x{i}", [128, 1024], BF16) for i in range(nb)]
        bcs = [P.sb(f"ms_b{i}", [128, 128], BF16) for i in range(nb)]
        txb = [Tok() for _ in range(nb)]
        sm = P.sb("ms_sm", [128, 256], F32)
        tsm = Tok()
        zs = P.sb("ms_zs", [128, 1024], F32)
        tzs = Tok()
        rseg = P.sb("ms_rseg", [128, 16, 128], F32)
        trs = Tok()
        LT = P.sb("ms_LT", [128, 16, 128], BF16)
        tLT = Tok()
        CBm = P.sb("ms_CBm", [128, 128], BF16)
        tCB = Tok()
        Wm = P.sb("ms_Wm", [128, 16, 128], BF16)
        tWm = Tok()
        xdt = P.sb("ms_xdt", [128, 1024], BF16)
        txdt = Tok()
        xdd = P.sb("ms_xdd", [128, 1024], BF16)
        txdd = Tok()
        t1 = P.sb("ms_t1", [128, 1024], F32)
        tt1 = Tok()
        yg = P.sb("ms_yg", [128, 1024], F32)
        tyg = Tok()
        junk = P.sb("ms_junk", [128, 1024], BF16)
        tjk = Tok()
        yn = P.sb("ms_yn", [128, 1024], BF16)
        tyn = Tok()
        yTs = [P.sb(f"ms_yT{i}", [128, 8, 512], BF16) for i in range(nb)]
        tyT = [Tok() for _ in range(nb)]
        for ci in range(T // 128):
            sup, sub = ci // 4, ci % 4
            b = sup % nb
            xb_ = ci % nb
            t0 = ci * 128
            hb2 = (ci // 2) % nb
            hsub = ci % 2
            if hsub == 0:
                P.dma(hsb[hb2][:], S["hT"][ci // 2], w=[th[hb2]], grp=f"ms_h{hb2}", eng="sp")
            if sub == 0:
                P.dma(bts[b][:], S["BT"][:, sup * 512:(sup + 1) * 512], w=[tbc[b]], grp=f"ms_bc{b}", eng="act")
                P.dma(cts[b][:], S["CT"][:, sup * 512:(sup + 1) * 512], w=[tbc[b]], grp=f"ms_bc{b}", eng="act")
            P.dma(xcs[xb_][:], S["xs"][t0:t0 + 128, :], w=[txb[xb_]], grp=f"ms_x{xb_}", eng="sp")
            P.dma(bcs[xb_][:], S["Bm"][t0:t0 + 128, :], w=[txb[xb_]], grp=f"ms_x{xb_}", eng="sp")
            BTc = bts[b][:, sub * 128:(sub + 1) * 128]
            CTc = cts[b][:, sub * 128:(sub + 1) * 128]
            xc_ = xcs[xb_]
            for cbk in range(2):
                for k in range(KD):
                    P.mm(pA[:, cbk * 512:(cbk + 1) * 512], hsb[hb2][:, k, hsub * 128:(hsub + 1) * 128],
                         wz[:, k, cbk * 512:(cbk + 1) * 512], start=(k == 0), stop=(k == KD - 1), r=[tw, th[hb2]],
                         w=[tA[cbk]])
            for k in range(KD):
                P.mm(pB[:, 0:16], hsb[hb2][:, k, hsub * 128:(hsub + 1) * 128], wz[:, k, 1024:1040], start=(k == 0),
                     stop=(k == KD - 1), r=[tw, th[hb2]], w=[tB])
            P.tt(sm[:, 0:16], pB[:, 0:16], pc[:, 16:32], ALU.add, r=[tB, tpc], w=[tsm])
            P.ts(sm[:, 16:32], sm[:, 0:16], -1.0, None, ALU.mult, r=[tsm], w=[tsm])
            P.tt(sm[:, 16:32], sm[:, 16:32], sm[:, 0:16], ALU.min, r=[tsm], w=[tsm])
            P.act(sm[:, 32:48], sm[:, 16:32], AF.Exp, r=[tsm], w=[tsm])
            P.act(sm[:, 32:48], sm[:, 32:48], AF.Ln, r=[tsm], w=[tsm], bias=1.0)
            P.ts(sm[:, 16:32], sm[:, 0:16], 0.0, None, ALU.max, r=[tsm], w=[tsm])
            P.tt(sm[:, 48:64], sm[:, 16:32], sm[:, 32:48], ALU.add, r=[tsm], w=[tsm])
            P.tt(sm[:, 64:80], sm[:, 48:64], pc[:, 0:16], ALU.mult, r=[tsm, tpc], w=[tsm])
            for cbk in range(2):
                P.act(zs[:, cbk * 512:(cbk + 1) * 512], pA[:, cbk * 512:(cbk + 1) * 512], AF.Silu, r=[tA[cbk]],
                      w=[tzs])
            a_ = sm[:, 64:80]
            P.mm(pB[:, 16:32], C["trif"][:], a_, r=[tsm, self.Ctok], w=[tB])
            P.mm(pB[:, 32:48], C["Uf"][:], a_, r=[tsm, self.Ctok], w=[tB])
            P.mm(pB[:, 48:64], C["onesf"][:], a_, r=[tsm, self.Ctok], w=[tB])
            P.act(sm[:, 80:128], pB[:, 16:64], AF.Exp, r=[tB], w=[tsm])
            eacs = sm[:, 80:96]
            dec = sm[:, 96:112]
            dcl = sm[:, 112:128]
            P.tt(sm[:, 128:144], sm[:, 48:64], dec, ALU.mult, r=[tsm], w=[tsm])
            P.tt(rseg[:], a_.unsqueeze(2).broadcast_to([128, 16, 128]),
                 C["trif"][:].unsqueeze(1).broadcast_to([128, 16, 128]), ALU.mult, r=[tsm, self.Ctok], w=[trs],
                 eng="pool")
            rseg2 = rseg[:].rearrange("p j t -> p (j t)")
            LT2 = LT[:].rearrange("p j t -> p (j t)")
            for q in range(4):
                P.mm(pC[:, q * 512:(q + 1) * 512], C["Uf"][:], rseg2[:, q * 512:(q + 1) * 512], r=[trs, self.Ctok],
                     w=[tCk[q]])
            for q in range(4):
                P.act(LT2[:, q * 512:(q + 1) * 512], pC[:, q * 512:(q + 1) * 512], AF.Exp, r=[tCk[q]], w=[tLT])
            P.mm(pB[:, 128:256], BTc, CTc, r=[tbc[b]], w=[tB])
            P.tt(CBm[:], pB[:, 128:256], C["trif"][:], ALU.mult, r=[tB, self.Ctok], w=[tCB])
            P.tt(Wm[:], LT[:], CBm[:].unsqueeze(1).broadcast_to([128, 16, 128]), ALU.mult, r=[tLT, tCB], w=[tWm])
            x3 = xc_[:].rearrange("p (j d) -> p j d", j=16)
            P.tt(xdt[:].rearrange("p (j d) -> p j d", j=16), x3,
                 sm[:, 48:64].unsqueeze(2).broadcast_to([128, 16, 64]), ALU.mult, r=[txb[xb_], tsm], w=[txdt],
                 eng="pool")
            P.tt(xdd[:].rearrange("p (j d) -> p j d", j=16), x3,
                 sm[:, 128:144].unsqueeze(2).broadcast_to([128, 16, 64]), ALU.mult, r=[txb[xb_], tsm], w=[txdd],
                 eng="pool")
            for j in range(16):
                q = (j * 64) // 512
                P.mm(pC[:, j * 64:(j + 1) * 64], Wm[:, j, :], xdt[:, j * 64:(j + 1) * 64], start=True, stop=False,
                     r=[tWm, txdt], w=[tCk[q]])
                P.mm(pC[:, j * 64:(j + 1) * 64], Dd[:, j, :], xc_[:, j * 64:(j + 1) * 64], start=False, stop=True,
                     r=[tDd, txb[xb_]], w=[tCk[q]])
            for hh in range(2):
                P.mm(pA[:, hh * 512:(hh + 1) * 512], CTc, stb[:, hh * 512:(hh + 1) * 512], r=[tbc[b], tstb],
                     w=[tA[hh]])
            for hh in range(2):
                sl = slice(hh * 512, (hh + 1) * 512)
                P.tt(t1[:, sl].rearrange("p (j d) -> p j d", j=8), pA[:, sl].rearrange("p (j d) -> p j d", j=8),
                     eacs[:, hh * 8:(hh + 1) * 8].unsqueeze(2).broadcast_to([128, 8, 64]), ALU.mult,
                     r=[tA[hh], tsm], w=[tt1])
                P.tt(t1[:, sl], t1[:, sl], pC[:, sl], ALU.add, r=[tt1, tCk[hh]], w=[tt1])
            P.tt(yg[:], t1[:], zs[:], ALU.mult, r=[tt1, tzs], w=[tyg])
            P.act(junk[:], yg[:], AF.Square, r=[tyg], w=[tjk, tsm], accum_out=sm[:, 144:145])
            P.ts(sm[:, 145:146], sm[:, 144:145], 1.0 / 1024, EPS, ALU.mult, ALU.add, r=[tsm], w=[tsm])
            P.act(sm[:, 146:147], sm[:, 145:146], AF.Sqrt, r=[tsm], w=[tsm])
            P.op("dve", lambda e: e.reciprocal(sm[:, 147:148], sm[:, 146:147]), r=[tsm], w=[tsm])
            P.stt(yn[:], yg[:], sm[:, 147:148], gw[:], ALU.mult, ALU.mult, r=[tyg, tsm, tgw], w=[tyn])
            for i8 in range(8):
                P.tr(pD[:, i8, :], yn[:, i8 * 128:(i8 + 1) * 128], C["ident"][:], r=[tyn, self.Ctok], w=[tD])
            P.copy(yTs[b][:, :, sub * 128:(sub + 1) * 128], pD[:], r=[tD], w=[tyT[b]], eng="act")
            if sub == 3:
                P.dma(S["yT"][g * 1024:(g + 1) * 1024, sup * 512:(sup + 1) * 512].rearrange("(i p) t -> p i t", p=128),
                      yTs[b][:], r=[tyT[b]], grp=f"ms_yT{b}", eng="pool")
            for hh in range(2):
                P.mm(pC[:, 1024 + hh * 512:1024 + (hh + 1) * 512], bcs[xb_][:], xdd[:, hh * 512:(hh + 1) * 512],
                     r=[txb[xb_], txdd], w=[tCk[2 + hh]])
            P.tt(stf[:].rearrange("p (j d) -> p j d", j=16), stf[:].rearrange("p (j d) -> p j d", j=16),
                 dcl.unsqueeze(2).broadcast_to([128, 16, 64]), ALU.mult, r=[tstf, tsm], w=[tstf])
            for hh in range(2):
                sl = slice(hh * 512, (hh + 1) * 512)
                P.tt(stf[:, sl], stf[:, sl], pC[:, 1024 + hh * 512:1024 + (hh + 1) * 512], ALU.add,
                     r=[tstf, tCk[2 + hh]], w=[tstf])
            P.copy(stb[:], stf[:], r=[tstf], w=[tstb], eng="act")
        P.end_phase()

    def out_proj(self, KC, gate_off, src):
        P, c, S = self.P, self.cfg, self.S
        D, T = c.D, c.T
        NBW = 512
        P.begin_phase()
        gbc = P.sb("op_g", [128, D], F32)
        tg = Tok()
        P.dma(gbc[:], S["modbc"][:, gate_off:gate_off + D], w=[tg], grp="op_g")
        nb = 2
        ysb = [P.sb(f"op_y{i}", [128, KC, 512], BF16) for i in range(1)]
        ty = [Tok() for _ in range(1)]
        nwb = 2 if KC <= 32 else 1
        wo = [P.sb(f"op_w{i}", [128, KC, NBW], BF16) for i in range(nwb)]
        two = [Tok() for _ in range(nwb)]
        xb = [P.sb(f"op_x{i}", [128, NBW], F32) for i in range(4)]
        txb = [Tok() for _ in range(4)]
        tm = [P.sb(f"op_t{i}", [128, NBW], F32) for i in range(4)]
        ttm = [Tok() for _ in range(4)]
        pst = [P.ps(f"op_ps{i}", [128, 512], F32) for i in range(4)]
        tps = [PTok() for _ in range(4)]
        Wb = S["wb_out"]
        wi = 0
        ei = 0
        for sup in range(T // 512):
            b = 0
            P.dma(ysb[b][:], src[0:KC * 128, sup * 512:(sup + 1) * 512].rearrange("(k p) t -> p k t", p=128), w=[ty[b]],
                  grp=f"op_y{b}", eng="act")
            for n0 in range(0, D, NBW):
                wb_ = wi % nwb
                wi += 1
                P.dma(wo[wb_][:], Wb[n0 // 512, :, 0:KC, :], w=[two[wb_]], grp=f"op_w{wb_}", eng="sp")
                for m in range(4):
                    e4 = ei % 4
                    ei += 1
                    r0 = sup * 512 + m * 128
                    P.dma(xb[e4][:], S["xres"][r0:r0 + 128, n0:n0 + NBW], w=[txb[e4]], grp=f"op_x{e4}", eng="pool")
                    for k in range(KC):
                        P.mm(pst[e4][:, :NBW], ysb[b][:, k, m * 128:(m + 1) * 128], wo[wb_][:, k, :], start=(k == 0),
                             stop=(k == KC - 1), r=[ty[b], two[wb_]], w=[tps[e4]])
                    P.tt(tm[e4][:], pst[e4][:, :NBW], gbc[:, n0:n0 + NBW], ALU.mult, r=[tps[e4], tg], w=[ttm[e4]])
                    P.tt(tm[e4][:], tm[e4][:], xb[e4][:], ALU.add, r=[ttm[e4], txb[e4]], w=[ttm[e4]], eng="pool")
                    P.dma(S["xres"][r0:r0 + 128, n0:n0 + NBW], tm[e4][:], r=[ttm[e4]], grp=f"op_s{e4}", eng="pool")
        P.end_phase()

    def kv_stream(self):
        P, c, I, S, C = self.P, self.cfg, self.I, self.S, self.C
        KD, T = c.KD, c.T
        Wb = S["wb_kv"]
        Wv = lambda o, n: Wb[:, o:o + n].rearrange("(k p) n -> p k n", p=128)
        P.begin_phase()
        wk = P.sb("kv_wk", [128, KD, c.KVD], BF16)
        tw = Tok()
        P.dma(wk[:], Wv(0, c.KVD), w=[tw], grp="kv_w")
        ko = [P.sb(f"kv_ko{i}", [128, 512], BF16) for i in range(2)]
        tko = [Tok() for _ in range(2)]
        cnt = [0]

        def epik(ps, tps, ct, tt):
            b = cnt[0] % 2
            cnt[0] += 1
            P.copy(ko[b][:], ps[:], r=[tps], w=[tko[b]], eng=("act" if b == 0 else "dve"))
            P.dma(S["KT"][ct * 128:(ct + 1) * 128, tt * 512:(tt + 1) * 512], ko[b][:], r=[tko[b]], grp=f"kv_ko{b}",
                  eng="pool")

        self.proj_fm(wk, tw, c.KVD, epik)
        P.end_phase()
        P.begin_phase()
        NV = c.KVD
        BH = c.BH
        wv = P.sb("kv_wv", [128, KD, NV + BH], BF16)
        tw = Tok()
        P.dma(wv[:], Wv(c.KVD, NV + BH), w=[tw], grp="kv_w")
        bfb = P.sb("kv_bf", [128, BH], F32)
        tbf = Tok()
        P.dma(bfb[:], I["b_f"][0:1, :].partition_broadcast(128), w=[tbf], grp="kv_bf")
        vo = [P.sb(f"kv_vo{i}", [128, 512], BF16) for i in range(2)]
        tvo = [Tok() for _ in range(2)]
        sm = P.sb("kv_sm", [128, 4 * BH], F32)
        tsm = Tok()
        carT = P.sb("kv_carT", [BH, 4], F32)
        tcar = Tok()
        P.memset(carT[:], 0.0, w=[tcar])
        pF = P.ps("kv_pF", [128, 512], F32)
        tpF = PTok()
        ft = [P.sb(f"kv_ft{i}", [BH, 128], F32) for i in range(2)]
        tft = [Tok() for _ in range(2)]
        n3 = [P.sb(f"kv_n3{i}", [BH, 3, 128], BF16) for i in range(2)]
        tn3 = [Tok() for _ in range(2)]
        wk_ = P.sb("kv_wk2", [BH, 4, 128], F32)
        twk = Tok()
        cnt = [0]
        SQ = math.sqrt(128.0)

        def epiv(ps, tps, it, c0, w):
            if c0 < NV:
                wv_ = min(w, NV - c0)
                b = cnt[0] % 2
                cnt[0] += 1
                P.copy(vo[b][:, :wv_], ps[:, :wv_], r=[tps], w=[tvo[b]], eng=("act" if b == 0 else "dve"))
                P.dma(S["V"][it * 128:(it + 1) * 128, c0:c0 + wv_], vo[b][:, :wv_], r=[tvo[b]], grp=f"kv_vo{b}",
                      eng="pool")
            if c0 + w <= NV:
                return
            o = NV - c0
            x_ = sm[:, 0:BH]
            P.tt(x_, ps[:, o:o + BH], bfb[:], ALU.add, r=[tps, tbf], w=[tsm])
            P.ts(x_, x_, -1.0, None, ALU.mult, r=[tsm], w=[tsm])
            P.ts(sm[:, BH:2 * BH], x_, -1.0, None, ALU.mult, r=[tsm], w=[tsm])
            P.tt(sm[:, BH:2 * BH], sm[:, BH:2 * BH], x_, ALU.min, r=[tsm], w=[tsm])
            P.act(sm[:, BH:2 * BH], sm[:, BH:2 * BH], AF.Exp, r=[tsm], w=[tsm])
            P.act(sm[:, BH:2 * BH], sm[:, BH:2 * BH], AF.Ln, r=[tsm], w=[tsm], bias=1.0)
            P.ts(sm[:, 2 * BH:3 * BH], x_, 0.0, None, ALU.max, r=[tsm], w=[tsm])
            P.tt(sm[:, 3 * BH:4 * BH], sm[:, 2 * BH:3 * BH], sm[:, BH:2 * BH], ALU.add, r=[tsm], w=[tsm])
            lf = sm[:, 3 * BH:4 * BH]
            P.ts(lf, lf, -1.0, None, ALU.mult, r=[tsm], w=[tsm])
            P.mm(pF[0:BH, 0:128], lf, C["trif"][:], r=[tsm, self.Ctok], w=[tpF])
            P.mm(pF[0:BH, 128:129], lf, C["onesf"][:, 0:1], r=[tsm, self.Ctok], w=[tpF])
            b = it % 2
            P.ts(ft[b][:], pF[0:BH, 0:128], carT[:, 0:1], None, ALU.add, r=[tpF, tcar], w=[tft[b]])
            P.tt(carT[:, 0:1], carT[:, 0:1], pF[0:BH, 128:129], ALU.add, r=[tcar, tpF], w=[tcar])
            P.dma(S["FT"][:, it * 128:(it + 1) * 128], ft[b][:], r=[tft[b]], grp=f"kv_ft{b}", eng="pool")
            P.ts(wk_[:, 0, :], ft[b][:], -SQ, None, ALU.mult, r=[tft[b]], w=[twk])
            P.copy(n3[b][:, 0, :], wk_[:, 0, :], r=[twk], w=[tn3[b]])
            P.copy(wk_[:, 1, :], n3[b][:, 0, :], r=[tn3[b]], w=[twk])
            P.tt(wk_[:, 2, :], wk_[:, 0, :], wk_[:, 1, :], ALU.subtract, r=[twk], w=[twk])
            P.copy(n3[b][:, 1, :], wk_[:, 2, :], r=[twk], w=[tn3[b]])
            P.copy(wk_[:, 1, :], n3[b][:, 1, :], r=[tn3[b]], w=[twk])
            P.tt(wk_[:, 3, :], wk_[:, 2, :], wk_[:, 1, :], ALU.subtract, r=[twk], w=[twk])
            P.copy(n3[b][:, 2, :], wk_[:, 3, :], r=[twk], w=[tn3[b]])
            for part in range(3):
                P.dma(S["NF3"][part, :, it * 128:(it + 1) * 128], n3[b][:, part, :], r=[tn3[b]], grp=f"kv_n3{b}",
                      eng="pool")

        self.proj_tm(wv, tw, NV + BH, epiv)
        P.end_phase()

    def attn_proj(self):
        P, c, S = self.P, self.cfg, self.S
        KD, T = c.KD, c.T
        Wb = S["wb_in"]
        Wv = lambda o, n: Wb[:, o:o + n].rearrange("(k p) n -> p k n", p=128)
        CG = 1024 if c.BI >= 1024 else c.BI
        for c0 in range(0, c.BI, CG):
            P.begin_phase()
            wq = P.sb("ap_wq", [128, KD, CG], BF16)
            tw = Tok()
            P.dma(wq[:], Wv(c0, CG), w=[tw], grp="ap_w")
            qo = [P.sb(f"ap_qo{i}", [128, 512], BF16) for i in range(2)]
            tqo = [Tok() for _ in range(2)]
            cnt = [0]

            def epiq(ps, tps, ct, tt, c0=c0):
                b = cnt[0] % 2
                cnt[0] += 1
                P.copy(qo[b][:], ps[:], r=[tps], w=[tqo[b]], eng=("act" if b == 0 else "dve"))
                P.dma(S["QT"][c0 + ct * 128:c0 + (ct + 1) * 128, tt * 512:(tt + 1) * 512], qo[b][:], r=[tqo[b]],
                      grp=f"ap_qo{b}", eng="pool")

            self.proj_fm(wq, tw, CG, epiq)
            P.end_phase()
        for c0 in range(0, c.BI, CG):
            P.begin_phase()
            wz = P.sb("ap_wz", [128, KD, CG], BF16)
            tw = Tok()
            P.dma(wz[:], Wv(c.BI + c0, CG), w=[tw], grp="ap_w")
            zo = [P.sb(f"ap_zo{i}", [128, 512], BF16) for i in range(2)]
            tzo = [Tok() for _ in range(2)]
            cnt = [0]

            def epiz(ps, tps, ct, tt, c0=c0):
                b = cnt[0] % 2
                cnt[0] += 1
                P.act(zo[b][:], ps[:], AF.Silu, r=[tps], w=[tzo[b]])
                P.dma(S["ZT"][c0 + ct * 128:c0 + (ct + 1) * 128, tt * 512:(tt + 1) * 512], zo[b][:], r=[tzo[b]],
                      grp=f"ap_zo{b}", eng="pool")

            self.proj_fm(wz, tw, CG, epiz)
            P.end_phase()

    def attn_core(self):
        P, c, S, C = self.P, self.cfg, self.S, self.C
        T = c.T
        NT = T // 128
        scale = 128.0 ** -0.5
        SQ = math.sqrt(128.0)
        for hk in range(c.NKV):
            P.begin_phase()
            KT = P.sb("at_KT", [128, T], BF16)
            Vs = P.sb("at_V", [128, NT, 128], BF16)
            QT = P.sb("at_QT", [128, NT, 4, 128], BF16)
            Fc = P.sb("at_Fc", [128, 4, NT], F32)
            Fr = P.sb("at_Fr", [128, 4, NT], F32)
            bM = [P.sb(f"at_bM{i}", [128, 4, NT], F32) for i in range(2)]
            tbM = [Tok() for _ in range(2)]
            tl = Tok()
            P.dma(KT[:], S["KT"][hk * 128:(hk + 1) * 128, :], w=[tl], grp="at_l")
            P.dma(Vs[:], S["V"][:, hk * 128:(hk + 1) * 128].rearrange("(i p) d -> p i d", p=128), w=[tl], grp="at_l",
                  eng="act")
            for g in range(4):
                h = hk * 4 + g
                P.dma(QT[:, :, g, :], S["QT"][h * 128:(h + 1) * 128, :].rearrange("p (i q) -> p i q", q=128), w=[tl],
                      grp="at_l", eng=("sp" if g % 2 == 0 else "act"))
            ftl = P.sb("at_ftl", [NT, 4, 128], F32)
            sps = [P.ps(f"at_s{i}", [128, 512], F32) for i in range(3)]
            tsp = [PTok() for _ in range(3)]
            for g in range(4):
                h = hk * 4 + g
                P.dma(ftl[:, g, :], S["FT"][h:h + 1, :].rearrange("o (i p) -> (o i) p", p=128), w=[tl], grp="at_l",
                      eng="act")
            for g in range(4):
                P.mm(sps[0][:, g * NT:(g + 1) * NT], ftl[:, g, :], C["identf"][0:NT, 0:NT], r=[tl, self.Ctok],
                     w=[tsp[0]])
            P.copy(Fc[:].rearrange("p g i -> p (g i)"), sps[0][:, 0:4 * NT], r=[tsp[0]], w=[tl])
            P.mm(sps[1][:, 0:4 * NT], C["sel64"][:], Fc[:].rearrange("p g i -> p (g i)"), r=[tl, self.Ctok],
                 w=[tsp[1]])
            P.ts(Fr[:].rearrange("p g i -> p (g i)"), sps[1][:, 0:4 * NT], SQ, None, ALU.mult, r=[tsp[1]], w=[tl])
            P.ts(Fc[:], Fc[:], -SQ, None, ALU.mult, r=[tl], w=[tl])
            ops = [P.ps(f"at_o{i}", [128, 512], F32) for i in range(2)]
            tops = [PTok() for _ in range(2)]
            lps = [P.ps(f"at_l{i}", [128, 512], F32) for i in range(2)]
            tlps = [PTok() for _ in range(2)]
            sm = [P.sb(f"at_sm{i}", [128, 4, 128], F32) for i in range(3)]
            tsm = [Tok() for _ in range(3)]
            PT = [P.sb(f"at_PT{i}", [128, 512], BF16) for i in range(3)]
            tPT = [Tok() for _ in range(3)]
            rl = P.sb("at_rl", [128, 512], F32)
            trl = Tok()
            zt = [P.sb(f"at_z{i}", [128, 4, 512], BF16) for i in range(2)]
            tz = [Tok() for _ in range(2)]
            oT = [P.sb(f"at_oT{i}", [128, 4, 512], BF16) for i in range(2)]
            toT = [Tok() for _ in range(2)]
            bi = 0
            for qt in range(NT):
                sup, sub = qt // 4, qt % 4
                ob = sup % 2
                ab = qt % 2
                if sub == 0:
                    P.dma(zt[ob][:], S["ZT"][hk * 512:(hk + 1) * 512, sup * 512:(sup + 1) * 512]
                          .rearrange("(g p) t -> p g t", p=128), w=[tz[ob]], grp=f"at_z{ob}", eng="sp")
                nk = qt + 1
                P.tt(bM[ab][:, :, 0:nk], Fc[:, :, 0:nk], Fr[:, :, qt:qt + 1].broadcast_to([128, 4, nk]), ALU.add,
                     r=[tl], w=[tbM[ab]])
                qsl = QT[:, qt, :, :].rearrange("p g q -> p (g q)")
                for kb in range(nk):
                    b = bi % 3
                    bi += 1
                    P.mm(sps[b][:], KT[:, kb * 128:(kb + 1) * 128], qsl, r=[tl], w=[tsp[b]])
                    P.tt(sm[b][:], sps[b][:].rearrange("p (g q) -> p g q", g=4),
                         bM[ab][:, :, kb:kb + 1].broadcast_to([128, 4, 128]), ALU.add, r=[tsp[b], tbM[ab]],
                         w=[tsm[b]])
                    if kb == qt:
                        P.tt(sm[b][:], sm[b][:], C["maskT"][:].unsqueeze(1).broadcast_to([128, 4, 128]), ALU.add,
                             r=[tsm[b], self.Ctok], w=[tsm[b]], eng="pool")
                    P.act(PT[b][:], sm[b][:].rearrange("p g q -> p (g q)"), AF.Exp, r=[tsm[b]], w=[tPT[b]],
                          scale=scale)
                    P.mm(ops[ab][:], Vs[:, kb, :], PT[b][:], start=(kb == 0), stop=(kb == qt), r=[tl, tPT[b]],
                         w=[tops[ab]])
                    P.mm(lps[ab][:], C["onesb"][:], PT[b][:], start=(kb == 0), stop=(kb == qt),
                         r=[self.Ctok, tPT[b]], w=[tlps[ab]])
                P.op("dve", lambda e, o_=rl[:], i_=lps[ab][:]: e.reciprocal(o_, i_), r=[tlps[ab]], w=[trl])
                P.tt(rl[:], rl[:], ops[ab][:], ALU.mult, r=[trl, tops[ab]], w=[trl])
                P.tt(oT[ob][:, :, sub * 128:(sub + 1) * 128], rl[:].rearrange("p (g q) -> p g q", g=4),
                     zt[ob][:, :, sub * 128:(sub + 1) * 128], ALU.mult, r=[trl, tz[ob]], w=[toT[ob]], eng="pool")
                if sub == 3:
                    P.dma(S["yT"][hk * 512:(hk + 1) * 512, sup * 512:(sup + 1) * 512].rearrange("(g p) t -> p g t", p=128),
                          oT[ob][:], r=[toT[ob]], grp=f"at_oT{ob}", eng="pool")
            P.end_phase()

    def copy_x(self):
        P, c, I, S = self.P, self.cfg, self.I, self.S
        P.begin_phase()
        t = [P.sb(f"cx{i}", [128, c.D], F32) for i in range(2)]
        tk = [Tok() for _ in range(2)]
        for it in range(c.T // 128):
            b = it % 2
            P.dma(t[b][:], I["x"][it * 128:(it + 1) * 128, :], w=[tk[b]], grp=f"cx_l{b}", eng="sp")
            P.dma(S["xres"][it * 128:(it + 1) * 128, :], t[b][:], r=[tk[b]], grp=f"cx_s{b}", eng="act")
        P.end_phase()

    def build(self, stop_after=None):
        c, I, S = self.cfg, self.I, self.S
        D = c.D

        def done(tag):
            return stop_after == tag

        self.copy_x()
        self.mods()
        self.dbg_out("modbc", S["modbc"], [128, 14 * D], F32)
        if not done("mods"):
            for li in range(c.NA):
                self.cast_w(I[f"a_in{li}"], S["wb_in"], D, c.APROJ)
                self.cast_w_blk(I[f"a_out{li}"], S["wb_out"], c.AI, D)
                self.norm(I["norm_w"][li:li + 1, :], li * 3 * D)
                if li == 0:
                    pass
                if done("norm0"):
                    break
                for g in range(c.NG):
                    self.mamba_xbc(li, g)
                    if li == 0 and g == 0:
                        self.dbg_out("xs", S["xs"], [c.T, 1024], BF16)
                        self.dbg_out("BT", S["BT"], [128, c.T], BF16)
                        self.dbg_out("CT", S["CT"], [128, c.T], BF16)
                        self.dbg_out("Bm", S["Bm"], [c.T, 128], BF16)
                    if done("xbc0"):
                        break
                    self.mamba_scan(li, g)
                if done("xbc0"):
                    break
                if li == 0:
                    self.dbg_out("yT0", S["yT"], [c.AI, c.T], BF16)
                if done("scan0"):
                    break
                self.out_proj(c.AI // 128, li * 3 * D + 2 * D, S["yT"])
                if li == 0:
                    self.dbg_out("x1", S["xres"], [c.T, D], F32)
                if done("layer0"):
                    break
            else:
                self.dbg_out("xA", S["xres"], [c.T, D], F32)
                if not done("mamba"):
                    self.cast_w(I["w_kv"], S["wb_kv"], D, 2 * c.KVD)
                    self.cast_w(I["w_f"], S["wb_kv"], D, c.BH, d0=2 * c.KVD)
                    self.norm(I["kv_norm"][0:1, :], 12 * D)
                    self.kv_stream()
                    self.dbg_out("KT", S["KT"], [c.KVD, c.T], BF16)
                    self.dbg_out("V", S["V"], [c.T, c.KVD], BF16)
                    self.dbg_out("FT", S["FT"], [c.BH, c.T], F32)
                    if not done("kv"):
                        for lj in range(c.NB):
                            li = c.NA + lj
                            self.cast_w(I[f"b_in{lj}"], S["wb_in"], D, 2 * c.BI)
                            self.cast_w_blk(I[f"b_out{lj}"], S["wb_out"], c.BI, D)
                            self.norm(I["norm_w"][li:li + 1, :], li * 3 * D)
                            self.attn_proj()
                            if lj == 0:
                                self.dbg_out("QT", S["QT"], [c.BI, c.T], BF16)
                                self.dbg_out("ZT", S["ZT"], [c.BI, c.T], BF16)
                            if done("aproj"):
                                break
                            self.attn_core()
                            if lj == 0:
                                self.dbg_out("oT", S["yT"], [c.BI, c.T], BF16)
                            if done("acore"):
                                break
                            self.out_proj(c.BI // 128, li * 3 * D + 2 * D, S["yT"])
                            if lj == 0:
                                self.dbg_out("x3", S["xres"], [c.T, D], F32)
        self.norm(I["final_norm"][0:1, :], 0, final=True)
        self.P.emit()
        return self.nc


def make_inputs(cfg, b, x, c, ada_w, ada_b, norm_w, a_in_proj, a_conv_w, a_conv_b, a_dt_bias, a_A_log, a_D,
                a_gnorm, a_out_proj, kv_norm, kv_ada_w, kv_ada_b, w_kv, w_f, b_f, b_in_proj, b_out_proj, final_norm):
    f = lambda a: np.ascontiguousarray(a, dtype=np.float32)
    m = {}
    m["x"] = f(x[b])
    m["cT"] = f(np.asarray(c[b]).reshape(cfg.KD, 128).T)
    for i in range(4):
        m[f"ada_w{i}"] = f(ada_w[i])
    m["ada_b"] = f(ada_b)
    m["norm_w"] = f(norm_w)
    for i in range(cfg.NA):
        m[f"a_in{i}"] = f(a_in_proj[i])
        m[f"a_out{i}"] = f(a_out_proj[i])
    m["a_convT"] = f(np.transpose(np.asarray(a_conv_w), (0, 2, 1)))
    m["a_conv_b"] = f(a_conv_b)
    m["a_dt_bias"] = f(a_dt_bias)
    m["a_A_log"] = f(a_A_log)
    m["a_D"] = f(a_D)
    m["a_gnorm"] = f(a_gnorm)
    m["kv_norm"] = f(np.asarray(kv_norm).reshape(1, -1))
    m["kv_ada_w"] = f(kv_ada_w)
    m["kv_ada_b"] = f(np.asarray(kv_ada_b).reshape(1, -1))
    m["w_kv"] = f(w_kv)
    m["w_f"] = f(w_f)
    m["b_f"] = f(np.asarray(b_f).reshape(1, -1))
    for i in range(cfg.NB):
        m[f"b_in{i}"] = f(b_in_proj[i])
        m[f"b_out{i}"] = f(b_out_proj[i])
    m["final_norm"] = f(np.asarray(final_norm).reshape(1, -1))
    return m


def kernel(**inputs):
    x = np.asarray(inputs["x"])
    B, T, D = x.shape
    cfg = Cfg(D, T)
    kb = K(cfg)
    nc = kb.build()
    in_maps = [make_inputs(cfg, b, **inputs) for b in range(B)]
    res = run_bass_kernel_spmd(nc, in_maps, core_ids=list(range(B)))
    return np.stack([np.asarray(res.results[b]["out"]) for b in range(B)], axis=0).astype(np.float32)
```
